# Optimizing a Trainium2 kernel written in Bass

```python
import jax, jax.numpy as jnp
from jax import lax
import numpy as np

D_MODEL = 1024
BATCH = 8
SEQ = 2048
DEPTH = 4

ATT_HEAD_DIM = 64
ATT_HEADS_PER_GROUP = D_MODEL // 256
DILATED_GROUPS = ((128, 1), (512, 4), (2048, 16))
N_GROUPS = len(DILATED_GROUPS)
ATT_WIDTH = N_GROUPS * ATT_HEADS_PER_GROUP * ATT_HEAD_DIM
ATT_OUT_WIDTH = ATT_HEADS_PER_GROUP * ATT_HEAD_DIM
ATT_BLOCK = 128

DN_HEAD_DIM = 128
DN_HEADS = D_MODEL // DN_HEAD_DIM
DN_WIDTH = DN_HEADS * DN_HEAD_DIM
CONV_WIDTH = 4
DN_CHUNK = 64

D_FF = 2816
EPS = 1e-6
N_ADA = 9

OFF_DN_QKV = 3 * ATT_WIDTH
OFF_DN_GATE = OFF_DN_QKV + 3 * DN_WIDTH
OFF_DN_A = OFF_DN_GATE + DN_WIDTH
OFF_DN_B = OFF_DN_A + DN_HEADS
OFF_MERGE = OFF_DN_B + DN_HEADS
N_IN = OFF_MERGE + 2 * D_MODEL

kernel_name = "hybrid_dilated_attn_gated_deltanet_macaron_adaln"


def rms_norm(x, g):
    xf = x.astype(jnp.float32)
    y = xf * lax.rsqrt(jnp.mean(xf * xf, axis=-1, keepdims=True) + EPS)
    return (y * g.astype(jnp.float32)).astype(x.dtype)


def l2_norm(x):
    xf = x.astype(jnp.float32)
    return xf * lax.rsqrt(jnp.sum(xf * xf, axis=-1, keepdims=True) + EPS)


def modulate(h, shift, scale):
    return h * (1.0 + scale[:, None, :]) + shift[:, None, :]


def swiglu(h, w_up, w_down):
    gate, up = jnp.split(h @ w_up, 2, axis=-1)
    return (jax.nn.silu(gate) * up) @ w_down


def dilated_group_attention(q, k, v, window, dilation):
    B, T, H, hd = q.shape
    L = T // dilation
    w_sub = window // dilation
    Lp = -(-L // ATT_BLOCK) * ATT_BLOCK
    nb = Lp // ATT_BLOCK

    def to_blocks(t):
        t = t.reshape(B, L, dilation, H, hd).transpose(0, 2, 1, 3, 4)
        t = jnp.pad(t, ((0, 0), (0, 0), (0, Lp - L), (0, 0), (0, 0)))
        return t.reshape(B, dilation, nb, ATT_BLOCK, H, hd)

    qb, kb, vb = to_blocks(q), to_blocks(k), to_blocks(v)

    def with_prev(t):
        prev = jnp.pad(t, ((0, 0), (0, 0), (1, 0), (0, 0), (0, 0), (0, 0)))[:, :, :-1]
        return jnp.concatenate([prev, t], axis=3)

    kk, vv = with_prev(kb), with_prev(vb)
    s = jnp.einsum('bgnqhd,bgnkhd->bgnhqk', qb, kk,
                   preferred_element_type=jnp.float32) * (ATT_HEAD_DIM ** -0.5)
    qi = jnp.arange(ATT_BLOCK)[:, None]
    kj = jnp.arange(2 * ATT_BLOCK)[None, :]
    dist = ATT_BLOCK + qi - kj
    blk = jnp.arange(nb)[:, None, None]
    valid = (dist >= 0) & (dist <= w_sub) & ((blk > 0) | (kj >= ATT_BLOCK))
    s = jnp.where(valid[None, None, :, None], s, -jnp.inf)
    m = jnp.max(s, axis=-1, keepdims=True)
    p = jnp.exp(s - m)
    denom = jnp.sum(p, axis=-1, keepdims=True)
    o = jnp.einsum('bgnhqk,bgnkhd->bgnqhd', p / denom, vv.astype(jnp.float32))
    lse = (m + jnp.log(denom))[..., 0].transpose(0, 1, 2, 4, 3)
    o = o.reshape(B, dilation, Lp, H, hd)[:, :, :L].transpose(0, 2, 1, 3, 4).reshape(B, T, H, hd)
    lse = lse.reshape(B, dilation, Lp, H)[:, :, :L].transpose(0, 2, 1, 3).reshape(B, T, H)
    return o, lse


def causal_depthwise_conv(x, w):
    T = x.shape[1]
    xp = jnp.pad(x, ((0, 0), (CONV_WIDTH - 1, 0), (0, 0)))
    return sum(xp[:, i:i + T] * w[i] for i in range(CONV_WIDTH))


def chunk_gated_delta_rule(q, k, v, g, beta):
    B, T, H, dk = q.shape
    dv = v.shape[-1]
    C = DN_CHUNK
    N = T // C

    def chunks(t):
        t = t.reshape(B, N, C, H, *t.shape[3:])
        return jnp.moveaxis(t, 3, 2)

    q, k, v, g, beta = chunks(q), chunks(k), chunks(v), chunks(g), chunks(beta)
    gc = jnp.cumsum(g, axis=-1)
    tril = jnp.tril(jnp.ones((C, C), dtype=bool))
    strict = tril & ~jnp.eye(C, dtype=bool)
    diff = gc[..., :, None] - gc[..., None, :]
    ldec = jnp.where(tril, jnp.exp(jnp.where(tril, diff, 0.0)), 0.0)
    kb = k * beta[..., None]
    vb = v * beta[..., None]
    a_mat = jnp.where(strict, jnp.einsum('bnhid,bnhjd->bnhij', kb, k) * ldec, 0.0)
    eye = jnp.broadcast_to(jnp.eye(C, dtype=jnp.float32), a_mat.shape)
    t_inv = lax.linalg.triangular_solve(eye + a_mat, eye, left_side=True, lower=True)
    u = jnp.einsum('bnhij,bnhjd->bnhid', t_inv, vb)
    w = jnp.einsum('bnhij,bnhjd->bnhid', t_inv, kb * jnp.exp(gc)[..., None])
    attn_intra = jnp.where(tril, jnp.einsum('bnhid,bnhjd->bnhij', q, k) * ldec, 0.0)
    q_dec = q * jnp.exp(gc)[..., None]
    k_dec = k * jnp.exp(gc[..., -1:] - gc)[..., None]
    g_last = jnp.exp(gc[..., -1])

    def step(S, inp):
        w_c, u_c, qd_c, kd_c, a_c, gl_c = inp
        v_new = u_c - jnp.einsum('bhcd,bhde->bhce', w_c, S)
        o = jnp.einsum('bhcd,bhde->bhce', qd_c, S) + jnp.einsum('bhij,bhje->bhie', a_c, v_new)
        S = S * gl_c[..., None, None] + jnp.einsum('bhcd,bhce->bhde', kd_c, v_new)
        return S, o

    xs = tuple(jnp.moveaxis(t, 1, 0) for t in (w, u, q_dec, k_dec, attn_intra, g_last))
    S0 = jnp.zeros((B, H, dk, dv), jnp.float32)
    _, o = lax.scan(step, S0, xs)
    return o.transpose(1, 0, 3, 2, 4).reshape(B, T, H, dv)


def hybrid_mixer(h, w_in, q_norm, k_norm, conv_w, a_log, dt_bias, dn_norm,
                 w_proj_att, w_proj_dn, w_out):
    B, T, _ = h.shape
    z = h @ w_in
    qkv = z[..., :OFF_DN_QKV].reshape(B, T, 3, N_GROUPS, ATT_HEADS_PER_GROUP, ATT_HEAD_DIM)
    q = rms_norm(qkv[:, :, 0], q_norm)
    k = rms_norm(qkv[:, :, 1], k_norm)
    v = qkv[:, :, 2]
    outs, lses = [], []
    for gi, (window, dilation) in enumerate(DILATED_GROUPS):
        o, lse = dilated_group_attention(q[:, :, gi], k[:, :, gi], v[:, :, gi], window, dilation)
        outs.append(o)
        lses.append(lse)
    wts = jax.nn.softmax(jnp.stack(lses), axis=0)
    y_att = jnp.sum(wts[..., None] * jnp.stack(outs), axis=0)
    y_att = y_att.reshape(B, T, ATT_OUT_WIDTH).astype(h.dtype) @ w_proj_att
    dn_qkv = jax.nn.silu(causal_depthwise_conv(z[..., OFF_DN_QKV:OFF_DN_GATE], conv_w))
    dq, dk, dv = jnp.split(dn_qkv, 3, axis=-1)
    dq = l2_norm(dq.reshape(B, T, DN_HEADS, DN_HEAD_DIM)) * (DN_HEAD_DIM ** -0.5)
    dk = l2_norm(dk.reshape(B, T, DN_HEADS, DN_HEAD_DIM))
    dv = dv.reshape(B, T, DN_HEADS, DN_HEAD_DIM).astype(jnp.float32)
    a_in = z[..., OFF_DN_A:OFF_DN_B].astype(jnp.float32)
    b_in = z[..., OFF_DN_B:OFF_MERGE].astype(jnp.float32)
    g_log = -jnp.exp(a_log.astype(jnp.float32)) * jax.nn.softplus(a_in + dt_bias.astype(jnp.float32))
    beta = jax.nn.sigmoid(b_in)
    o_dn = chunk_gated_delta_rule(dq, dk, dv, g_log, beta)
    out_gate = z[..., OFF_DN_GATE:OFF_DN_A].reshape(B, T, DN_HEADS, DN_HEAD_DIM).astype(jnp.float32)
    o_dn = rms_norm(o_dn, dn_norm) * jax.nn.silu(out_gate)
    y_dn = o_dn.reshape(B, T, DN_WIDTH).astype(h.dtype) @ w_proj_dn
    g_att, g_dn = jnp.split(jax.nn.sigmoid(z[..., OFF_MERGE:]), 2, axis=-1)
    return (g_att * y_att + g_dn * y_dn) @ w_out


def setup_inputs(seed: int = 0) -> dict:
    key = jax.random.key(seed)
    ks = iter(jax.random.split(key, 32))

    def nrm(shape, scale):
        return jax.random.normal(next(ks), shape, jnp.float32) * scale

    def gain(shape):
        return 1.0 + nrm(shape, 0.05)

    L, D = DEPTH, D_MODEL
    x = nrm((BATCH, SEQ, D), 1.0)
    c = nrm((BATCH, D), 1.0)
    ada_w = nrm((L, D, N_ADA * D), 0.02)
    ada_b = nrm((L, N_ADA * D), 0.1)
    norm_ff1 = gain((L, D))
    ffn1_w_up = nrm((L, D, 2 * D_FF), D ** -0.5)
    ffn1_w_down = nrm((L, D_FF, D), D_FF ** -0.5)
    norm_mix = gain((L, D))
    w_in = nrm((L, D, N_IN), D ** -0.5)
    q_norm = gain((L, ATT_HEAD_DIM))
    k_norm = gain((L, ATT_HEAD_DIM))
    conv_w = nrm((L, CONV_WIDTH, 3 * DN_WIDTH), CONV_WIDTH ** -0.5)
    a_log = jnp.log(jax.random.uniform(next(ks), (L, DN_HEADS), jnp.float32, 1.0, 16.0))
    dt = jnp.exp(jax.random.uniform(next(ks), (L, DN_HEADS), jnp.float32,
                                    float(np.log(1e-3)), float(np.log(1e-1))))
    dt_bias = jnp.log(jnp.expm1(dt))
    dn_norm = gain((L, DN_HEAD_DIM))
    w_proj_att = nrm((L, ATT_OUT_WIDTH, D), ATT_OUT_WIDTH ** -0.5)
    w_proj_dn = nrm((L, DN_WIDTH, D), DN_WIDTH ** -0.5)
    w_out = nrm((L, D, D), D ** -0.5)
    norm_ff2 = gain((L, D))
    ffn2_w_up = nrm((L, D, 2 * D_FF), D ** -0.5)
    ffn2_w_down = nrm((L, D_FF, D), D_FF ** -0.5)
    return {"x": x, "c": c, "ada_w": ada_w, "ada_b": ada_b,
            "norm_ff1": norm_ff1, "ffn1_w_up": ffn1_w_up, "ffn1_w_down": ffn1_w_down,
            "norm_mix": norm_mix, "w_in": w_in, "q_norm": q_norm, "k_norm": k_norm,
            "conv_w": conv_w, "a_log": a_log, "dt_bias": dt_bias, "dn_norm": dn_norm,
            "w_proj_att": w_proj_att, "w_proj_dn": w_proj_dn, "w_out": w_out,
            "norm_ff2": norm_ff2, "ffn2_w_up": ffn2_w_up, "ffn2_w_down": ffn2_w_down}


def reference(x, c, ada_w, ada_b, norm_ff1, ffn1_w_up, ffn1_w_down, norm_mix, w_in,
              q_norm, k_norm, conv_w, a_log, dt_bias, dn_norm, w_proj_att, w_proj_dn,
              w_out, norm_ff2, ffn2_w_up, ffn2_w_down):
    c_act = jax.nn.silu(c)
    for l in range(DEPTH):
        mod = c_act @ ada_w[l] + ada_b[l]
        (sh1, sc1, gt1, sh2, sc2, gt2, sh3, sc3, gt3) = jnp.split(mod, N_ADA, axis=-1)
        h = modulate(rms_norm(x, norm_ff1[l]), sh1, sc1)
        x = x + 0.5 * gt1[:, None, :] * swiglu(h, ffn1_w_up[l], ffn1_w_down[l])
        h = modulate(rms_norm(x, norm_mix[l]), sh2, sc2)
        x = x + gt2[:, None, :] * hybrid_mixer(h, w_in[l], q_norm[l], k_norm[l], conv_w[l],
                                                a_log[l], dt_bias[l], dn_norm[l],
                                                w_proj_att[l], w_proj_dn[l], w_out[l])
        h = modulate(rms_norm(x, norm_ff2[l]), sh3, sc3)
        x = x + 0.5 * gt3[:, None, :] * swiglu(h, ffn2_w_up[l], ffn2_w_down[l])
    return x
```

```python
import numpy as np
import concourse.bass as bass
import concourse.mybir as mybir
from concourse.bass_utils import run_bass_kernel_spmd

F32 = mybir.dt.float32
BF16 = mybir.dt.bfloat16
AF = mybir.ActivationFunctionType
ALU = mybir.AluOpType

D = 1024
T = 2048
DEPTH = 4
DFF = 2816
NFF = DFF // 128
KC = D // 128
NT = T // 512
NB = T // 128
N_IN = 8464
OFF_DN_QKV = 2304
OFF_DN_GATE = 5376
OFF_DN_A = 6400
OFF_DN_B = 6408
OFF_MERGE = 6416
EPS = 1e-6
SEM_LIMIT = 30000


class Buf:
    __slots__ = ("name", "w", "rs", "dsem", "excl")

    def __init__(self, name, excl=False):
        self.name = name
        self.w = None
        self.rs = []
        self.dsem = None
        self.excl = excl


class Op:
    __slots__ = ("eng", "fn", "deps", "dma", "ndma", "sem", "val", "signals", "idx")


class DmaSem:
    def __init__(self, sched):
        self.s = sched
        self.sem = sched.new_sem()
        self.count = 0

    def take(self, n):
        if self.count + 16 * n > SEM_LIMIT:
            self.sem = self.s.new_sem()
            self.count = 0
        self.count += 16 * n
        return self.sem, self.count


class Sched:
    ENGS = ("pe", "act", "dve", "pool", "sp")

    def __init__(self, nc):
        self.nc = nc
        self.q = {e: [] for e in self.ENGS}
        self.nsem = 0
        self.dma_ops = []
        self.all_bufs = []

    def new_sem(self):
        self.nsem += 1
        return self.nc.alloc_semaphore("s%d" % self.nsem)

    def buf(self, name):
        b = Buf(name)
        return b

    def bufs(self, name, n, excl=False):
        return [Buf("%s%d" % (name, i), excl) for i in range(n)]

    def op(self, eng, fn, reads=(), writes=(), dma=False, ndma=1):
        o = Op()
        o.eng = eng
        o.fn = fn
        o.dma = dma
        o.ndma = ndma
        o.signals = dma
        o.sem = None
        o.val = 0
        deps = []
        for b in reads:
            if b.w is not None:
                deps.append((b.w, True))
            if b.excl:
                for r in b.rs:
                    if r.eng != eng:
                        deps.append((r, False))
        for b in writes:
            if b.w is not None:
                deps.append((b.w, False))
            for r in b.rs:
                deps.append((r, False))
        fd = {}
        for d, raw in deps:
            if d is o:
                continue
            if not d.dma and not dma and d.eng == eng:
                if eng == "pe":
                    continue
            if d.dma and dma and not raw:
                pass
            fd[id(d)] = d
        o.deps = list(fd.values())
        for b in reads:
            b.rs.append(o)
        for b in writes:
            b.w = o
            b.rs = []
        if dma:
            wb = writes[0]
            if wb.dsem is None:
                wb.dsem = DmaSem(self)
            o.sem, o.val = wb.dsem.take(ndma)
            self.dma_ops.append(o)
        o.idx = len(self.q[eng])
        self.q[eng].append(o)
        return o

    def barrier(self, engines=("pe", "act", "dve", "sp")):
        lasts = []
        for e in ("pe", "act", "dve", "pool"):
            for o in reversed(self.q[e]):
                if not o.dma and o.fn is not None:
                    lasts.append(o)
                    break
        dm = list(self.dma_ops)
        self.dma_ops = []
        for e in engines:
            o = Op()
            o.eng = e
            o.fn = None
            o.dma = False
            o.ndma = 0
            o.signals = False
            o.sem = None
            o.val = 0
            o.deps = [d for d in lasts if d.eng != e or e != "pe"] + dm
            o.idx = len(self.q[e])
            self.q[e].append(o)

    def emit(self):
        nc = self.nc
        for e in self.ENGS:
            for o in self.q[e]:
                for d in o.deps:
                    if not d.dma:
                        d.signals = True
        for e in self.ENGS:
            sem = None
            cnt = 0
            for o in self.q[e]:
                if o.dma or not o.signals:
                    continue
                if sem is None or cnt >= SEM_LIMIT:
                    sem = self.new_sem()
                    cnt = 0
                cnt += 1
                o.sem = sem
                o.val = cnt
        sched = self

        def run(e, eng):
            waited = {}
            for o in sched.q[e]:
                for d in o.deps:
                    key = id(d.sem)
                    if waited.get(key, 0) >= d.val:
                        continue
                    waited[key] = d.val
                    eng.wait_ge(d.sem, d.val)
                if o.fn is None:
                    continue
                r = o.fn(eng)
                if o.dma:
                    if not isinstance(r, (list, tuple)):
                        r = [r]
                    assert len(r) == o.ndma, (len(r), o.ndma)
                    for ins in r:
                        ins.then_inc(o.sem, 16)
                elif o.signals:
                    r.then_inc(o.sem, 1)

        with nc.Block() as block:
            @block.tensor
            def _(eng):
                run("pe", eng)

            @block.scalar
            def _(eng):
                run("act", eng)

            @block.vector
            def _(eng):
                run("dve", eng)

            @block.gpsimd
            def _(eng):
                run("pool", eng)

            @block.sync
            def _(eng):
                run("sp", eng)


class Prog:
    def __init__(self, n_layers=DEPTH, stages=("ffn1", "mix", "ffn2"), dbg=None):
        self.n_layers = n_layers
        self.stages = stages
        self.dbg = dbg
        nc = bass.Bass("TRN2", target_bir_lowering=False)
        self.nc = nc
        self.S = Sched(nc)
        self.decl_io()
        self.alloc()
        self.build()
        self.S.emit()

    def decl_io(self):
        nc = self.nc
        L = DEPTH

        def inp(name, shape):
            return nc.dram_tensor(name, list(shape), F32, kind="ExternalInput").ap()

        self.x_d = inp("x", (T, D))
        self.c_d = inp("c", (128, KC))
        self.ada_w = inp("ada_w", (L, D, 9 * D))
        self.ada_b = inp("ada_b", (L, 128, 72))
        self.norms = inp("norms", (L, 128, 3 * KC))
        self.w_up = [inp("ffn1_w_up", (L, D, 2 * DFF)), inp("ffn2_w_up", (L, D, 2 * DFF))]
        self.w_dn = [inp("ffn1_w_down", (L, DFF, D)), inp("ffn2_w_down", (L, DFF, D))]
        self.w_in = inp("w_in", (L, D, N_IN))
        self.cst = inp("consts", (128, 1024))
        self.small_d = inp("small", (L, 128, 512))
        self.w_pa = inp("w_proj_att", (L, 256, D))
        self.w_pd = inp("w_proj_dn", (L, D, D))
        self.w_o = inp("w_out", (L, D, D))
        self.odn_d = nc.dram_tensor("odn_scr", [8, 128, T], BF16, kind="Internal").ap()
        self.odn_b = self.S.bufs("odn_d", 8)
        self.att_d = nc.dram_tensor("att_scr", [4, 64, T], BF16, kind="Internal").ap()
        self.att_b = self.S.bufs("att_d", 4)
        self.out_d = nc.dram_tensor("out", [T, D], F32, kind="ExternalOutput").ap()

    def alloc(self):
        nc = self.nc
        S = self.S
        self.xT = nc.alloc_sbuf_tensor("xT", [128, KC, T], F32)
        self.x_b = S.bufs("x", KC)
        self.hT = nc.alloc_sbuf_tensor("hT", [128, KC, T], BF16)
        self.h_b = S.bufs("h", KC)
        self.NW = 4
        self.wbuf = [nc.alloc_sbuf_tensor("wbuf%d" % i, [128, KC, 256], BF16) for i in range(self.NW)]
        self.w_b = S.bufs("w", self.NW)
        self.wi = 0
        self.wd = nc.alloc_sbuf_tensor("wd", [128, 11, D], BF16)
        self.wd_b = S.bufs("wd", 11)
        self.cf = nc.alloc_sbuf_tensor("cf", [128, 1024], F32)
        self.cf_b = S.buf("cf")
        self.cb = nc.alloc_sbuf_tensor("cb", [128, 1024], BF16)
        self.cb_b = S.buf("cb")
        self.modv = nc.alloc_sbuf_tensor("modv", [128, 72], F32)
        self.modv_b = S.buf("modv")
        self.adab = nc.alloc_sbuf_tensor("adab", [128, 72], F32)
        self.adab_b = S.buf("adab")
        self.nrm = nc.alloc_sbuf_tensor("nrm", [128, 3 * KC], F32)
        self.nrm_b = S.buf("nrm")
        self.modc = nc.alloc_sbuf_tensor("modc", [128, 9 * KC], F32)
        self.modc_b = S.buf("modc")
        self.cact = nc.alloc_sbuf_tensor("cact", [128, KC], BF16)
        self.cact_b = S.buf("cact")
        self.c32 = nc.alloc_sbuf_tensor("c32", [128, 2 * KC], F32)
        self.c32_b = S.buf("c32")
        self.AR = getattr(Prog, 'AR_OVERRIDE', 33 * 1024)
        self.ar = nc.alloc_sbuf_tensor("arena", [128, self.AR], BF16)
        self.ps = nc.alloc_psum_tensor("ps", [128, 8, 512], F32)
        self.ps_b = S.bufs("ps", 8, excl=True)
        self.pi = 0

    def bank(self):
        i = self.pi
        self.pi = (self.pi + 1) % 8
        return i

    def next_w(self):
        i = self.wi
        self.wi = (self.wi + 1) % self.NW
        return i

    def ar_bf(self, off, n):
        return self.ar[:, off:off + n]

    def ar_f32(self, off, n):
        return self.ar[:, off:off + 2 * n].bitcast(F32)

    def build(self):
        S = self.S
        self.load_consts()
        self.load_x()
        for l in range(self.n_layers):
            self.mod_layer(l)
            if "ffn1" in self.stages:
                self.norm_mod(l, 0)
                self.ffn(l, 0)
            if "mix" in self.stages:
                self.norm_mod(l, 1)
                self.mixer(l)
            if "ffn2" in self.stages:
                self.norm_mod(l, 2)
                self.ffn(l, 1)
        self.store_x()

    def load_consts(self):
        S = self.S
        cf, cb = self.cf, self.cb
        S.op("sp", lambda e: e.dma_start(out=cf[:], in_=self.cst[:, :]), [], [self.cf_b], dma=True)
        S.op("dve", lambda e: e.tensor_copy(out=cb[:], in_=cf[:]), [self.cf_b], [self.cb_b])
        c32 = self.c32
        S.op("sp", lambda e: e.dma_start(out=c32[:, 0:KC], in_=self.c_d[:, :]), [], [self.c32_b], dma=True)
        S.op("act", lambda e: e.activation(out=c32[:, KC:2 * KC], in_=c32[:, 0:KC], func=AF.Exp, scale=-1.0),
             [self.c32_b], [self.c32_b])
        S.op("dve", lambda e: e.tensor_scalar_add(out=c32[:, KC:2 * KC], in0=c32[:, KC:2 * KC], scalar1=1.0),
             [self.c32_b], [self.c32_b])
        S.op("dve", lambda e: e.reciprocal(out=c32[:, KC:2 * KC], in_=c32[:, KC:2 * KC]),
             [self.c32_b], [self.c32_b])
        S.op("dve", lambda e: e.tensor_tensor(out=self.cact[:], in0=c32[:, 0:KC], in1=c32[:, KC:2 * KC], op=ALU.mult),
             [self.c32_b], [self.cact_b])

    def ident_f(self):
        return self.cf[:, 0:128]

    def load_x(self):
        S = self.S
        stg = [self.ar_f32(0, D), self.ar_f32(2 * D, D)]
        stg_b = S.bufs("xstg", 2)
        xT = self.xT
        for b in range(NB):
            s = b % 2
            S.op("sp", lambda e, b=b, s=s: e.dma_start(out=stg[s], in_=self.x_d[b * 128:(b + 1) * 128, :]),
                 [], [stg_b[s]], dma=True)
            for half in range(2):
                pb = self.bank()
                for q in range(4):
                    c = half * 4 + q
                    S.op("pe", lambda e, c=c, s=s, pb=pb, q=q: e.transpose(
                        self.ps[:, pb, q * 128:(q + 1) * 128], stg[s][:, c * 128:(c + 1) * 128], self.ident_f()),
                        [stg_b[s], self.cf_b], [self.ps_b[pb]])
                psv = self.ps[:, pb, :].rearrange("p (q i) -> p q i", i=128)
                dst = xT[:, half * 4:(half + 1) * 4, b * 128:(b + 1) * 128]
                xb = self.x_b[half * 4:(half + 1) * 4]
                if half == 0:
                    S.op("dve", lambda e, psv=psv, dst=dst: e.tensor_copy(out=dst, in_=psv), [self.ps_b[pb]], xb)
                else:
                    S.op("act", lambda e, psv=psv, dst=dst: e.copy(out=dst, in_=psv), [self.ps_b[pb]], xb)
        S.barrier()

    def store_x(self):
        S = self.S
        S.barrier()
        stg = [self.ar_f32(0, D), self.ar_f32(2 * D, D)]
        stg_b = [S.bufs("ostg%d_" % i, 2) for i in range(2)]
        xT = self.xT
        outs = []
        for b in range(NB):
            s = b % 2
            for half in range(2):
                pb = self.bank()
                for q in range(4):
                    c = half * 4 + q
                    S.op("pe", lambda e, c=c, b=b, pb=pb, q=q: e.transpose(
                        self.ps[:, pb, q * 128:(q + 1) * 128], xT[:, c, b * 128:(b + 1) * 128], self.ident_f()),
                        [self.x_b[c], self.cf_b], [self.ps_b[pb]])
                if half == 0:
                    S.op("dve", lambda e, s=s, pb=pb: e.tensor_copy(
                        out=stg[s][:, 0:512], in_=self.ps[:, pb, :]), [self.ps_b[pb]], [stg_b[s][0]])
                else:
                    S.op("act", lambda e, s=s, pb=pb: e.copy(
                        out=stg[s][:, 512:1024], in_=self.ps[:, pb, :]), [self.ps_b[pb]], [stg_b[s][1]])
            ob = S.buf("outd%d" % b)
            o = S.op("sp", lambda e, b=b, s=s: e.dma_start(out=self.out_d[b * 128:(b + 1) * 128, :], in_=stg[s]),
                     stg_b[s], [ob], dma=True)
            outs.append(ob)
        S.op("sp", None, outs, [])

    def mod_layer(self, l):
        S = self.S
        S.op("sp", lambda e: e.dma_start(out=self.adab[:], in_=self.ada_b[l]), [], [self.adab_b], dma=True)
        S.op("sp", lambda e: e.dma_start(out=self.nrm[:], in_=self.norms[l]), [], [self.nrm_b], dma=True)
        aw = self.ada_w[l].rearrange("(kc p) n -> p kc n", p=128)
        pb = self.bank()
        for blk in range(9 * D // 256):
            wi = self.next_w()
            wb = self.wbuf[wi]
            S.op("pool", lambda e, blk=blk, wb=wb: e.dma_start(out=wb[:], in_=aw[:, :, blk * 256:(blk + 1) * 256]),
                 [], [self.w_b[wi]], dma=True)
            for j in range(2):
                col = blk * 2 + j
                for kc in range(KC):
                    S.op("pe", lambda e, wb=wb, j=j, kc=kc, col=col: e.matmul(
                        self.ps[:, pb, col:col + 1], wb[:, kc, j * 128:(j + 1) * 128], self.cact[:, kc:kc + 1],
                        start=(kc == 0), stop=(kc == KC - 1)),
                        [self.w_b[wi], self.cact_b], [self.ps_b[pb]])
        S.op("dve", lambda e: e.tensor_tensor(out=self.modv[:], in0=self.ps[:, pb, 0:72], in1=self.adab[:], op=ALU.add),
             [self.ps_b[pb], self.adab_b], [self.modv_b])
        mv, mc, nr = self.modv, self.modc, self.nrm
        for j in range(3):
            S.op("dve", lambda e, j=j: e.scalar_tensor_tensor(
                out=mc[:, j * 24:j * 24 + 8], in0=mv[:, (3 * j + 1) * 8:(3 * j + 2) * 8], scalar=1.0,
                in1=nr[:, j * 8:(j + 1) * 8], op0=ALU.add, op1=ALU.mult),
                [self.modv_b, self.nrm_b], [self.modc_b])
            S.op("dve", lambda e, j=j: e.tensor_copy(out=mc[:, j * 24 + 8:j * 24 + 16], in_=mv[:, (3 * j) * 8:(3 * j + 1) * 8]),
                 [self.modv_b], [self.modc_b])
            gsc = 1.0 if j == 1 else 0.5
            S.op("dve", lambda e, j=j, gsc=gsc: e.tensor_scalar(
                out=mc[:, j * 24 + 16:j * 24 + 24], in0=mv[:, (3 * j + 2) * 8:(3 * j + 3) * 8], scalar1=gsc, scalar2=None,
                op0=ALU.mult),
                [self.modv_b], [self.modc_b])

    def norm_mod(self, l, j):
        S = self.S
        S.barrier()
        xT, hT = self.xT, self.hT
        sq = [self.ar_bf(0, T), self.ar_bf(T, T)]
        sq_b = S.bufs("sq", 2)
        rstd = self.ar_f32(2 * T, T)
        rstd_b = S.buf("rstd")
        tmp = [self.ar_bf(6 * T, T), self.ar_bf(7 * T, T)]
        tmp_b = S.bufs("ntmp", 2)
        ones = self.cb[:, 256:384]
        pbs = [self.bank() for _ in range(4)]
        for kc in range(KC):
            s = kc % 2
            S.op("act", lambda e, kc=kc, s=s: e.activation(out=sq[s], in_=xT[:, kc, :], func=AF.Square),
                 [self.x_b[kc]], [sq_b[s]])
            for t in range(NT):
                S.op("pe", lambda e, kc=kc, s=s, t=t: e.matmul(
                    self.ps[:, pbs[t], :], ones, sq[s][:, t * 512:(t + 1) * 512], start=(kc == 0), stop=(kc == KC - 1)),
                    [sq_b[s], self.cb_b], [self.ps_b[pbs[t]]])
        for t in range(NT):
            S.op("act", lambda e, t=t: e.activation(out=rstd[:, t * 512:(t + 1) * 512], in_=self.ps[:, pbs[t], :],
                                                    func=AF.Sqrt, scale=1.0 / D, bias=self.cf[:, 896:897]),
                 [self.ps_b[pbs[t]], self.cf_b], [rstd_b])
        S.op("dve", lambda e: e.reciprocal(out=rstd, in_=rstd), [rstd_b], [rstd_b])
        mc = self.modc
        for kc in range(KC):
            s = kc % 2
            S.op("dve", lambda e, kc=kc, s=s: e.tensor_tensor(out=tmp[s], in0=xT[:, kc, :], in1=rstd, op=ALU.mult),
                 [self.x_b[kc], rstd_b], [tmp_b[s]])
            S.op("act", lambda e, kc=kc, s=s: e.activation(
                out=hT[:, kc, :], in_=tmp[s], func=AF.Identity,
                scale=mc[:, j * 24 + kc:j * 24 + kc + 1], bias=mc[:, j * 24 + 8 + kc:j * 24 + 8 + kc + 1]),
                [tmp_b[s], self.modc_b], [self.h_b[kc]])

    def ffn(self, l, which):
        S = self.S
        S.barrier()
        j = 0 if which == 0 else 2
        xT, hT = self.xT, self.hT
        wu = self.w_up[which][l].rearrange("(kc p) n -> p kc n", p=128)
        wdn = self.w_dn[which][l]
        act = self.ar[:, 0:11 * T].rearrange("p (j t) -> p j t", t=T)
        act_b = S.bufs("act", 11)
        sg = [self.ar_f32(11 * T, 512), self.ar_f32(11 * T + 1024, 512)]
        sg_b = S.bufs("sg", 2)
        mc = self.modc
        sgi = 0
        for half in range(2):
            def load_wd():
                for jj in range(11):
                    r0 = (half * 11 + jj) * 128
                    S.op("pool", lambda e, jj=jj, r0=r0: e.dma_start(out=self.wd[:, jj, :], in_=wdn[r0:r0 + 128, :]),
                         [], [self.wd_b[jj]], dma=True)
            for blk in range(6):
                nj = 2 if blk < 5 else 1
                j0 = half * 11 + blk * 2
                wg_i = self.next_w()
                wu_i = self.next_w()
                wgb, wub = self.wbuf[wg_i], self.wbuf[wu_i]
                ncol = nj * 128
                S.op("pool", lambda e, wgb=wgb, j0=j0, ncol=ncol: e.dma_start(
                    out=wgb[:, :, 0:ncol], in_=wu[:, :, j0 * 128:j0 * 128 + ncol]), [], [self.w_b[wg_i]], dma=True)
                S.op("pool", lambda e, wub=wub, j0=j0, ncol=ncol: e.dma_start(
                    out=wub[:, :, 0:ncol], in_=wu[:, :, DFF + j0 * 128:DFF + j0 * 128 + ncol]), [], [self.w_b[wu_i]], dma=True)
                if blk == 2:
                    load_wd()
                for jj in range(nj):
                    ja = blk * 2 + jj
                    for t in range(NT):
                        pg, pu = self.bank(), self.bank()
                        for kc in range(KC):
                            S.op("pe", lambda e, kc=kc, jj=jj, t=t, pg=pg, wgb=wgb: e.matmul(
                                self.ps[:, pg, :], wgb[:, kc, jj * 128:(jj + 1) * 128], hT[:, kc, t * 512:(t + 1) * 512],
                                start=(kc == 0), stop=(kc == KC - 1)),
                                [self.w_b[wg_i], self.h_b[kc]], [self.ps_b[pg]])
                        for kc in range(KC):
                            S.op("pe", lambda e, kc=kc, jj=jj, t=t, pu=pu, wub=wub: e.matmul(
                                self.ps[:, pu, :], wub[:, kc, jj * 128:(jj + 1) * 128], hT[:, kc, t * 512:(t + 1) * 512],
                                start=(kc == 0), stop=(kc == KC - 1)),
                                [self.w_b[wu_i], self.h_b[kc]], [self.ps_b[pu]])
                        s = sgi % 2
                        sgi += 1
                        S.op("act", lambda e, s=s, pg=pg: e.activation(out=sg[s], in_=self.ps[:, pg, :], func=AF.Silu),
                             [self.ps_b[pg]], [sg_b[s]])
                        S.op("dve", lambda e, s=s, pu=pu, ja=ja, t=t: e.tensor_tensor(
                            out=act[:, ja, t * 512:(t + 1) * 512], in0=sg[s], in1=self.ps[:, pu, :], op=ALU.mult),
                            [sg_b[s], self.ps_b[pu]], [act_b[ja]])
            for t in range(NT):
                for dc in range(KC):
                    pb = self.bank()
                    for jj in range(11):
                        S.op("pe", lambda e, jj=jj, dc=dc, t=t, pb=pb: e.matmul(
                            self.ps[:, pb, :], self.wd[:, jj, dc * 128:(dc + 1) * 128], act[:, jj, t * 512:(t + 1) * 512],
                            start=(jj == 0), stop=(jj == 10)),
                            [self.wd_b[jj], act_b[jj]], [self.ps_b[pb]])
                    S.op("dve", lambda e, dc=dc, t=t, pb=pb: e.scalar_tensor_tensor(
                        out=xT[:, dc, t * 512:(t + 1) * 512], in0=self.ps[:, pb, :],
                        scalar=mc[:, j * 24 + 16 + dc:j * 24 + 16 + dc + 1],
                        in1=xT[:, dc, t * 512:(t + 1) * 512], op0=ALU.mult, op1=ALU.add),
                        [self.ps_b[pb], self.modc_b, self.x_b[dc]], [self.x_b[dc]])

    def PE(self, out, lhsT, rhs, r, w, start=True, stop=True):
        return self.S.op("pe", lambda e: e.matmul(out, lhsT, rhs, start=start, stop=stop), r, w)

    def TR(self, out, in_, ident, r, w):
        return self.S.op("pe", lambda e: e.transpose(out, in_, ident), r, w)

    def ACT(self, out, in_, func, r, w, scale=None, bias=None, accum=None):
        kw = {}
        if scale is not None:
            kw["scale"] = scale
        if bias is not None:
            kw["bias"] = bias
        if accum is not None:
            kw["accum_out"] = accum
        return self.S.op("act", lambda e: e.activation(out=out, in_=in_, func=func, **kw), r, w)

    def TT(self, out, in0, in1, op, r, w):
        return self.S.op("dve", lambda e: e.tensor_tensor(out=out, in0=in0, in1=in1, op=op), r, w)

    def TS(self, out, in0, s1, op0, r, w, s2=None, op1=None):
        if op1 is None:
            return self.S.op("dve", lambda e: e.tensor_scalar(out=out, in0=in0, scalar1=s1, scalar2=None, op0=op0), r, w)
        return self.S.op("dve", lambda e: e.tensor_scalar(out=out, in0=in0, scalar1=s1, scalar2=s2, op0=op0, op1=op1), r, w)

    def STT(self, out, in0, sc, in1, op0, op1, r, w):
        return self.S.op("dve", lambda e: e.scalar_tensor_tensor(out=out, in0=in0, scalar=sc, in1=in1, op0=op0, op1=op1), r, w)

    def CPY(self, eng, out, in_, r, w):
        if eng == "dve":
            return self.S.op("dve", lambda e: e.tensor_copy(out=out, in_=in_), r, w)
        return self.S.op("act", lambda e: e.copy(out=out, in_=in_), r, w)

    def tmp(self, name, n, dt=BF16):
        if name in self.tmps:
            return self.tmps[name]
        ne = n if dt == BF16 else 2 * n
        if self.aoff % 2:
            self.aoff += 1
        off = self.aoff
        self.aoff += ne
        assert self.aoff <= self.AR + 11 * D, ("arena overflow", name, self.aoff)
        if off + ne <= self.AR:
            base = self.ar[:, off:off + ne]
        else:
            if off < self.AR:
                off = self.AR
                self.aoff = off + ne
            o2 = off - self.AR
            base = self.wd[:].rearrange("p a b -> p (a b)")[:, o2:o2 + ne]
        ap = base if dt == BF16 else base.bitcast(F32)
        if not hasattr(self, "toff"):
            self.toff = {}
        self.toff[name] = (off, ne, dt == F32)
        b = self.S.buf(name)
        self.tmps[name] = (ap, b)
        return ap, b

    def mixer(self, l):
        S = self.S
        S.barrier(engines=("pe", "act", "dve", "sp", "pool"))
        self.tmps = {}
        self.aoff = 0
        self.mix_setup(l)
        self.dn_branch(l)
        self.att_branch(l)
        self.mix_final(l)
        S.barrier(engines=("pe", "act", "dve", "sp", "pool"))

    def mix_setup(self, l):
        S = self.S
        sm, sm_b = self.tmp("small", 512, F32)
        self.sm, self.sm_b = sm, sm_b
        S.op("sp", lambda e: e.dma_start(out=sm, in_=self.small_d[l]), [], [sm_b], dma=True)

    def dn_branch(self, l):
        S = self.S
        hT = self.hT
        sm, sm_b = self.sm, self.sm_b
        cb, cf = self.cb, self.cf
        ident_b = cb[:, 0:128]
        U_b = cb[:, 128:256]
        ones_b = cb[:, 256:384]
        SU_f = cf[:, 384:512]
        SL_b = cb[:, 512:640]
        ones_f = cf[:, 256:384]
        ident_f = cf[:, 0:128]
        tri_f = cf[:, 128:256]
        win = self.w_in[l].rearrange("(kc p) n -> p kc n", p=128)
        CB = [self.cb_b]
        CF = [self.cf_b]
        wi = self.next_w()
        wab = self.wbuf[wi]
        S.op("pool", lambda e: e.dma_start(out=wab[:, :, 0:16], in_=win[:, :, OFF_DN_A:OFF_DN_A + 16]), [], [self.w_b[wi]], dma=True)
        pab = self.bank()
        for blk in range(NB):
            for kc in range(KC):
                self.PE(self.ps[:, pab, blk * 16:(blk + 1) * 16], hT[:, kc, blk * 128:(blk + 1) * 128], wab[:, kc, 0:16],
                        [self.h_b[kc], self.w_b[wi]], [self.ps_b[pab]], start=(kc == 0), stop=(kc == KC - 1))
        abv = self.ps[:, pab, 0:256].rearrange("p (b c) -> p b c", c=16)
        g_t, g_b = self.tmp("g_tok", 128, F32)
        be_t, be_b = self.tmp("be_tok", 128, F32)
        gc_t, gc_b = self.tmp("gc_tok", 128, F32)
        t1, t1_b = self.tmp("ab_t1", 128, F32)
        g3 = g_t.rearrange("p (b c) -> p b c", c=8)
        be3 = be_t.rearrange("p (b c) -> p b c", c=8)
        t13 = t1.rearrange("p (b c) -> p b c", c=8)
        dtb = sm[:, 256:384].rearrange("p (b c) -> p b c", c=8)
        self.TT(t13, abv[:, :, 0:8], dtb, ALU.add, [self.ps_b[pab], sm_b], [t1_b])
        self.ACT(t1, t1, AF.Exp, [t1_b], [t1_b])
        self.ACT(t1, t1, AF.Ln, [t1_b], [t1_b], bias=cf[:, 898:899])
        ea, ea_b = self.tmp("expalog", 128, F32)
        self.ACT(ea, sm[:, 384:512], AF.Exp, [sm_b], [ea_b])
        self.STT(g_t, t1, -1.0, ea, ALU.mult, ALU.mult, [t1_b, ea_b], [g_b])
        self.ACT(be3, abv[:, :, 8:16], AF.Exp, [self.ps_b[pab]], [be_b], scale=-1.0)
        self.TS(be_t, be_t, 1.0, ALU.add, [be_b], [be_b])
        S.op("dve", lambda e: e.reciprocal(out=be_t, in_=be_t), [be_b], [be_b])
        pgc = self.bank()
        self.PE(self.ps[:, pgc, 0:128], tri_f, g_t, CF + [g_b], [self.ps_b[pgc]])
        self.CPY("dve", gc_t, self.ps[:, pgc, 0:128], [self.ps_b[pgc]], [gc_b])
        nbe_t, nbe_b = self.tmp("nbe_tok", 128, F32)
        begc_t, begc_b = self.tmp("begc_tok", 128, F32)
        self.TS(nbe_t, be_t, -1.0, ALU.mult, [be_b], [nbe_b])
        self.ACT(begc_t, gc_t, AF.Exp, [gc_b], [begc_b])
        self.TT(begc_t, begc_t, be_t, ALU.mult, [begc_b, be_b], [begc_b])

        zc = [self.tmp("zc%d" % f, 2052) for f in range(3)]
        sgate, sgate_b = self.tmp("sgate", T)
        fT = [self.tmp("fT%d" % f, T) for f in range(3)]
        ktok, ktok_b = self.tmp("ktok", T)
        vtok, vtok_b = self.tmp("vtok", T)
        ktok3 = ktok.rearrange("p (b d) -> p b d", d=128)
        vtok3 = vtok.rearrange("p (b d) -> p b d", d=128)
        sq, sq_b = self.tmp("dsq", 512)
        rs, rs_b = self.tmp("drs", 512, F32)
        dg = [self.tmp("diag%d" % i, 128) for i in range(12)]
        S32, S32_b = self.tmp("S32", 128, F32)
        S16, S16_b = self.tmp("S16", 128)
        ost = [self.tmp("ost0", T)] * 2
        for f in range(3):
            S.op("dve", lambda e, f=f: e.memset(zc[f][0][:, 0:4], 0.0), [], [zc[f][1]])

        for h in range(8):
            cols = [OFF_DN_QKV + h * 128, OFF_DN_QKV + 1024 + h * 128, OFF_DN_QKV + 2048 + h * 128, OFF_DN_GATE + h * 128]
            w1 = self.next_w()
            w2 = self.next_w()
            wb1, wb2 = self.wbuf[w1], self.wbuf[w2]
            S.op("pool", lambda e, wb1=wb1, cols=cols: [e.dma_start(out=wb1[:, :, 0:128], in_=win[:, :, cols[0]:cols[0] + 128]),
                                                          e.dma_start(out=wb1[:, :, 128:256], in_=win[:, :, cols[1]:cols[1] + 128])],
                 [], [self.w_b[w1]], dma=True, ndma=2)
            S.op("pool", lambda e, wb2=wb2, cols=cols: [e.dma_start(out=wb2[:, :, 0:128], in_=win[:, :, cols[2]:cols[2] + 128]),
                                                          e.dma_start(out=wb2[:, :, 128:256], in_=win[:, :, cols[3]:cols[3] + 128])],
                 [], [self.w_b[w2]], dma=True, ndma=2)
            for f in range(3):
                for i in range(4):
                    c = f * 8 + h
                    self.TS(dg[f * 4 + i][0], ident_b, sm[:, c * 4 + i:c * 4 + i + 1], ALU.mult, CB + [sm_b], [dg[f * 4 + i][1]])
            for f in range(4):
                wb, wbb = (wb1, self.w_b[w1]) if f < 2 else (wb2, self.w_b[w2])
                co = (f % 2) * 128
                for t in range(NT):
                    pb = self.bank()
                    for kc in range(KC):
                        self.PE(self.ps[:, pb, :], wb[:, kc, co:co + 128], hT[:, kc, t * 512:(t + 1) * 512],
                                [wbb, self.h_b[kc]], [self.ps_b[pb]], start=(kc == 0), stop=(kc == KC - 1))
                    if f < 3:
                        self.CPY("dve" if t % 2 == 0 else "act", zc[f][0][:, 4 + t * 512:4 + (t + 1) * 512], self.ps[:, pb, :],
                                 [self.ps_b[pb]], [zc[f][1]])
                    else:
                        self.ACT(sgate[:, t * 512:(t + 1) * 512], self.ps[:, pb, :], AF.Silu, [self.ps_b[pb]], [sgate_b])
            for f in range(3):
                for t in range(NT):
                    pb = self.bank()
                    for i in range(4):
                        self.PE(self.ps[:, pb, :], dg[f * 4 + i][0], zc[f][0][:, 1 + t * 512 + i:1 + t * 512 + i + 512],
                                [dg[f * 4 + i][1], zc[f][1]], [self.ps_b[pb]], start=(i == 0), stop=(i == 3))
                    self.ACT(fT[f][0][:, t * 512:(t + 1) * 512], self.ps[:, pb, :], AF.Silu, [self.ps_b[pb]], [fT[f][1]])
            for f in range(2):
                for t in range(NT):
                    sl = slice(t * 512, (t + 1) * 512)
                    self.TT(sq, fT[f][0][:, sl], fT[f][0][:, sl], ALU.mult, [fT[f][1]], [sq_b])
                    pb = self.bank()
                    self.PE(self.ps[:, pb, :], ones_b, sq, CB + [sq_b], [self.ps_b[pb]])
                    if f == 0:
                        self.ACT(rs, self.ps[:, pb, :], AF.Sqrt, [self.ps_b[pb]] + CF, [rs_b], scale=128.0, bias=cf[:, 897:898])
                    else:
                        self.ACT(rs, self.ps[:, pb, :], AF.Sqrt, [self.ps_b[pb]] + CF, [rs_b], scale=1.0, bias=cf[:, 896:897])
                    S.op("dve", lambda e: e.reciprocal(out=rs, in_=rs), [rs_b], [rs_b])
                    self.TT(fT[f][0][:, sl], fT[f][0][:, sl], rs, ALU.mult, [fT[f][1], rs_b], [fT[f][1]])
            for (src, dst3, dst_b) in ((fT[1], ktok3, ktok_b), (fT[2], vtok3, vtok_b)):
                for g4 in range(4):
                    pb = self.bank()
                    pbv = self.ps[:, pb, :].bitcast(BF16)
                    for q in range(4):
                        blk = g4 * 4 + q
                        self.TR(pbv[:, q * 128:(q + 1) * 128], src[0][:, blk * 128:(blk + 1) * 128], ident_b,
                                [src[1]] + CB, [self.ps_b[pb]])
                    self.CPY("dve" if g4 % 2 == 0 else "act", dst3[:, g4 * 4:(g4 + 1) * 4, :],
                             pbv[:, 0:512].rearrange("p (q d) -> p q d", d=128), [self.ps_b[pb]], [dst_b])
            S.op("dve", lambda e: e.memset(S32, 0.0), [], [S32_b])
            S.op("dve", lambda e: e.memset(S16, 0.0), [], [S16_b])
            oT, oT_b = ost[h % 2]
            qT, kT = fT[0], fT[1]
            for n in range(NB):
                u = "%d" % (n % 2)
                bs = slice(n * 128, (n + 1) * 128)
                col = n * 8 + h
                gcc = gc_t[:, col:col + 1]
                rr, rr_b = self.tmp("rr" + u, 256, F32)
                self.TS(rr[:, 0:128], tri_f, g_t[:, col:col + 1], ALU.mult, CF + [g_b], [rr_b])
                self.TS(rr[:, 128:256], ident_f, be_t[:, col:col + 1], ALU.mult, CF + [be_b], [rr_b])
                pR = self.bank()
                self.PE(self.ps[:, pR, 0:256], ones_f, rr, CF + [rr_b], [self.ps_b[pR]])
                Rgc = self.ps[:, pR, 0:128]
                Rbe = self.ps[:, pR, 128:256]
                dmn, dmn_b = self.tmp("dmn" + u, 128, F32)
                dmx, dmx_b = self.tmp("dmx" + u, 128, F32)
                Rex, Rex_b = self.tmp("Rex" + u, 128, F32)
                glc, glc_b = self.tmp("glc" + u, 2, F32)
                nRb, nRb_b = self.tmp("nRb" + u, 128, F32)
                self.TS(dmn, Rgc, gcc, ALU.subtract, [self.ps_b[pR], gc_b], [dmn_b], s2=0.0, op1=ALU.min)
                self.TS(dmx, Rgc, gcc, ALU.subtract, [self.ps_b[pR], gc_b], [dmx_b], s2=0.0, op1=ALU.max)
                self.STT(nRb, Rbe, -1.0, SU_f, ALU.mult, ALU.mult, [self.ps_b[pR]] + CF, [nRb_b])
                self.CPY("dve", glc[:, 0:1], self.ps[:, pR, 127:128], [self.ps_b[pR]], [glc_b])
                self.ACT(Rex, Rgc, AF.Exp, [self.ps_b[pR]], [Rex_b])
                LtM, LtM_b = self.tmp("LtM" + u, 128, F32)
                LM, LM_b = self.tmp("LM" + u, 128, F32)
                self.ACT(LtM, dmn, AF.Exp, [dmn_b], [LtM_b])
                self.ACT(LM, dmx, AF.Exp, [dmx_b], [LM_b], scale=-1.0)
                self.TT(LtM, LtM, tri_f, ALU.mult, [LtM_b] + CF, [LtM_b])
                self.TT(LM, LM, cf[:, 512:640], ALU.mult, [LM_b] + CF, [LM_b])
                kdc, kdc_b = self.tmp("kdc" + u, 2, F32)
                self.ACT(kdc[:, 0:1], gcc, AF.Exp, [gc_b, glc_b], [kdc_b], scale=-1.0, bias=glc[:, 0:1])
                pK = self.bank()
                self.PE(self.ps[:, pK, 0:128], kT[0][:, bs], kT[0][:, bs], [kT[1]], [self.ps_b[pK]])
                self.PE(self.ps[:, pK, 128:256], kT[0][:, bs], qT[0][:, bs], [kT[1], qT[1]], [self.ps_b[pK]])
                KK = self.ps[:, pK, 0:128]
                QKt = self.ps[:, pK, 128:256]
                X = [self.tmp("X%d_%s" % (m, u), 128, F32) for m in range(6)]
                Xt = [self.tmp("Xt%d_%s" % (m, u), 128, F32) for m in range(5)]
                bt1, bt1_b = self.tmp("bt1" + u, 128, F32)
                Bf, Bf_b = self.tmp("Bf" + u, 128, F32)
                Bo, Bo_b = self.tmp("Bo" + u, 128, F32)
                BD = cf[:, 768:896]
                self.TT(bt1, KK, LtM, ALU.mult, [self.ps_b[pK], LtM_b], [bt1_b])
                self.TT(Bf, bt1, nRb, ALU.mult, [bt1_b, nRb_b], [Bf_b])
                self.TT(X[0][0], Bf, BD, ALU.mult, [Bf_b] + CF, [X[0][1]])
                self.TT(Bo, Bf, X[0][0], ALU.subtract, [Bf_b, X[0][1]], [Bo_b])
                self.STT(bt1, KK, nbe_t[:, col:col + 1], LM, ALU.mult, ALU.mult, [self.ps_b[pK], nbe_b, LM_b, bt1_b], [bt1_b])
                self.TT(Xt[0][0], bt1, BD, ALU.mult, [bt1_b] + CF, [Xt[0][1]])
                Aq, Aq_b = self.tmp("Aq" + u, 128)
                self.TT(Aq, QKt, LtM, ALU.mult, [self.ps_b[pK], LtM_b], [Aq_b])
                qd, qd_b = self.tmp("qd" + u, 128)
                self.TT(qd, qT[0][:, bs], Rex, ALU.mult, [qT[1], Rex_b], [qd_b])
                kd, kd_b = self.tmp("kd" + u, 128)
                self.TS(kd, ktok3[:, n, :], kdc[:, 0:1], ALU.mult, [ktok_b, kdc_b], [kd_b])
                for m in range(5):
                    pb = self.bank()
                    self.PE(self.ps[:, pb, 0:128], Xt[m][0], X[m][0], [Xt[m][1], X[m][1]], [self.ps_b[pb]])
                    if m < 4:
                        self.PE(self.ps[:, pb, 128:256], X[m][0], Xt[m][0], [Xt[m][1], X[m][1]], [self.ps_b[pb]])
                    self.CPY("act", X[m + 1][0], self.ps[:, pb, 0:128], [self.ps_b[pb]], [X[m + 1][1]])
                    if m < 4:
                        self.CPY("act", Xt[m + 1][0], self.ps[:, pb, 128:256], [self.ps_b[pb]], [Xt[m + 1][1]])
                ya = [self.tmp("y%d_%s" % (i, u), 256, F32) for i in range(3)]
                self.TS(ya[0][0][:, 0:128], vtok3[:, n, :], be_t[:, col:col + 1], ALU.mult, [vtok_b, be_b], [ya[0][1]])
                self.TS(ya[0][0][:, 128:256], ktok3[:, n, :], begc_t[:, col:col + 1], ALU.mult, [ktok_b, begc_b], [ya[0][1]])

                def neumann(cur, other):
                    for m in range(5, -1, -1):
                        pb = self.bank()
                        self.PE(self.ps[:, pb, 0:256], X[m][0], ya[cur][0], [X[m][1], ya[cur][1]], [self.ps_b[pb]])
                        self.TT(ya[other][0], self.ps[:, pb, 0:256], ya[cur][0], ALU.add, [self.ps_b[pb], ya[cur][1]], [ya[other][1]])
                        cur, other = other, cur
                    return cur, other
                zc_i, fr = neumann(0, 1)
                pb = self.bank()
                self.PE(self.ps[:, pb, 0:256], Bo, ya[zc_i][0], [Bo_b, ya[zc_i][1]], [self.ps_b[pb]])
                self.CPY("dve", ya[2][0], self.ps[:, pb, 0:256], [self.ps_b[pb]], [ya[2][1]])
                cur, other = 2, fr
                for m in range(5, -1, -1):
                    pb = self.bank()
                    self.PE(self.ps[:, pb, 0:256], X[m][0], ya[cur][0], [X[m][1], ya[cur][1]], [self.ps_b[pb]])
                    self.TT(ya[other][0], self.ps[:, pb, 0:256], ya[cur][0], ALU.add, [self.ps_b[pb], ya[cur][1]], [ya[other][1]])
                    cur, other = other, cur
                yb, yb_b = self.tmp("yb" + u, 256)
                self.TT(yb, ya[zc_i][0], ya[cur][0], ALU.add, [ya[zc_i][1], ya[cur][1]], [yb_b])
                y, y_b = yb, yb_b
                wT, wT_b = self.tmp("wT" + u, 128)
                pb = self.bank()
                pbv = self.ps[:, pb, :].bitcast(BF16)
                self.TR(pbv[:, 0:128], y[:, 128:256], ident_b, [y_b] + CB, [self.ps_b[pb]])
                self.CPY("act", wT, pbv[:, 0:128], [self.ps_b[pb]], [wT_b])
                pa = self.bank()
                po = self.bank()
                self.PE(self.ps[:, pa, 0:128], wT, S16, [wT_b, S16_b], [self.ps_b[pa]])
                vn, vn_b = self.tmp("vn" + u, 128)
                self.TT(vn, y[:, 0:128], self.ps[:, pa, 0:128], ALU.subtract, [y_b, self.ps_b[pa]], [vn_b])
                self.PE(self.ps[:, po, 0:128], qd, S16, [qd_b, S16_b], [self.ps_b[po]], start=True, stop=False)
                self.PE(self.ps[:, po, 0:128], Aq, vn, [Aq_b, vn_b], [self.ps_b[po]], start=False, stop=True)
                pd = self.bank()
                self.PE(self.ps[:, pd, 0:128], kd, vn, [kd_b, vn_b], [self.ps_b[pd]])
                self.STT(S32, S32, Rex[:, 127:128], self.ps[:, pd, 0:128], ALU.mult, ALU.add, [S32_b, Rex_b, self.ps_b[pd]], [S32_b])
                self.CPY("act", S16, S32, [S32_b], [S16_b])
                oj, oj_b = self.tmp("oj" + u, 128, F32)
                ss, ss_b = self.tmp("oss" + u, 2, F32)
                self.ACT(oj, self.ps[:, po, 0:128], AF.Square, [self.ps_b[po]], [oj_b, ss_b], accum=ss[:, 0:1])
                self.ACT(ss[:, 1:2], ss[:, 0:1], AF.Ln, [ss_b] + CF, [ss_b], scale=1.0 / 128.0, bias=cf[:, 896:897])
                self.ACT(ss[:, 1:2], ss[:, 1:2], AF.Exp, [ss_b], [ss_b], scale=-0.5)
                on, on_b = self.tmp("on" + u, 128)
                self.STT(on, self.ps[:, po, 0:128], ss[:, 1:2], sm[:, 128:256], ALU.mult, ALU.mult, [self.ps_b[po], ss_b, sm_b], [on_b])
                pb = self.bank()
                pbv = self.ps[:, pb, :].bitcast(BF16)
                self.TR(pbv[:, 0:128], on, ident_b, [on_b] + CB, [self.ps_b[pb]])
                self.TT(oT[:, bs], pbv[:, 0:128], sgate[:, bs], ALU.mult, [self.ps_b[pb], sgate_b], [oT_b])
            S.op("sp", lambda e, h=h, oT=oT: e.dma_start(out=self.odn_d[h], in_=oT), [oT_b], [self.odn_b[h]], dma=True)

    def att_branch(self, l):
        S = self.S
        S.barrier()
        self.tmps = {}
        self.aoff = 0
        hT = self.hT
        cb, cf = self.cb, self.cf
        CB, CF = [self.cb_b], [self.cf_b]
        sm, sm_b = self.tmp("small", 512, F32)
        sm_b = self.sm_b
        win = self.w_in[l].rearrange("(kc p) n -> p kc n", p=128)
        attn, attn_b = self.tmp("attn", 4 * T)
        self.attn, self.attn_b = attn, attn_b
        mrg, mrg_b = self.tmp("mrg", 8 * T)
        self.mrg, self.mrg_b = mrg, mrg_b
        acc, acc_b = self.tmp("attacc", 2 * T, F32)
        acc3 = acc.rearrange("p (h t) -> p h t", t=T)
        attn3 = attn.rearrange("p (h t) -> p h t", t=T)
        BD_b = cb[:, 768:896]
        msk = cb[:, 640:768]
        U_b = cb[:, 128:256]
        self.aoff = self.AR
        qn, qn_b = self.tmp("qn", T)
        kn, kn_b = self.tmp("kn", T)
        z16, z16_b = self.tmp("z16", 512)
        sq, sq_b = self.tmp("asq", 512)
        rs, rs_b = self.tmp("ars", 512, F32)
        vt, vt_b = self.tmp("vt", 16 * 2 * 66)
        vt4 = vt.rearrange("p (b h d) -> p b h d", h=2, d=66)
        pt = [self.tmp("pt%d" % i, 256) for i in range(2)]
        S.op("dve", lambda e: e.memset(vt, 1.0), [], [vt_b])
        groups = ((128, 1), (512, 4), (2048, 16))
        pti = 0
        for c in (0, 2, 4, 1, 3, 5):
            g = c // 2
            d = groups[g][1]
            nblk = (T // d) // 128
            w1 = self.next_w()
            w2 = self.next_w()
            wb1, wb2 = self.wbuf[w1], self.wbuf[w2]
            S.op("pool", lambda e, wb1=wb1, c=c: [e.dma_start(out=wb1[:, :, 0:128], in_=win[:, :, c * 128:(c + 1) * 128]),
                                                  e.dma_start(out=wb1[:, :, 128:256], in_=win[:, :, 768 + c * 128:768 + (c + 1) * 128])],
                 [], [self.w_b[w1]], dma=True, ndma=2)
            S.op("pool", lambda e, wb2=wb2, c=c: e.dma_start(out=wb2[:, :, 0:128], in_=win[:, :, 1536 + c * 128:1536 + (c + 1) * 128]),
                 [], [self.w_b[w2]], dma=True)
            for qi, (dst, dst_b) in enumerate(((qn, qn_b), (kn, kn_b))):
                for t in range(NT):
                    sl = slice(t * 512, (t + 1) * 512)
                    pb = self.bank()
                    for kc in range(KC):
                        self.PE(self.ps[:, pb, :], wb1[:, kc, qi * 128:(qi + 1) * 128], hT[:, kc, sl],
                                [self.w_b[w1], self.h_b[kc]], [self.ps_b[pb]], start=(kc == 0), stop=(kc == KC - 1))
                    self.ACT(sq, self.ps[:, pb, :], AF.Square, [self.ps_b[pb]], [sq_b])
                    self.CPY("dve", z16, self.ps[:, pb, :], [self.ps_b[pb]], [z16_b])
                    p2 = self.bank()
                    self.PE(self.ps[:, p2, :], BD_b, sq, CB + [sq_b], [self.ps_b[p2]])
                    if qi == 0:
                        self.ACT(rs, self.ps[:, p2, :], AF.Sqrt, [self.ps_b[p2]] + CF, [rs_b], scale=1.0, bias=cf[:, 899:900])
                    else:
                        self.ACT(rs, self.ps[:, p2, :], AF.Sqrt, [self.ps_b[p2]] + CF, [rs_b], scale=1.0 / 64.0, bias=cf[:, 896:897])
                    S.op("dve", lambda e: e.reciprocal(out=rs, in_=rs), [rs_b], [rs_b])
                    self.STT(dst[:, sl], z16, sm[:, 96 + qi:97 + qi], rs, ALU.mult, ALU.mult, [z16_b, sm_b, rs_b], [dst_b])
            blocks = [(r, nb) for r in range(d) for nb in range(nblk)]
            for bi, (r, nb) in enumerate(blocks):
                st = nb * 128 * d + r
                tsl = slice(st, st + 127 * d + 1, d)
                pb = self.bank()
                for kc in range(KC):
                    self.PE(self.ps[:, pb, 0:128], hT[:, kc, tsl], wb2[:, kc, 0:128], [self.h_b[kc], self.w_b[w2]], [self.ps_b[pb]],
                            start=(kc == 0), stop=(kc == KC - 1))
                self.CPY("act", vt4[:, bi, :, 0:64], self.ps[:, pb, 0:128].rearrange("p (h d) -> p h d", d=64), [self.ps_b[pb]], [vt_b])
            for hh2 in range(2):
                hh = (c % 2) * 2 + hh2
                ps_ = slice(hh2 * 64, hh2 * 64 + 64)
                for bi, (r, nb) in enumerate(blocks):
                    st = nb * 128 * d + r
                    qsl = slice(st, st + 127 * d + 1, d)
                    pS = self.bank()
                    has_prev = nb > 0
                    if has_prev:
                        sp_ = (nb - 1) * 128 * d + r
                        ksl = slice(sp_, sp_ + 127 * d + 1, d)
                        self.PE(self.ps[:, pS, 0:128], kn[ps_, ksl], qn[ps_, qsl], [kn_b, qn_b], [self.ps_b[pS]])
                    self.PE(self.ps[:, pS, 128:256], kn[ps_, qsl], qn[ps_, qsl], [kn_b, qn_b], [self.ps_b[pS]])
                    P, P_b = pt[pti % 2]
                    pti += 1
                    lo = 0 if has_prev else 128
                    self.ACT(P[:, lo:256], self.ps[:, pS, lo:256], AF.Exp, [self.ps_b[pS]], [P_b])
                    if has_prev:
                        self.TT(P[:, 0:128], P[:, 0:128], msk, ALU.mult, [P_b] + CB, [P_b])
                    self.TT(P[:, 128:256], P[:, 128:256], U_b, ALU.mult, [P_b] + CB, [P_b])
                    pO = self.bank()
                    if has_prev:
                        self.PE(self.ps[0:65, pO, 0:128], vt4[:, bi - 1, hh2, 0:65], P[:, 0:128], [vt_b, P_b], [self.ps_b[pO]], start=True, stop=False)
                    self.PE(self.ps[0:65, pO, 0:128], vt4[:, bi, hh2, 0:65], P[:, 128:256], [vt_b, P_b], [self.ps_b[pO]],
                            start=(not has_prev), stop=True)
                    if g == 0:
                        self.CPY("dve", acc3[0:65, hh2, qsl], self.ps[0:65, pO, 0:128], [self.ps_b[pO]], [acc_b])
                    else:
                        self.TT(acc3[0:65, hh2, qsl], self.ps[0:65, pO, 0:128], acc3[0:65, hh2, qsl], ALU.add, [self.ps_b[pO], acc_b], [acc_b])
            if g == 2:
                S.op("dve", lambda e: e.reciprocal(out=acc3[64:65, :, :], in_=acc3[64:65, :, :]), [acc_b], [acc_b])
                for hh2 in range(2):
                    hh = (c % 2) * 2 + hh2
                    for t in range(NT):
                        sl = slice(t * 512, (t + 1) * 512)
                        pb = self.bank()
                        self.PE(self.ps[0:64, pb, :], cf[64:65, 256:320], acc3[64:65, hh2, sl], CF + [acc_b], [self.ps_b[pb]])
                        self.TT(attn3[0:64, hh, sl], acc3[0:64, hh2, sl], self.ps[0:64, pb, :], ALU.mult, [acc_b, self.ps_b[pb]], [attn_b])

    def mix_final(self, l):
        S = self.S
        S.barrier(engines=("pe", "act", "dve", "sp", "pool"))
        hT, xT = self.hT, self.xT
        cb, cf = self.cb, self.cf
        attn3 = self.attn.rearrange("p (h t) -> p h t", t=T)
        mrg3 = self.mrg.rearrange("p (c t) -> p c t", t=T)
        mrg_b = self.mrg_b
        odn_all = self.tmps["attacc"][0].bitcast(BF16)
        odn_t = [odn_all[:, i * 4096:(i + 1) * 4096].rearrange("p (h t) -> p h t", t=512) for i in range(2)]
        odn_tb = S.bufs("odn_re", 2)
        odn_src = self.odn_d.rearrange("h p t -> p h t")
        oi = 0
        win = self.w_in[l].rearrange("(kc p) n -> p kc n", p=128)
        wpd = self.w_pd[l].rearrange("(kc p) n -> p kc n", p=128)
        wo = self.w_o[l].rearrange("(kc p) n -> p kc n", p=128)
        wpa_d = self.w_pa[l].rearrange("(hh p) n -> p hh n", p=64)
        self.aoff = self.AR
        self.tmps.pop("qn", None)
        wpa, wpa_b = self.tmp("wpa", 4 * D)
        wpa3 = wpa.rearrange("p (h n) -> p h n", n=D)
        S.op("pool", lambda e: e.dma_start(out=wpa3[0:64, :, :], in_=wpa_d), [], [wpa_b], dma=True)
        ge = [self.tmp("ge%d" % i, 512, F32) for i in range(2)]
        m1, m1_b = self.tmp("m1", 512, F32)
        mc = self.modc
        for dc in range(KC):
            w1 = self.next_w()
            w2 = self.next_w()
            wb1, wb2 = self.wbuf[w1], self.wbuf[w2]
            S.op("pool", lambda e, wb1=wb1, dc=dc: [e.dma_start(out=wb1[:, :, 0:128], in_=win[:, :, OFF_MERGE + dc * 128:OFF_MERGE + (dc + 1) * 128]),
                                                   e.dma_start(out=wb1[:, :, 128:256], in_=win[:, :, OFF_MERGE + D + dc * 128:OFF_MERGE + D + (dc + 1) * 128])],
                 [], [self.w_b[w1]], dma=True, ndma=2)
            S.op("pool", lambda e, wb2=wb2, dc=dc: e.dma_start(out=wb2[:, :, 0:128], in_=wpd[:, :, dc * 128:(dc + 1) * 128]),
                 [], [self.w_b[w2]], dma=True)
            for t in range(NT):
                sl = slice(t * 512, (t + 1) * 512)
                od3, odn_b = odn_t[oi % 2], odn_tb[oi % 2]
                oi += 1
                S.op("sp", lambda e, od3=od3, sl=sl: e.dma_start(out=od3, in_=odn_src[:, :, sl]), self.odn_b, [odn_b], dma=True)
                pga, pgd, pya, pyd = self.bank(), self.bank(), self.bank(), self.bank()
                for gi, pg in enumerate((pga, pgd)):
                    for kc in range(KC):
                        self.PE(self.ps[:, pg, :], wb1[:, kc, gi * 128:(gi + 1) * 128], hT[:, kc, sl], [self.w_b[w1], self.h_b[kc]],
                                [self.ps_b[pg]], start=(kc == 0), stop=(kc == KC - 1))
                for hh in range(4):
                    self.PE(self.ps[:, pya, :], wpa3[0:64, hh, dc * 128:(dc + 1) * 128], attn3[0:64, hh, sl], [wpa_b, self.attn_b],
                            [self.ps_b[pya]], start=(hh == 0), stop=(hh == 3))
                for h8 in range(8):
                    self.PE(self.ps[:, pyd, :], wb2[:, h8, 0:128], od3[:, h8, :], [self.w_b[w2], odn_b], [self.ps_b[pyd]],
                            start=(h8 == 0), stop=(h8 == 7))
                for gi, (pg, py) in enumerate(((pga, pya), (pgd, pyd))):
                    gt_, gt_b = ge[gi]
                    self.ACT(gt_, self.ps[:, pg, :], AF.Exp, [self.ps_b[pg]], [gt_b], scale=-1.0)
                    self.TS(gt_, gt_, 1.0, ALU.add, [gt_b], [gt_b])
                    S.op("dve", lambda e, gt_=gt_: e.reciprocal(out=gt_, in_=gt_), [gt_b], [gt_b])
                    self.TT(gt_, gt_, self.ps[:, py, :], ALU.mult, [gt_b, self.ps_b[py]], [gt_b])
                self.TT(mrg3[:, dc, sl], ge[0][0], ge[1][0], ALU.add, [ge[0][1], ge[1][1]], [mrg_b])
        for half in range(4):
            w1 = self.next_w()
            wb1 = self.wbuf[w1]
            S.op("pool", lambda e, wb1=wb1, half=half: e.dma_start(out=wb1[:, :, 0:256], in_=wo[:, :, half * 256:(half + 1) * 256]),
                 [], [self.w_b[w1]], dma=True)
            for j2 in range(2):
                dc = half * 2 + j2
                for t in range(NT):
                    sl = slice(t * 512, (t + 1) * 512)
                    pb = self.bank()
                    for kc in range(KC):
                        self.PE(self.ps[:, pb, :], wb1[:, kc, j2 * 128:(j2 + 1) * 128], mrg3[:, kc, sl], [self.w_b[w1], mrg_b],
                                [self.ps_b[pb]], start=(kc == 0), stop=(kc == KC - 1))
                    self.STT(xT[:, dc, sl], self.ps[:, pb, :], mc[:, 24 + 16 + dc:24 + 16 + dc + 1], xT[:, dc, sl], ALU.mult, ALU.add,
                             [self.ps_b[pb], self.modc_b, self.x_b[dc]], [self.x_b[dc]])

def make_consts():
    c = np.zeros((128, 1024), np.float32)
    c[:, 0:128] = np.eye(128, dtype=np.float32)
    jj = np.arange(128)[:, None]
    ii = np.arange(128)[None, :]
    c[:, 128:256] = (jj <= ii).astype(np.float32)
    c[:, 256:384] = 1.0
    c[:, 384:512] = (jj < ii).astype(np.float32)
    c[:, 512:640] = (jj > ii).astype(np.float32)
    c[:, 640:768] = (jj >= ii).astype(np.float32)
    c[:, 768:896] = ((jj < 64) == (ii < 64)).astype(np.float32)
    c[:, 896] = EPS
    c[:, 897] = 128.0 * EPS
    c[:, 898] = 1.0
    c[:, 899] = 64.0 * EPS
    return c


def make_small(inputs):
    L = DEPTH
    sm = np.zeros((L, 128, 512), np.float32)
    cw = np.asarray(inputs["conv_w"], np.float32)
    sm[:, :, 0:96] = cw.reshape(L, 4, 24, 128).transpose(0, 3, 2, 1).reshape(L, 128, 96)
    p = np.arange(128) % 64
    sm[:, :, 96] = np.asarray(inputs["q_norm"], np.float32)[:, p]
    sm[:, :, 97] = np.asarray(inputs["k_norm"], np.float32)[:, p]
    sm[:, :, 104:112] = np.asarray(inputs["a_log"], np.float32)[:, None, :]
    sm[:, :, 112:120] = np.asarray(inputs["dt_bias"], np.float32)[:, None, :]
    sm[:, :, 128:256] = np.asarray(inputs["dn_norm"], np.float32)[:, None, :]
    sm[:, :, 256:384] = np.tile(np.asarray(inputs["dt_bias"], np.float32), (1, 16))[:, None, :]
    sm[:, :, 384:512] = np.tile(np.asarray(inputs["a_log"], np.float32), (1, 16))[:, None, :]
    return sm


def prep_inputs(inputs, b):
    f = lambda a: np.ascontiguousarray(a, dtype=np.float32)
    m = {}
    m["x"] = f(inputs["x"][b])
    m["c"] = f(inputs["c"][b].reshape(KC, 128).T)
    m["ada_w"] = f(inputs["ada_w"])
    m["ada_b"] = f(inputs["ada_b"].reshape(DEPTH, 72, 128).transpose(0, 2, 1))
    nr = np.stack([inputs["norm_ff1"], inputs["norm_mix"], inputs["norm_ff2"]], axis=1)
    m["norms"] = f(nr.reshape(DEPTH, 3 * KC, 128).transpose(0, 2, 1))
    m["ffn1_w_up"] = f(inputs["ffn1_w_up"])
    m["ffn2_w_up"] = f(inputs["ffn2_w_up"])
    m["ffn1_w_down"] = f(inputs["ffn1_w_down"])
    m["ffn2_w_down"] = f(inputs["ffn2_w_down"])
    m["w_in"] = f(inputs["w_in"])
    m["w_proj_att"] = f(inputs["w_proj_att"])
    m["w_proj_dn"] = f(inputs["w_proj_dn"])
    m["w_out"] = f(inputs["w_out"])
    m["consts"] = make_consts()
    m["small"] = make_small(inputs)
    return m


_PROG = {}


def run(inputs, n_layers=DEPTH, stages=("ffn1", "mix", "ffn2"), cores=8, trace=False):
    key = (n_layers, tuple(stages))
    if key not in _PROG:
        _PROG[key] = Prog(n_layers, stages)
    p = _PROG[key]
    shared = prep_inputs(inputs, 0)
    in_maps = []
    for b in range(cores):
        m = dict(shared)
        m["x"] = np.ascontiguousarray(inputs["x"][b], dtype=np.float32)
        m["c"] = np.ascontiguousarray(inputs["c"][b].reshape(KC, 128).T, dtype=np.float32)
        in_maps.append(m)
    res = run_bass_kernel_spmd(p.nc, in_maps, core_ids=list(range(cores)), trace=trace)
    out = np.stack([r["out"] for r in res.results], axis=0)
    return out, res


def kernel(**inputs):
    out, _ = run(inputs)
    return out.astype(np.float32)
```

```python
import numpy as np
import concourse.bass as bass
import concourse.mybir as mybir
from concourse.bass_utils import run_bass_kernel_spmd

F32 = mybir.dt.float32
BF16 = mybir.dt.bfloat16
AF = mybir.ActivationFunctionType
ALU = mybir.AluOpType

D = 1024
T = 2048
DEPTH = 4
DFF = 2816
NFF = DFF // 128
KC = D // 128
NT = T // 512
NB = T // 128
N_IN = 8464
OFF_DN_QKV = 2304
OFF_DN_GATE = 5376
OFF_DN_A = 6400
OFF_DN_B = 6408
OFF_MERGE = 6416
EPS = 1e-6
SEM_LIMIT = 30000


class Buf:
    __slots__ = ("name", "w", "rs", "dsem", "excl")

    def __init__(self, name, excl=False):
        self.name = name
        self.w = None
        self.rs = []
        self.dsem = None
        self.excl = excl


class Op:
    __slots__ = ("eng", "fn", "deps", "dma", "ndma", "sem", "val", "signals", "idx")


class DmaSem:
    def __init__(self, sched):
        self.s = sched
        self.sem = sched.new_sem()
        self.count = 0

    def take(self, n):
        if self.count + 16 * n > SEM_LIMIT:
            self.sem = self.s.new_sem()
            self.count = 0
        self.count += 16 * n
        return self.sem, self.count


class Sched:
    ENGS = ("pe", "act", "dve", "pool", "sp")

    def __init__(self, nc):
        self.nc = nc
        self.q = {e: [] for e in self.ENGS}
        self.nsem = 0
        self.dma_ops = []
        self.all_bufs = []

    def new_sem(self):
        self.nsem += 1
        return self.nc.alloc_semaphore("s%d" % self.nsem)

    def buf(self, name):
        b = Buf(name)
        return b

    def bufs(self, name, n, excl=False):
        return [Buf("%s%d" % (name, i), excl) for i in range(n)]

    def op(self, eng, fn, reads=(), writes=(), dma=False, ndma=1):
        o = Op()
        o.eng = eng
        o.fn = fn
        o.dma = dma
        o.ndma = ndma
        o.signals = dma
        o.sem = None
        o.val = 0
        deps = []
        for b in reads:
            if b.w is not None:
                deps.append((b.w, True))
            if b.excl:
                for r in b.rs:
                    if r.eng != eng:
                        deps.append((r, False))
        for b in writes:
            if b.w is not None:
                deps.append((b.w, False))
            for r in b.rs:
                deps.append((r, False))
        fd = {}
        for d, raw in deps:
            if d is o:
                continue
            if not d.dma and not dma and d.eng == eng:
                if eng == "pe":
                    continue
            if d.dma and dma and not raw:
                pass
            fd[id(d)] = d
        o.deps = list(fd.values())
        for b in reads:
            b.rs.append(o)
        for b in writes:
            b.w = o
            b.rs = []
        if dma:
            wb = writes[0]
            if wb.dsem is None:
                wb.dsem = DmaSem(self)
            o.sem, o.val = wb.dsem.take(ndma)
            self.dma_ops.append(o)
        o.idx = len(self.q[eng])
        self.q[eng].append(o)
        return o

    def barrier(self, engines=("pe", "act", "dve", "sp")):
        lasts = []
        for e in ("pe", "act", "dve", "pool"):
            for o in reversed(self.q[e]):
                if not o.dma and o.fn is not None:
                    lasts.append(o)
                    break
        dm = list(self.dma_ops)
        self.dma_ops = []
        for e in engines:
            o = Op()
            o.eng = e
            o.fn = None
            o.dma = False
            o.ndma = 0
            o.signals = False
            o.sem = None
            o.val = 0
            o.deps = [d for d in lasts if d.eng != e or e != "pe"] + dm
            o.idx = len(self.q[e])
            self.q[e].append(o)

    def emit(self):
        nc = self.nc
        for e in self.ENGS:
            for o in self.q[e]:
                for d in o.deps:
                    if not d.dma:
                        d.signals = True
        for e in self.ENGS:
            sem = None
            cnt = 0
            for o in self.q[e]:
                if o.dma or not o.signals:
                    continue
                if sem is None or cnt >= SEM_LIMIT:
                    sem = self.new_sem()
                    cnt = 0
                cnt += 1
                o.sem = sem
                o.val = cnt
        sched = self

        def run(e, eng):
            waited = {}
            for o in sched.q[e]:
                for d in o.deps:
                    key = id(d.sem)
                    if waited.get(key, 0) >= d.val:
                        continue
                    waited[key] = d.val
                    eng.wait_ge(d.sem, d.val)
                if o.fn is None:
                    continue
                r = o.fn(eng)
                if o.dma:
                    if not isinstance(r, (list, tuple)):
                        r = [r]
                    assert len(r) == o.ndma, (len(r), o.ndma)
                    for ins in r:
                        ins.then_inc(o.sem, 16)
                elif o.signals:
                    r.then_inc(o.sem, 1)

        with nc.Block() as block:
            @block.tensor
            def _(eng):
                run("pe", eng)

            @block.scalar
            def _(eng):
                run("act", eng)

            @block.vector
            def _(eng):
                run("dve", eng)

            @block.gpsimd
            def _(eng):
                run("pool", eng)

            @block.sync
            def _(eng):
                run("sp", eng)


class Prog:
    def __init__(self, n_layers=DEPTH, stages=("ffn1", "mix", "ffn2"), dbg=None):
        self.n_layers = n_layers
        self.stages = stages
        self.dbg = dbg
        nc = bass.Bass("TRN2", target_bir_lowering=False)
        self.nc = nc
        self.S = Sched(nc)
        self.decl_io()
        self.alloc()
        self.build()
        self.S.emit()

    def decl_io(self):
        nc = self.nc
        L = DEPTH

        def inp(name, shape):
            return nc.dram_tensor(name, list(shape), F32, kind="ExternalInput").ap()

        self.x_d = inp("x", (T, D))
        self.c_d = inp("c", (128, KC))
        self.ada_w = inp("ada_w", (L, D, 9 * D))
        self.ada_b = inp("ada_b", (L, 128, 72))
        self.norms = inp("norms", (L, 128, 3 * KC))
        self.w_up = [inp("ffn1_w_up", (L, D, 2 * DFF)), inp("ffn2_w_up", (L, D, 2 * DFF))]
        self.w_dn = [inp("ffn1_w_down", (L, DFF, D)), inp("ffn2_w_down", (L, DFF, D))]
        self.w_in = inp("w_in", (L, D, N_IN))
        self.cst = inp("consts", (128, 1024))
        self.small_d = inp("small", (L, 128, 512))
        self.w_pa = inp("w_proj_att", (L, 256, D))
        self.w_pd = inp("w_proj_dn", (L, D, D))
        self.w_o = inp("w_out", (L, D, D))
        self.odn_d = nc.dram_tensor("odn_scr", [8, 128, T], BF16, kind="Internal").ap()
        self.odn_b = self.S.bufs("odn_d", 8)
        self.att_d = nc.dram_tensor("att_scr", [4, 64, T], BF16, kind="Internal").ap()
        self.att_b = self.S.bufs("att_d", 4)
        self.out_d = nc.dram_tensor("out", [T, D], F32, kind="ExternalOutput").ap()

    def alloc(self):
        nc = self.nc
        S = self.S
        self.xT = nc.alloc_sbuf_tensor("xT", [128, KC, T], F32)
        self.x_b = S.bufs("x", KC)
        self.hT = nc.alloc_sbuf_tensor("hT", [128, KC, T], BF16)
        self.h_b = S.bufs("h", KC)
        self.NW = 4
        self.wbuf = [nc.alloc_sbuf_tensor("wbuf%d" % i, [128, KC, 256], BF16) for i in range(self.NW)]
        self.w_b = S.bufs("w", self.NW)
        self.wi = 0
        self.wd = nc.alloc_sbuf_tensor("wd", [128, 11, D], BF16)
        self.wd_b = S.bufs("wd", 11)
        self.cf = nc.alloc_sbuf_tensor("cf", [128, 1024], F32)
        self.cf_b = S.buf("cf")
        self.cb = nc.alloc_sbuf_tensor("cb", [128, 1024], BF16)
        self.cb_b = S.buf("cb")
        self.modv = nc.alloc_sbuf_tensor("modv", [128, 72], F32)
        self.modv_b = S.buf("modv")
        self.adab = nc.alloc_sbuf_tensor("adab", [128, 72], F32)
        self.adab_b = S.buf("adab")
        self.nrm = nc.alloc_sbuf_tensor("nrm", [128, 3 * KC], F32)
        self.nrm_b = S.buf("nrm")
        self.modc = nc.alloc_sbuf_tensor("modc", [128, 9 * KC], F32)
        self.modc_b = S.buf("modc")
        self.cact = nc.alloc_sbuf_tensor("cact", [128, KC], BF16)
        self.cact_b = S.buf("cact")
        self.c32 = nc.alloc_sbuf_tensor("c32", [128, 2 * KC], F32)
        self.c32_b = S.buf("c32")
        self.AR = getattr(Prog, 'AR_OVERRIDE', 33 * 1024)
        self.ar = nc.alloc_sbuf_tensor("arena", [128, self.AR], BF16)
        self.ps = nc.alloc_psum_tensor("ps", [128, 8, 512], F32)
        self.ps_b = S.bufs("ps", 8, excl=True)
        self.pi = 0

    def bank(self):
        i = self.pi
        self.pi = (self.pi + 1) % 7
        return i

    def next_w(self):
        i = self.wi
        self.wi = (self.wi + 1) % self.NW
        return i

    def ar_bf(self, off, n):
        return self.ar[:, off:off + n]

    def ar_f32(self, off, n):
        return self.ar[:, off:off + 2 * n].bitcast(F32)

    def build(self):
        S = self.S
        self.load_consts()
        self.load_x()
        for l in range(self.n_layers):
            self.mod_layer(l)
            if "ffn1" in self.stages:
                self.norm_mod(l, 0)
                self.ffn(l, 0)
            if "mix" in self.stages:
                self.norm_mod(l, 1)
                self.mixer(l)
            if "ffn2" in self.stages:
                self.norm_mod(l, 2)
                self.ffn(l, 1)
        self.store_x()

    def load_consts(self):
        S = self.S
        cf, cb = self.cf, self.cb
        S.op("sp", lambda e: e.dma_start(out=cf[:], in_=self.cst[:, :]), [], [self.cf_b], dma=True)
        S.op("dve", lambda e: e.tensor_copy(out=cb[:], in_=cf[:]), [self.cf_b], [self.cb_b])
        c32 = self.c32
        S.op("sp", lambda e: e.dma_start(out=c32[:, 0:KC], in_=self.c_d[:, :]), [], [self.c32_b], dma=True)
        S.op("act", lambda e: e.activation(out=c32[:, KC:2 * KC], in_=c32[:, 0:KC], func=AF.Exp, scale=-1.0),
             [self.c32_b], [self.c32_b])
        S.op("dve", lambda e: e.tensor_scalar_add(out=c32[:, KC:2 * KC], in0=c32[:, KC:2 * KC], scalar1=1.0),
             [self.c32_b], [self.c32_b])
        S.op("dve", lambda e: e.reciprocal(out=c32[:, KC:2 * KC], in_=c32[:, KC:2 * KC]),
             [self.c32_b], [self.c32_b])
        S.op("dve", lambda e: e.tensor_tensor(out=self.cact[:], in0=c32[:, 0:KC], in1=c32[:, KC:2 * KC], op=ALU.mult),
             [self.c32_b], [self.cact_b])

    def ident_f(self):
        return self.cf[:, 0:128]

    def load_x(self):
        S = self.S
        stg = [self.ar_f32(0, D), self.ar_f32(2 * D, D)]
        stg_b = S.bufs("xstg", 2)
        xT = self.xT
        for b in range(NB):
            s = b % 2
            S.op("sp", lambda e, b=b, s=s: e.dma_start(out=stg[s], in_=self.x_d[b * 128:(b + 1) * 128, :]),
                 [], [stg_b[s]], dma=True)
            for half in range(2):
                pb = self.bank()
                for q in range(4):
                    c = half * 4 + q
                    S.op("pe", lambda e, c=c, s=s, pb=pb, q=q: e.transpose(
                        self.ps[:, pb, q * 128:(q + 1) * 128], stg[s][:, c * 128:(c + 1) * 128], self.ident_f()),
                        [stg_b[s], self.cf_b], [self.ps_b[pb]])
                psv = self.ps[:, pb, :].rearrange("p (q i) -> p q i", i=128)
                dst = xT[:, half * 4:(half + 1) * 4, b * 128:(b + 1) * 128]
                xb = self.x_b[half * 4:(half + 1) * 4]
                if half == 0:
                    S.op("dve", lambda e, psv=psv, dst=dst: e.tensor_copy(out=dst, in_=psv), [self.ps_b[pb]], xb)
                else:
                    S.op("act", lambda e, psv=psv, dst=dst: e.copy(out=dst, in_=psv), [self.ps_b[pb]], xb)
        S.barrier()

    def store_x(self):
        S = self.S
        S.barrier()
        stg = [self.ar_f32(0, D), self.ar_f32(2 * D, D)]
        stg_b = [S.bufs("ostg%d_" % i, 2) for i in range(2)]
        xT = self.xT
        outs = []
        for b in range(NB):
            s = b % 2
            for half in range(2):
                pb = self.bank()
                for q in range(4):
                    c = half * 4 + q
                    S.op("pe", lambda e, c=c, b=b, pb=pb, q=q: e.transpose(
                        self.ps[:, pb, q * 128:(q + 1) * 128], xT[:, c, b * 128:(b + 1) * 128], self.ident_f()),
                        [self.x_b[c], self.cf_b], [self.ps_b[pb]])
                if half == 0:
                    S.op("dve", lambda e, s=s, pb=pb: e.tensor_copy(
                        out=stg[s][:, 0:512], in_=self.ps[:, pb, :]), [self.ps_b[pb]], [stg_b[s][0]])
                else:
                    S.op("act", lambda e, s=s, pb=pb: e.copy(
                        out=stg[s][:, 512:1024], in_=self.ps[:, pb, :]), [self.ps_b[pb]], [stg_b[s][1]])
            ob = S.buf("outd%d" % b)
            o = S.op("sp", lambda e, b=b, s=s: e.dma_start(out=self.out_d[b * 128:(b + 1) * 128, :], in_=stg[s]),
                     stg_b[s], [ob], dma=True)
            outs.append(ob)
        S.op("sp", None, outs, [])

    def mod_layer(self, l):
        S = self.S
        S.op("sp", lambda e: e.dma_start(out=self.adab[:], in_=self.ada_b[l]), [], [self.adab_b], dma=True)
        S.op("sp", lambda e: e.dma_start(out=self.nrm[:], in_=self.norms[l]), [], [self.nrm_b], dma=True)
        aw = self.ada_w[l].rearrange("(kc p) n -> p kc n", p=128)
        pb = self.bank()
        for blk in range(9 * D // 256):
            wi = self.next_w()
            wb = self.wbuf[wi]
            S.op("pool", lambda e, blk=blk, wb=wb: e.dma_start(out=wb[:], in_=aw[:, :, blk * 256:(blk + 1) * 256]),
                 [], [self.w_b[wi]], dma=True)
            for j in range(2):
                col = blk * 2 + j
                for kc in range(KC):
                    S.op("pe", lambda e, wb=wb, j=j, kc=kc, col=col: e.matmul(
                        self.ps[:, pb, col:col + 1], wb[:, kc, j * 128:(j + 1) * 128], self.cact[:, kc:kc + 1],
                        start=(kc == 0), stop=(kc == KC - 1)),
                        [self.w_b[wi], self.cact_b], [self.ps_b[pb]])
        S.op("dve", lambda e: e.tensor_tensor(out=self.modv[:], in0=self.ps[:, pb, 0:72], in1=self.adab[:], op=ALU.add),
             [self.ps_b[pb], self.adab_b], [self.modv_b])
        mv, mc, nr = self.modv, self.modc, self.nrm
        for j in range(3):
            S.op("dve", lambda e, j=j: e.scalar_tensor_tensor(
                out=mc[:, j * 24:j * 24 + 8], in0=mv[:, (3 * j + 1) * 8:(3 * j + 2) * 8], scalar=1.0,
                in1=nr[:, j * 8:(j + 1) * 8], op0=ALU.add, op1=ALU.mult),
                [self.modv_b, self.nrm_b], [self.modc_b])
            S.op("dve", lambda e, j=j: e.tensor_copy(out=mc[:, j * 24 + 8:j * 24 + 16], in_=mv[:, (3 * j) * 8:(3 * j + 1) * 8]),
                 [self.modv_b], [self.modc_b])
            gsc = 1.0 if j == 1 else 0.5
            S.op("dve", lambda e, j=j, gsc=gsc: e.tensor_scalar(
                out=mc[:, j * 24 + 16:j * 24 + 24], in0=mv[:, (3 * j + 2) * 8:(3 * j + 3) * 8], scalar1=gsc, scalar2=None,
                op0=ALU.mult),
                [self.modv_b], [self.modc_b])

    def norm_mod(self, l, j):
        S = self.S
        S.barrier()
        xT, hT = self.xT, self.hT
        sq = [self.ar_bf(0, T), self.ar_bf(T, T)]
        sq_b = S.bufs("sq", 2)
        rstd = self.ar_f32(2 * T, T)
        rstd_b = S.buf("rstd")
        tmp = [self.ar_bf(6 * T, T), self.ar_bf(7 * T, T)]
        tmp_b = S.bufs("ntmp", 2)
        ones = self.cb[:, 256:384]
        pbs = [self.bank() for _ in range(4)]
        for kc in range(KC):
            s = kc % 2
            S.op("act", lambda e, kc=kc, s=s: e.activation(out=sq[s], in_=xT[:, kc, :], func=AF.Square),
                 [self.x_b[kc]], [sq_b[s]])
            for t in range(NT):
                S.op("pe", lambda e, kc=kc, s=s, t=t: e.matmul(
                    self.ps[:, pbs[t], :], ones, sq[s][:, t * 512:(t + 1) * 512], start=(kc == 0), stop=(kc == KC - 1)),
                    [sq_b[s], self.cb_b], [self.ps_b[pbs[t]]])
        for t in range(NT):
            S.op("act", lambda e, t=t: e.activation(out=rstd[:, t * 512:(t + 1) * 512], in_=self.ps[:, pbs[t], :],
                                                    func=AF.Sqrt, scale=1.0 / D, bias=self.cf[:, 896:897]),
                 [self.ps_b[pbs[t]], self.cf_b], [rstd_b])
        S.op("dve", lambda e: e.reciprocal(out=rstd, in_=rstd), [rstd_b], [rstd_b])
        mc = self.modc
        for kc in range(KC):
            s = kc % 2
            S.op("dve", lambda e, kc=kc, s=s: e.tensor_tensor(out=tmp[s], in0=xT[:, kc, :], in1=rstd, op=ALU.mult),
                 [self.x_b[kc], rstd_b], [tmp_b[s]])
            S.op("act", lambda e, kc=kc, s=s: e.activation(
                out=hT[:, kc, :], in_=tmp[s], func=AF.Identity,
                scale=mc[:, j * 24 + kc:j * 24 + kc + 1], bias=mc[:, j * 24 + 8 + kc:j * 24 + 8 + kc + 1]),
                [tmp_b[s], self.modc_b], [self.h_b[kc]])

    def ffn(self, l, which):
        S = self.S
        S.barrier()
        j = 0 if which == 0 else 2
        xT, hT = self.xT, self.hT
        wu = self.w_up[which][l].rearrange("(kc p) n -> p kc n", p=128)
        wdn = self.w_dn[which][l]
        act = self.ar[:, 0:11 * T].rearrange("p (j t) -> p j t", t=T)
        act_b = S.bufs("act", 11)
        sg = [self.ar_f32(11 * T, 512), self.ar_f32(11 * T + 1024, 512)]
        sg_b = S.bufs("sg", 2)
        mc = self.modc
        sgi = 0
        for half in range(2):
            def load_wd():
                for jj in range(11):
                    r0 = (half * 11 + jj) * 128
                    S.op("pool", lambda e, jj=jj, r0=r0: e.dma_start(out=self.wd[:, jj, :], in_=wdn[r0:r0 + 128, :]),
                         [], [self.wd_b[jj]], dma=True)
            for blk in range(6):
                nj = 2 if blk < 5 else 1
                j0 = half * 11 + blk * 2
                wg_i = self.next_w()
                wu_i = self.next_w()
                wgb, wub = self.wbuf[wg_i], self.wbuf[wu_i]
                ncol = nj * 128
                S.op("pool", lambda e, wgb=wgb, j0=j0, ncol=ncol: e.dma_start(
                    out=wgb[:, :, 0:ncol], in_=wu[:, :, j0 * 128:j0 * 128 + ncol]), [], [self.w_b[wg_i]], dma=True)
                S.op("pool", lambda e, wub=wub, j0=j0, ncol=ncol: e.dma_start(
                    out=wub[:, :, 0:ncol], in_=wu[:, :, DFF + j0 * 128:DFF + j0 * 128 + ncol]), [], [self.w_b[wu_i]], dma=True)
                if blk == 2:
                    load_wd()
                for jj in range(nj):
                    ja = blk * 2 + jj
                    for t in range(NT):
                        pg, pu = self.bank(), self.bank()
                        for kc in range(KC):
                            S.op("pe", lambda e, kc=kc, jj=jj, t=t, pg=pg, wgb=wgb: e.matmul(
                                self.ps[:, pg, :], wgb[:, kc, jj * 128:(jj + 1) * 128], hT[:, kc, t * 512:(t + 1) * 512],
                                start=(kc == 0), stop=(kc == KC - 1)),
                                [self.w_b[wg_i], self.h_b[kc]], [self.ps_b[pg]])
                        for kc in range(KC):
                            S.op("pe", lambda e, kc=kc, jj=jj, t=t, pu=pu, wub=wub: e.matmul(
                                self.ps[:, pu, :], wub[:, kc, jj * 128:(jj + 1) * 128], hT[:, kc, t * 512:(t + 1) * 512],
                                start=(kc == 0), stop=(kc == KC - 1)),
                                [self.w_b[wu_i], self.h_b[kc]], [self.ps_b[pu]])
                        s = sgi % 2
                        sgi += 1
                        S.op("act", lambda e, s=s, pg=pg: e.activation(out=sg[s], in_=self.ps[:, pg, :], func=AF.Silu),
                             [self.ps_b[pg]], [sg_b[s]])
                        S.op("dve", lambda e, s=s, pu=pu, ja=ja, t=t: e.tensor_tensor(
                            out=act[:, ja, t * 512:(t + 1) * 512], in0=sg[s], in1=self.ps[:, pu, :], op=ALU.mult),
                            [sg_b[s], self.ps_b[pu]], [act_b[ja]])
            for t in range(NT):
                for dc in range(KC):
                    pb = self.bank()
                    for jj in range(11):
                        S.op("pe", lambda e, jj=jj, dc=dc, t=t, pb=pb: e.matmul(
                            self.ps[:, pb, :], self.wd[:, jj, dc * 128:(dc + 1) * 128], act[:, jj, t * 512:(t + 1) * 512],
                            start=(jj == 0), stop=(jj == 10)),
                            [self.wd_b[jj], act_b[jj]], [self.ps_b[pb]])
                    S.op("dve", lambda e, dc=dc, t=t, pb=pb: e.scalar_tensor_tensor(
                        out=xT[:, dc, t * 512:(t + 1) * 512], in0=self.ps[:, pb, :],
                        scalar=mc[:, j * 24 + 16 + dc:j * 24 + 16 + dc + 1],
                        in1=xT[:, dc, t * 512:(t + 1) * 512], op0=ALU.mult, op1=ALU.add),
                        [self.ps_b[pb], self.modc_b, self.x_b[dc]], [self.x_b[dc]])

    def PE(self, out, lhsT, rhs, r, w, start=True, stop=True):
        return self.S.op("pe", lambda e: e.matmul(out, lhsT, rhs, start=start, stop=stop), r, w)

    def TR(self, out, in_, ident, r, w):
        return self.S.op("pe", lambda e: e.transpose(out, in_, ident), r, w)

    def ACT(self, out, in_, func, r, w, scale=None, bias=None, accum=None):
        kw = {}
        if scale is not None:
            kw["scale"] = scale
        if bias is not None:
            kw["bias"] = bias
        if accum is not None:
            kw["accum_out"] = accum
        return self.S.op("act", lambda e: e.activation(out=out, in_=in_, func=func, **kw), r, w)

    def TT(self, out, in0, in1, op, r, w):
        return self.S.op("dve", lambda e: e.tensor_tensor(out=out, in0=in0, in1=in1, op=op), r, w)

    def TS(self, out, in0, s1, op0, r, w, s2=None, op1=None):
        if op1 is None:
            return self.S.op("dve", lambda e: e.tensor_scalar(out=out, in0=in0, scalar1=s1, scalar2=None, op0=op0), r, w)
        return self.S.op("dve", lambda e: e.tensor_scalar(out=out, in0=in0, scalar1=s1, scalar2=s2, op0=op0, op1=op1), r, w)

    def STT(self, out, in0, sc, in1, op0, op1, r, w):
        return self.S.op("dve", lambda e: e.scalar_tensor_tensor(out=out, in0=in0, scalar=sc, in1=in1, op0=op0, op1=op1), r, w)

    def CPY(self, eng, out, in_, r, w):
        if eng == "dve":
            return self.S.op("dve", lambda e: e.tensor_copy(out=out, in_=in_), r, w)
        return self.S.op("act", lambda e: e.copy(out=out, in_=in_), r, w)

    def tmp(self, name, n, dt=BF16):
        if name in self.tmps:
            return self.tmps[name]
        ne = n if dt == BF16 else 2 * n
        if self.aoff % 2:
            self.aoff += 1
        off = self.aoff
        self.aoff += ne
        assert self.aoff <= self.AR + 11 * D, ("arena overflow", name, self.aoff)
        if off + ne <= self.AR:
            base = self.ar[:, off:off + ne]
        else:
            if off < self.AR:
                off = self.AR
                self.aoff = off + ne
            o2 = off - self.AR
            base = self.wd[:].rearrange("p a b -> p (a b)")[:, o2:o2 + ne]
        ap = base if dt == BF16 else base.bitcast(F32)
        if not hasattr(self, "toff"):
            self.toff = {}
        self.toff[name] = (off, ne, dt == F32)
        b = self.S.buf(name)
        self.tmps[name] = (ap, b)
        return ap, b

    def mixer(self, l):
        S = self.S
        S.barrier(engines=("pe", "act", "dve", "sp", "pool"))
        self.tmps = {}
        self.aoff = 0
        self.mix_setup(l)
        if "nodn" not in self.stages:
            self.dn_branch(l)
        if "noatt" not in self.stages:
            self.att_branch(l)
        if "nofinal" not in self.stages:
            self.mix_final(l)
        S.barrier(engines=("pe", "act", "dve", "sp", "pool"))

    def mix_setup(self, l):
        S = self.S
        sm, sm_b = self.tmp("small", 512, F32)
        self.sm, self.sm_b = sm, sm_b
        S.op("sp", lambda e: e.dma_start(out=sm, in_=self.small_d[l]), [], [sm_b], dma=True)

    def dn_branch(self, l):
        S = self.S
        hT = self.hT
        sm, sm_b = self.sm, self.sm_b
        cb, cf = self.cb, self.cf
        ident_b = cb[:, 0:128]
        U_b = cb[:, 128:256]
        ones_b = cb[:, 256:384]
        SU_f = cf[:, 384:512]
        SL_b = cb[:, 512:640]
        ones_f = cf[:, 256:384]
        ident_f = cf[:, 0:128]
        tri_f = cf[:, 128:256]
        win = self.w_in[l].rearrange("(kc p) n -> p kc n", p=128)
        CB = [self.cb_b]
        CF = [self.cf_b]
        wi = self.next_w()
        wab = self.wbuf[wi]
        S.op("pool", lambda e: e.dma_start(out=wab[:, :, 0:16], in_=win[:, :, OFF_DN_A:OFF_DN_A + 16]), [], [self.w_b[wi]], dma=True)
        pab = self.bank()
        for blk in range(NB):
            for kc in range(KC):
                self.PE(self.ps[:, pab, blk * 16:(blk + 1) * 16], hT[:, kc, blk * 128:(blk + 1) * 128], wab[:, kc, 0:16],
                        [self.h_b[kc], self.w_b[wi]], [self.ps_b[pab]], start=(kc == 0), stop=(kc == KC - 1))
        abv = self.ps[:, pab, 0:256].rearrange("p (b c) -> p b c", c=16)
        g_t, g_b = self.tmp("g_tok", 128, F32)
        be_t, be_b = self.tmp("be_tok", 128, F32)
        gc_t, gc_b = self.tmp("gc_tok", 128, F32)
        t1, t1_b = self.tmp("ab_t1", 128, F32)
        g3 = g_t.rearrange("p (b c) -> p b c", c=8)
        be3 = be_t.rearrange("p (b c) -> p b c", c=8)
        t13 = t1.rearrange("p (b c) -> p b c", c=8)
        dtb = sm[:, 256:384].rearrange("p (b c) -> p b c", c=8)
        self.TT(t13, abv[:, :, 0:8], dtb, ALU.add, [self.ps_b[pab], sm_b], [t1_b])
        self.ACT(t1, t1, AF.Exp, [t1_b], [t1_b])
        self.ACT(t1, t1, AF.Ln, [t1_b], [t1_b], bias=cf[:, 898:899])
        ea, ea_b = self.tmp("expalog", 128, F32)
        self.ACT(ea, sm[:, 384:512], AF.Exp, [sm_b], [ea_b])
        self.STT(g_t, t1, -1.0, ea, ALU.mult, ALU.mult, [t1_b, ea_b], [g_b])
        self.ACT(be3, abv[:, :, 8:16], AF.Exp, [self.ps_b[pab]], [be_b], scale=-1.0)
        self.TS(be_t, be_t, 1.0, ALU.add, [be_b], [be_b])
        S.op("dve", lambda e: e.reciprocal(out=be_t, in_=be_t), [be_b], [be_b])
        pgc = self.bank()
        self.PE(self.ps[:, pgc, 0:128], tri_f, g_t, CF + [g_b], [self.ps_b[pgc]])
        self.CPY("dve", gc_t, self.ps[:, pgc, 0:128], [self.ps_b[pgc]], [gc_b])
        nbe_t, nbe_b = self.tmp("nbe_tok", 128, F32)
        begc_t, begc_b = self.tmp("begc_tok", 128, F32)
        self.TS(nbe_t, be_t, -1.0, ALU.mult, [be_b], [nbe_b])
        self.ACT(begc_t, gc_t, AF.Exp, [gc_b], [begc_b])
        self.TT(begc_t, begc_t, be_t, ALU.mult, [begc_b, be_b], [begc_b])

        zc = [self.tmp("zc%d" % f, 2052) for f in range(3)]
        sgate, sgate_b = self.tmp("sgate", T)
        fT = [self.tmp("fT%d" % f, T) for f in range(3)]
        ktok, ktok_b = self.tmp("ktok", T)
        vtok, vtok_b = self.tmp("vtok", T)
        ktok3 = ktok.rearrange("p (b d) -> p b d", d=128)
        vtok3 = vtok.rearrange("p (b d) -> p b d", d=128)
        sq, sq_b = self.tmp("dsq", 512)
        rs, rs_b = self.tmp("drs", 512, F32)
        dg = [self.tmp("diag%d" % i, 128) for i in range(12)]
        S32, S32_b = self.tmp("S32", 128, F32)
        S16, S16_b = self.tmp("S16", 128)
        ost = [self.tmp("ost0", T)] * 2
        for f in range(3):
            S.op("dve", lambda e, f=f: e.memset(zc[f][0][:, 0:4], 0.0), [], [zc[f][1]])

        for h in range(8):
            cols = [OFF_DN_QKV + h * 128, OFF_DN_QKV + 1024 + h * 128, OFF_DN_QKV + 2048 + h * 128, OFF_DN_GATE + h * 128]
            w1 = self.next_w()
            w2 = self.next_w()
            wb1, wb2 = self.wbuf[w1], self.wbuf[w2]
            S.op("pool", lambda e, wb1=wb1, cols=cols: [e.dma_start(out=wb1[:, :, 0:128], in_=win[:, :, cols[0]:cols[0] + 128]),
                                                          e.dma_start(out=wb1[:, :, 128:256], in_=win[:, :, cols[1]:cols[1] + 128])],
                 [], [self.w_b[w1]], dma=True, ndma=2)
            S.op("pool", lambda e, wb2=wb2, cols=cols: [e.dma_start(out=wb2[:, :, 0:128], in_=win[:, :, cols[2]:cols[2] + 128]),
                                                          e.dma_start(out=wb2[:, :, 128:256], in_=win[:, :, cols[3]:cols[3] + 128])],
                 [], [self.w_b[w2]], dma=True, ndma=2)
            for f in range(3):
                for i in range(4):
                    c = f * 8 + h
                    self.TS(dg[f * 4 + i][0], ident_b, sm[:, c * 4 + i:c * 4 + i + 1], ALU.mult, CB + [sm_b], [dg[f * 4 + i][1]])
            for f in range(4):
                wb, wbb = (wb1, self.w_b[w1]) if f < 2 else (wb2, self.w_b[w2])
                co = (f % 2) * 128
                for t in range(NT):
                    pb = self.bank()
                    for kc in range(KC):
                        self.PE(self.ps[:, pb, :], wb[:, kc, co:co + 128], hT[:, kc, t * 512:(t + 1) * 512],
                                [wbb, self.h_b[kc]], [self.ps_b[pb]], start=(kc == 0), stop=(kc == KC - 1))
                    if f < 3:
                        self.CPY("dve" if t % 2 == 0 else "act", zc[f][0][:, 4 + t * 512:4 + (t + 1) * 512], self.ps[:, pb, :],
                                 [self.ps_b[pb]], [zc[f][1]])
                    else:
                        self.ACT(sgate[:, t * 512:(t + 1) * 512], self.ps[:, pb, :], AF.Silu, [self.ps_b[pb]], [sgate_b])
            for f in range(3):
                for t in range(NT):
                    pb = self.bank()
                    for i in range(4):
                        self.PE(self.ps[:, pb, :], dg[f * 4 + i][0], zc[f][0][:, 1 + t * 512 + i:1 + t * 512 + i + 512],
                                [dg[f * 4 + i][1], zc[f][1]], [self.ps_b[pb]], start=(i == 0), stop=(i == 3))
                    self.ACT(fT[f][0][:, t * 512:(t + 1) * 512], self.ps[:, pb, :], AF.Silu, [self.ps_b[pb]], [fT[f][1]])
            for f in range(2):
                for t in range(NT):
                    sl = slice(t * 512, (t + 1) * 512)
                    self.TT(sq, fT[f][0][:, sl], fT[f][0][:, sl], ALU.mult, [fT[f][1]], [sq_b])
                    pb = self.bank()
                    self.PE(self.ps[:, pb, :], ones_b, sq, CB + [sq_b], [self.ps_b[pb]])
                    if f == 0:
                        self.ACT(rs, self.ps[:, pb, :], AF.Sqrt, [self.ps_b[pb]] + CF, [rs_b], scale=128.0, bias=cf[:, 897:898])
                    else:
                        self.ACT(rs, self.ps[:, pb, :], AF.Sqrt, [self.ps_b[pb]] + CF, [rs_b], scale=1.0, bias=cf[:, 896:897])
                    S.op("dve", lambda e: e.reciprocal(out=rs, in_=rs), [rs_b], [rs_b])
                    self.TT(fT[f][0][:, sl], fT[f][0][:, sl], rs, ALU.mult, [fT[f][1], rs_b], [fT[f][1]])
            for (src, dst3, dst_b) in ((fT[1], ktok3, ktok_b), (fT[2], vtok3, vtok_b)):
                for g4 in range(4):
                    pb = self.bank()
                    pbv = self.ps[:, pb, :].bitcast(BF16)
                    for q in range(4):
                        blk = g4 * 4 + q
                        self.TR(pbv[:, q * 128:(q + 1) * 128], src[0][:, blk * 128:(blk + 1) * 128], ident_b,
                                [src[1]] + CB, [self.ps_b[pb]])
                    self.CPY("dve" if g4 % 2 == 0 else "act", dst3[:, g4 * 4:(g4 + 1) * 4, :],
                             pbv[:, 0:512].rearrange("p (q d) -> p q d", d=128), [self.ps_b[pb]], [dst_b])
            S.op("dve", lambda e: e.memset(S32, 0.0), [], [S32_b])
            S.op("dve", lambda e: e.memset(S16, 0.0), [], [S16_b])
            oT, oT_b = ost[h % 2]
            qT, kT = fT[0], fT[1]
            def alias(name, base, off, n, dt=F32):
                bap, bb = base
                ne = n if dt == BF16 else 2 * n
                ap = bap[:, off:off + ne]
                return (ap if dt == BF16 else ap.bitcast(F32)), bb

            def v4(ap):
                return ap.rearrange("p (c i) -> p c i", i=128)

            def bm(ap):
                return ap.unsqueeze(1).to_broadcast([128, 4, 128])

            def bc(ap4):
                return ap4.unsqueeze(2).to_broadcast([128, 4, 128])

            Xs = [alias("Xa", zc[0], 4, 512), alias("Xb", zc[0], 1028, 512)]
            Xts = [alias("Xta", zc[1], 4, 512), alias("Xtb", zc[1], 1028, 512)]
            Qs = [alias("Qa", zc[2], 4, 512), alias("Qb", zc[2], 1028, 512)]
            rr, rr_b = self.tmp("rr", 1024, F32)
            df, df_b = self.tmp("df", 512, F32)
            LtM, LtM_b = self.tmp("LtM", 512, F32)
            LM, LM_b = self.tmp("LM", 512, F32)
            nRb, nRb_b = self.tmp("nRb", 512, F32)
            Rex, Rex_b = self.tmp("Rex", 512, F32)
            Bf, Bf_b = self.tmp("Bf", 512, F32)
            Bo, Bo_b = self.tmp("Bo", 512, F32)
            R, R_b = self.tmp("Rr", 1024, F32)
            cp_, cp_b = self.tmp("cpr", 1024, F32)
            y16, y16_b = self.tmp("y16", 1024)
            Aq, Aq_b = self.tmp("Aq", 512)
            qd, qd_b = self.tmp("qd", 512)
            kd, kd_b = self.tmp("kd", 512)
            wT4, wT4_b = self.tmp("wT4", 512)
            on16, on16_b = self.tmp("on16", 512)
            vns = [self.tmp("vn%d" % i, 128) for i in range(2)]
            sc4, sc4_b = self.tmp("sc4", 16, F32)
            z_, z_b = rr, rr_b
            R3 = R.rearrange("p (c i) -> p c i", i=256)
            z3 = z_.rearrange("p (c i) -> p c i", i=256)
            c3 = cp_.rearrange("p (c i) -> p c i", i=256)
            y3 = y16.rearrange("p (c i) -> p c i", i=256)
            POSL = cf[:, 512:640]
            BD = cf[:, 768:896]
            PO = 7
            for qi in range(4):
                n0 = qi * 4
                ts = slice(n0 * 128, n0 * 128 + 512)

                def cols(t_):
                    return t_[:, n0 * 8 + h:n0 * 8 + h + 25:8]
                self.TT(v4(rr[:, 0:512]), bm(tri_f), bc(cols(g_t)), ALU.mult, CF + [g_b], [rr_b])
                self.TT(v4(rr[:, 512:1024]), bm(ident_f), bc(cols(be_t)), ALU.mult, CF + [be_b], [rr_b])
                pRg, pRb = self.bank(), self.bank()
                self.PE(self.ps[:, pRg, :], ones_f, rr[:, 0:512], CF + [rr_b], [self.ps_b[pRg]])
                self.PE(self.ps[:, pRb, :], ones_f, rr[:, 512:1024], CF + [rr_b], [self.ps_b[pRb]])
                self.TT(v4(df), v4(self.ps[:, pRg, :]), bc(cols(gc_t)), ALU.subtract, [self.ps_b[pRg], gc_b], [df_b])
                self.TS(LtM, df, 0.0, ALU.min, [df_b], [LtM_b])
                self.STT(v4(LM), v4(df), 0.0, bm(POSL), ALU.max, ALU.add, [df_b] + CF, [LM_b])
                self.STT(v4(nRb), v4(self.ps[:, pRb, :]), -1.0, bm(SU_f), ALU.mult, ALU.mult, [self.ps_b[pRb]] + CF, [nRb_b])
                self.TT(sc4[:, 0:4], self.ps[:, pRg, 127:512:128], cols(gc_t), ALU.subtract, [self.ps_b[pRg], gc_b], [sc4_b])
                self.ACT(Rex, self.ps[:, pRg, :], AF.Exp, [self.ps_b[pRg]], [Rex_b])
                self.ACT(LtM, LtM, AF.Exp, [LtM_b], [LtM_b])
                self.ACT(LM, LM, AF.Exp, [LM_b], [LM_b], scale=-1.0)
                self.ACT(sc4[:, 0:4], sc4[:, 0:4], AF.Exp, [sc4_b], [sc4_b])
                pK, pQ = self.bank(), self.bank()
                for c in range(4):
                    bs = slice((n0 + c) * 128, (n0 + c + 1) * 128)
                    cs = slice(c * 128, (c + 1) * 128)
                    self.PE(self.ps[:, pK, cs], kT[0][:, bs], kT[0][:, bs], [kT[1]], [self.ps_b[pK]])
                    self.PE(self.ps[:, pQ, cs], kT[0][:, bs], qT[0][:, bs], [kT[1], qT[1]], [self.ps_b[pQ]])
                X0, X0_b = Xs[0]
                Xt0, Xt0_b = Xts[0]
                self.TT(Bf, self.ps[:, pK, :], LtM, ALU.mult, [self.ps_b[pK], LtM_b], [Bf_b])
                self.TT(Bf, Bf, nRb, ALU.mult, [Bf_b, nRb_b], [Bf_b])
                self.TT(v4(X0), v4(Bf), bm(BD), ALU.mult, [Bf_b] + CF, [X0_b])
                self.TT(Bo, Bf, X0, ALU.subtract, [Bf_b, X0_b], [Bo_b])
                self.TT(v4(df), v4(self.ps[:, pK, :]), bc(cols(nbe_t)), ALU.mult, [self.ps_b[pK], nbe_b], [df_b])
                self.TT(Xt0, df, LM, ALU.mult, [df_b, LM_b], [Xt0_b])
                self.TT(nRb, self.ps[:, pQ, :], LtM, ALU.mult, [self.ps_b[pQ], LtM_b, Bf_b], [nRb_b])
                self.TT(v4(Aq), v4(nRb), bm(tri_f), ALU.mult, [nRb_b] + CF, [Aq_b])
                self.TT(qd, qT[0][:, ts], Rex, ALU.mult, [qT[1], Rex_b], [qd_b])
                self.TT(v4(kd), ktok3[:, n0:n0 + 4, :], bc(sc4[:, 0:4]), ALU.mult, [ktok_b, sc4_b], [kd_b])
                self.TT(R3[:, :, 0:128], vtok3[:, n0:n0 + 4, :], bc(cols(be_t)), ALU.mult, [vtok_b, be_b], [R_b])
                self.TT(R3[:, :, 128:256], ktok3[:, n0:n0 + 4, :], bc(cols(begc_t)), ALU.mult, [ktok_b, begc_b], [R_b])
                qc = 0
                self.TT(v4(Qs[0][0]), v4(X0), bm(ident_f), ALU.add, [X0_b] + CF, [Qs[0][1]])
                xc = 0
                for m in range(5):
                    Xc, Xc_b = Xs[xc]
                    Xn, Xn_b = Xs[1 - xc]
                    Xtc, Xtc_b = Xts[xc]
                    Xtn, Xtn_b = Xts[1 - xc]
                    if m < 4:
                        pX = self.bank()
                        for c in range(4):
                            cs = slice(c * 128, (c + 1) * 128)
                            self.PE(self.ps[:, pX, cs], Xtc[:, cs], Xc[:, cs], [Xtc_b, Xc_b], [self.ps_b[pX]])
                    pXt = self.bank()
                    for c in range(4):
                        cs = slice(c * 128, (c + 1) * 128)
                        self.PE(self.ps[:, pXt, cs], Xc[:, cs], Xtc[:, cs], [Xtc_b, Xc_b], [self.ps_b[pXt]])
                    if m < 4:
                        self.CPY("act", Xn, self.ps[:, pX, :], [self.ps_b[pX]], [Xn_b])
                    self.CPY("act", Xtn, self.ps[:, pXt, :], [self.ps_b[pXt]], [Xtn_b])
                    pQq = self.bank()
                    Qc, Qc_b = Qs[qc]
                    Qn, Qn_b = Qs[1 - qc]
                    for c in range(4):
                        cs = slice(c * 128, (c + 1) * 128)
                        self.PE(self.ps[:, pQq, cs], Xtn[:, cs], Qc[:, cs], [Xtn_b, Qc_b], [self.ps_b[pQq]])
                    self.TT(Qn, self.ps[:, pQq, :], Qc, ALU.add, [self.ps_b[pQq], Qc_b], [Qn_b])
                    qc = 1 - qc
                    xc = 1 - xc
                Qf, Qf_b = Qs[qc]
                pz = [self.bank(), self.bank()]
                for c in range(4):
                    self.PE(self.ps[:, pz[c // 2], (c % 2) * 256:(c % 2) * 256 + 256], Qf[:, c * 128:(c + 1) * 128], R3[:, c, :],
                            [Qf_b, R_b], [self.ps_b[pz[c // 2]]])
                for k2 in range(2):
                    self.CPY("act", z_[:, k2 * 512:(k2 + 1) * 512], self.ps[:, pz[k2], :], [self.ps_b[pz[k2]]], [z_b])
                pc = [self.bank(), self.bank()]
                for c in range(4):
                    self.PE(self.ps[:, pc[c // 2], (c % 2) * 256:(c % 2) * 256 + 256], Bo[:, c * 128:(c + 1) * 128], z3[:, c, :],
                            [Bo_b, z_b], [self.ps_b[pc[c // 2]]])
                for k2 in range(2):
                    self.CPY("act", cp_[:, k2 * 512:(k2 + 1) * 512], self.ps[:, pc[k2], :], [self.ps_b[pc[k2]]], [cp_b])
                py = [self.bank(), self.bank()]
                for c in range(4):
                    self.PE(self.ps[:, py[c // 2], (c % 2) * 256:(c % 2) * 256 + 256], Qf[:, c * 128:(c + 1) * 128], c3[:, c, :],
                            [Qf_b, cp_b], [self.ps_b[py[c // 2]]])
                for k2 in range(2):
                    self.TT(y16[:, k2 * 512:(k2 + 1) * 512], self.ps[:, py[k2], :], z_[:, k2 * 512:(k2 + 1) * 512], ALU.add,
                            [self.ps_b[py[k2]], z_b], [y16_b])
                pb = self.bank()
                pbv = self.ps[:, pb, :].bitcast(BF16)
                for c in range(4):
                    self.TR(pbv[:, c * 128:(c + 1) * 128], y3[:, c, 128:256], ident_b, [y16_b] + CB, [self.ps_b[pb]])
                self.CPY("act", wT4, pbv[:, 0:512], [self.ps_b[pb]], [wT4_b])
                for c in range(4):
                    cs = slice(c * 128, (c + 1) * 128)
                    vn, vn_b = vns[c % 2]
                    pa = self.bank()
                    self.PE(self.ps[:, pa, 0:128], wT4[:, cs], S16, [wT4_b, S16_b], [self.ps_b[pa]])
                    self.PE(self.ps[:, PO, cs], qd[:, cs], S16, [qd_b, S16_b], [self.ps_b[PO]], start=True, stop=False)
                    self.TT(vn, y3[:, c, 0:128], self.ps[:, pa, 0:128], ALU.subtract, [y16_b, self.ps_b[pa]], [vn_b])
                    self.PE(self.ps[:, PO, cs], Aq[:, cs], vn, [Aq_b, vn_b], [self.ps_b[PO]], start=False, stop=True)
                    pd = self.bank()
                    self.PE(self.ps[:, pd, 0:128], kd[:, cs], vn, [kd_b, vn_b], [self.ps_b[pd]])
                    gl = Rex[:, c * 128 + 127:c * 128 + 128]
                    self.STT(S16, S32, gl, self.ps[:, pd, 0:128], ALU.mult, ALU.add, [S32_b, Rex_b, self.ps_b[pd]], [S16_b])
                    self.STT(S32, S32, gl, self.ps[:, pd, 0:128], ALU.mult, ALU.add, [S32_b, Rex_b, self.ps_b[pd]], [S32_b])
                for c in range(4):
                    cs = slice(c * 128, (c + 1) * 128)
                    self.ACT(df[:, cs], self.ps[:, PO, cs], AF.Square, [self.ps_b[PO]], [df_b, sc4_b], accum=sc4[:, 4 + c:5 + c])
                self.ACT(sc4[:, 8:12], sc4[:, 4:8], AF.Ln, [sc4_b] + CF, [sc4_b], scale=1.0 / 128.0, bias=cf[:, 896:897])
                self.ACT(sc4[:, 8:12], sc4[:, 8:12], AF.Exp, [sc4_b], [sc4_b], scale=-0.5)
                self.TT(v4(nRb), v4(self.ps[:, PO, :]), bc(sc4[:, 8:12]), ALU.mult, [self.ps_b[PO], sc4_b], [nRb_b])
                self.TT(v4(on16), v4(nRb), bm(sm[:, 128:256]), ALU.mult, [nRb_b, sm_b], [on16_b])
                pb = self.bank()
                pbv = self.ps[:, pb, :].bitcast(BF16)
                for c in range(4):
                    cs = slice(c * 128, (c + 1) * 128)
                    self.TR(pbv[:, cs], on16[:, cs], ident_b, [on16_b] + CB, [self.ps_b[pb]])
                self.TT(oT[:, ts], pbv[:, 0:512], sgate[:, ts], ALU.mult, [self.ps_b[pb], sgate_b], [oT_b])
            S.op("sp", lambda e, h=h, oT=oT: e.dma_start(out=self.odn_d[h], in_=oT), [oT_b], [self.odn_b[h]], dma=True)

    def att_branch(self, l):
        S = self.S
        S.barrier()
        self.tmps = {}
        self.aoff = 0
        hT = self.hT
        cb, cf = self.cb, self.cf
        CB, CF = [self.cb_b], [self.cf_b]
        sm, sm_b = self.tmp("small", 512, F32)
        sm_b = self.sm_b
        win = self.w_in[l].rearrange("(kc p) n -> p kc n", p=128)
        attn, attn_b = self.tmp("attn", 4 * T)
        self.attn, self.attn_b = attn, attn_b
        mrg, mrg_b = self.tmp("mrg", 8 * T)
        self.mrg, self.mrg_b = mrg, mrg_b
        acc, acc_b = self.tmp("attacc", 2 * T, F32)
        acc3 = acc.rearrange("p (h t) -> p h t", t=T)
        attn3 = attn.rearrange("p (h t) -> p h t", t=T)
        BD_b = cb[:, 768:896]
        msk = cb[:, 640:768]
        U_b = cb[:, 128:256]
        self.aoff = self.AR
        qn, qn_b = self.tmp("qn", T)
        kn, kn_b = self.tmp("kn", T)
        z16, z16_b = self.tmp("z16", 512)
        sq, sq_b = self.tmp("asq", 512)
        rs, rs_b = self.tmp("ars", 512, F32)
        vt, vt_b = self.tmp("vt", 16 * 2 * 66)
        vt4 = vt.rearrange("p (b h d) -> p b h d", h=2, d=66)
        pt = [self.tmp("pt%d" % i, 256) for i in range(2)]
        S.op("dve", lambda e: e.memset(vt, 1.0), [], [vt_b])
        groups = ((128, 1), (512, 4), (2048, 16))
        pti = 0
        for c in (0, 2, 4, 1, 3, 5):
            g = c // 2
            d = groups[g][1]
            nblk = (T // d) // 128
            w1 = self.next_w()
            w2 = self.next_w()
            wb1, wb2 = self.wbuf[w1], self.wbuf[w2]
            S.op("pool", lambda e, wb1=wb1, c=c: [e.dma_start(out=wb1[:, :, 0:128], in_=win[:, :, c * 128:(c + 1) * 128]),
                                                  e.dma_start(out=wb1[:, :, 128:256], in_=win[:, :, 768 + c * 128:768 + (c + 1) * 128])],
                 [], [self.w_b[w1]], dma=True, ndma=2)
            S.op("pool", lambda e, wb2=wb2, c=c: e.dma_start(out=wb2[:, :, 0:128], in_=win[:, :, 1536 + c * 128:1536 + (c + 1) * 128]),
                 [], [self.w_b[w2]], dma=True)
            for qi, (dst, dst_b) in enumerate(((qn, qn_b), (kn, kn_b))):
                for t in range(NT):
                    sl = slice(t * 512, (t + 1) * 512)
                    pb = self.bank()
                    for kc in range(KC):
                        self.PE(self.ps[:, pb, :], wb1[:, kc, qi * 128:(qi + 1) * 128], hT[:, kc, sl],
                                [self.w_b[w1], self.h_b[kc]], [self.ps_b[pb]], start=(kc == 0), stop=(kc == KC - 1))
                    self.ACT(sq, self.ps[:, pb, :], AF.Square, [self.ps_b[pb]], [sq_b])
                    self.CPY("dve", z16, self.ps[:, pb, :], [self.ps_b[pb]], [z16_b])
                    p2 = self.bank()
                    self.PE(self.ps[:, p2, :], BD_b, sq, CB + [sq_b], [self.ps_b[p2]])
                    if qi == 0:
                        self.ACT(rs, self.ps[:, p2, :], AF.Sqrt, [self.ps_b[p2]] + CF, [rs_b], scale=1.0, bias=cf[:, 899:900])
                    else:
                        self.ACT(rs, self.ps[:, p2, :], AF.Sqrt, [self.ps_b[p2]] + CF, [rs_b], scale=1.0 / 64.0, bias=cf[:, 896:897])
                    S.op("dve", lambda e: e.reciprocal(out=rs, in_=rs), [rs_b], [rs_b])
                    self.STT(dst[:, sl], z16, sm[:, 96 + qi:97 + qi], rs, ALU.mult, ALU.mult, [z16_b, sm_b, rs_b], [dst_b])
            blocks = [(r, nb) for r in range(d) for nb in range(nblk)]
            for bi, (r, nb) in enumerate(blocks):
                st = nb * 128 * d + r
                tsl = slice(st, st + 127 * d + 1, d)
                pb = self.bank()
                for kc in range(KC):
                    self.PE(self.ps[:, pb, 0:128], hT[:, kc, tsl], wb2[:, kc, 0:128], [self.h_b[kc], self.w_b[w2]], [self.ps_b[pb]],
                            start=(kc == 0), stop=(kc == KC - 1))
                self.CPY("act", vt4[:, bi, :, 0:64], self.ps[:, pb, 0:128].rearrange("p (h d) -> p h d", d=64), [self.ps_b[pb]], [vt_b])
            for hh2 in range(2):
                hh = (c % 2) * 2 + hh2
                ps_ = slice(hh2 * 64, hh2 * 64 + 64)
                for bi, (r, nb) in enumerate(blocks):
                    st = nb * 128 * d + r
                    qsl = slice(st, st + 127 * d + 1, d)
                    pS = self.bank()
                    has_prev = nb > 0
                    if has_prev:
                        sp_ = (nb - 1) * 128 * d + r
                        ksl = slice(sp_, sp_ + 127 * d + 1, d)
                        self.PE(self.ps[:, pS, 0:128], kn[ps_, ksl], qn[ps_, qsl], [kn_b, qn_b], [self.ps_b[pS]])
                    self.PE(self.ps[:, pS, 128:256], kn[ps_, qsl], qn[ps_, qsl], [kn_b, qn_b], [self.ps_b[pS]])
                    P, P_b = pt[pti % 2]
                    pti += 1
                    lo = 0 if has_prev else 128
                    self.ACT(P[:, lo:256], self.ps[:, pS, lo:256], AF.Exp, [self.ps_b[pS]], [P_b])
                    if has_prev:
                        self.TT(P[:, 0:128], P[:, 0:128], msk, ALU.mult, [P_b] + CB, [P_b])
                    self.TT(P[:, 128:256], P[:, 128:256], U_b, ALU.mult, [P_b] + CB, [P_b])
                    pO = self.bank()
                    if has_prev:
                        self.PE(self.ps[0:65, pO, 0:128], vt4[:, bi - 1, hh2, 0:65], P[:, 0:128], [vt_b, P_b], [self.ps_b[pO]], start=True, stop=False)
                    self.PE(self.ps[0:65, pO, 0:128], vt4[:, bi, hh2, 0:65], P[:, 128:256], [vt_b, P_b], [self.ps_b[pO]],
                            start=(not has_prev), stop=True)
                    if g == 0:
                        self.CPY("dve", acc3[0:65, hh2, qsl], self.ps[0:65, pO, 0:128], [self.ps_b[pO]], [acc_b])
                    else:
                        self.TT(acc3[0:65, hh2, qsl], self.ps[0:65, pO, 0:128], acc3[0:65, hh2, qsl], ALU.add, [self.ps_b[pO], acc_b], [acc_b])
            if g == 2:
                S.op("dve", lambda e: e.reciprocal(out=acc3[64:65, :, :], in_=acc3[64:65, :, :]), [acc_b], [acc_b])
                for hh2 in range(2):
                    hh = (c % 2) * 2 + hh2
                    for t in range(NT):
                        sl = slice(t * 512, (t + 1) * 512)
                        pb = self.bank()
                        self.PE(self.ps[0:64, pb, :], cf[64:65, 256:320], acc3[64:65, hh2, sl], CF + [acc_b], [self.ps_b[pb]])
                        self.TT(attn3[0:64, hh, sl], acc3[0:64, hh2, sl], self.ps[0:64, pb, :], ALU.mult, [acc_b, self.ps_b[pb]], [attn_b])

    def mix_final(self, l):
        S = self.S
        S.barrier(engines=("pe", "act", "dve", "sp", "pool"))
        hT, xT = self.hT, self.xT
        cb, cf = self.cb, self.cf
        attn3 = self.attn.rearrange("p (h t) -> p h t", t=T)
        mrg3 = self.mrg.rearrange("p (c t) -> p c t", t=T)
        mrg_b = self.mrg_b
        odn_all = self.tmps["attacc"][0].bitcast(BF16)
        odn_t = [odn_all[:, i * 4096:(i + 1) * 4096].rearrange("p (h t) -> p h t", t=512) for i in range(2)]
        odn_tb = S.bufs("odn_re", 2)
        odn_src = self.odn_d.rearrange("h p t -> p h t")
        oi = 0
        win = self.w_in[l].rearrange("(kc p) n -> p kc n", p=128)
        wpd = self.w_pd[l].rearrange("(kc p) n -> p kc n", p=128)
        wo = self.w_o[l].rearrange("(kc p) n -> p kc n", p=128)
        wpa_d = self.w_pa[l].rearrange("(hh p) n -> p hh n", p=64)
        self.aoff = self.AR
        self.tmps.pop("qn", None)
        wpa, wpa_b = self.tmp("wpa", 4 * D)
        wpa3 = wpa.rearrange("p (h n) -> p h n", n=D)
        S.op("pool", lambda e: e.dma_start(out=wpa3[0:64, :, :], in_=wpa_d), [], [wpa_b], dma=True)
        ge = [self.tmp("ge%d" % i, 512, F32) for i in range(2)]
        m1, m1_b = self.tmp("m1", 512, F32)
        mc = self.modc
        for dc in range(KC):
            w1 = self.next_w()
            w2 = self.next_w()
            wb1, wb2 = self.wbuf[w1], self.wbuf[w2]
            S.op("pool", lambda e, wb1=wb1, dc=dc: [e.dma_start(out=wb1[:, :, 0:128], in_=win[:, :, OFF_MERGE + dc * 128:OFF_MERGE + (dc + 1) * 128]),
                                                   e.dma_start(out=wb1[:, :, 128:256], in_=win[:, :, OFF_MERGE + D + dc * 128:OFF_MERGE + D + (dc + 1) * 128])],
                 [], [self.w_b[w1]], dma=True, ndma=2)
            S.op("pool", lambda e, wb2=wb2, dc=dc: e.dma_start(out=wb2[:, :, 0:128], in_=wpd[:, :, dc * 128:(dc + 1) * 128]),
                 [], [self.w_b[w2]], dma=True)
            for t in range(NT):
                sl = slice(t * 512, (t + 1) * 512)
                od3, odn_b = odn_t[oi % 2], odn_tb[oi % 2]
                oi += 1
                S.op("sp", lambda e, od3=od3, sl=sl: e.dma_start(out=od3, in_=odn_src[:, :, sl]), self.odn_b, [odn_b], dma=True)
                pga, pgd, pya, pyd = self.bank(), self.bank(), self.bank(), self.bank()
                for gi, pg in enumerate((pga, pgd)):
                    for kc in range(KC):
                        self.PE(self.ps[:, pg, :], wb1[:, kc, gi * 128:(gi + 1) * 128], hT[:, kc, sl], [self.w_b[w1], self.h_b[kc]],
                                [self.ps_b[pg]], start=(kc == 0), stop=(kc == KC - 1))
                for hh in range(4):
                    self.PE(self.ps[:, pya, :], wpa3[0:64, hh, dc * 128:(dc + 1) * 128], attn3[0:64, hh, sl], [wpa_b, self.attn_b],
                            [self.ps_b[pya]], start=(hh == 0), stop=(hh == 3))
                for h8 in range(8):
                    self.PE(self.ps[:, pyd, :], wb2[:, h8, 0:128], od3[:, h8, :], [self.w_b[w2], odn_b], [self.ps_b[pyd]],
                            start=(h8 == 0), stop=(h8 == 7))
                for gi, (pg, py) in enumerate(((pga, pya), (pgd, pyd))):
                    gt_, gt_b = ge[gi]
                    self.ACT(gt_, self.ps[:, pg, :], AF.Exp, [self.ps_b[pg]], [gt_b], scale=-1.0)
                    self.TS(gt_, gt_, 1.0, ALU.add, [gt_b], [gt_b])
                    S.op("dve", lambda e, gt_=gt_: e.reciprocal(out=gt_, in_=gt_), [gt_b], [gt_b])
                    self.TT(gt_, gt_, self.ps[:, py, :], ALU.mult, [gt_b, self.ps_b[py]], [gt_b])
                self.TT(mrg3[:, dc, sl], ge[0][0], ge[1][0], ALU.add, [ge[0][1], ge[1][1]], [mrg_b])
        for half in range(4):
            w1 = self.next_w()
            wb1 = self.wbuf[w1]
            S.op("pool", lambda e, wb1=wb1, half=half: e.dma_start(out=wb1[:, :, 0:256], in_=wo[:, :, half * 256:(half + 1) * 256]),
                 [], [self.w_b[w1]], dma=True)
            for j2 in range(2):
                dc = half * 2 + j2
                for t in range(NT):
                    sl = slice(t * 512, (t + 1) * 512)
                    pb = self.bank()
                    for kc in range(KC):
                        self.PE(self.ps[:, pb, :], wb1[:, kc, j2 * 128:(j2 + 1) * 128], mrg3[:, kc, sl], [self.w_b[w1], mrg_b],
                                [self.ps_b[pb]], start=(kc == 0), stop=(kc == KC - 1))
                    self.STT(xT[:, dc, sl], self.ps[:, pb, :], mc[:, 24 + 16 + dc:24 + 16 + dc + 1], xT[:, dc, sl], ALU.mult, ALU.add,
                             [self.ps_b[pb], self.modc_b, self.x_b[dc]], [self.x_b[dc]])

def make_consts():
    c = np.zeros((128, 1024), np.float32)
    c[:, 0:128] = np.eye(128, dtype=np.float32)
    jj = np.arange(128)[:, None]
    ii = np.arange(128)[None, :]
    c[:, 128:256] = (jj <= ii).astype(np.float32)
    c[:, 256:384] = 1.0
    c[:, 384:512] = (jj < ii).astype(np.float32)
    c[:, 512:640] = (jj > ii).astype(np.float32)
    c[:, 640:768] = (jj >= ii).astype(np.float32)
    c[:, 768:896] = ((jj < 64) == (ii < 64)).astype(np.float32)
    c[:, 512:640] = np.where((jj > ii) & ((jj < 64) == (ii < 64)), 0.0, 30000.0)
    c[:, 896] = EPS
    c[:, 897] = 128.0 * EPS
    c[:, 898] = 1.0
    c[:, 899] = 64.0 * EPS
    return c


def make_small(inputs):
    L = DEPTH
    sm = np.zeros((L, 128, 512), np.float32)
    cw = np.asarray(inputs["conv_w"], np.float32)
    sm[:, :, 0:96] = cw.reshape(L, 4, 24, 128).transpose(0, 3, 2, 1).reshape(L, 128, 96)
    p = np.arange(128) % 64
    sm[:, :, 96] = np.asarray(inputs["q_norm"], np.float32)[:, p]
    sm[:, :, 97] = np.asarray(inputs["k_norm"], np.float32)[:, p]
    sm[:, :, 104:112] = np.asarray(inputs["a_log"], np.float32)[:, None, :]
    sm[:, :, 112:120] = np.asarray(inputs["dt_bias"], np.float32)[:, None, :]
    sm[:, :, 128:256] = np.asarray(inputs["dn_norm"], np.float32)[:, None, :]
    sm[:, :, 256:384] = np.tile(np.asarray(inputs["dt_bias"], np.float32), (1, 16))[:, None, :]
    sm[:, :, 384:512] = np.tile(np.asarray(inputs["a_log"], np.float32), (1, 16))[:, None, :]
    return sm


def prep_inputs(inputs, b):
    f = lambda a: np.ascontiguousarray(a, dtype=np.float32)
    m = {}
    m["x"] = f(inputs["x"][b])
    m["c"] = f(inputs["c"][b].reshape(KC, 128).T)
    m["ada_w"] = f(inputs["ada_w"])
    m["ada_b"] = f(inputs["ada_b"].reshape(DEPTH, 72, 128).transpose(0, 2, 1))
    nr = np.stack([inputs["norm_ff1"], inputs["norm_mix"], inputs["norm_ff2"]], axis=1)
    m["norms"] = f(nr.reshape(DEPTH, 3 * KC, 128).transpose(0, 2, 1))
    m["ffn1_w_up"] = f(inputs["ffn1_w_up"])
    m["ffn2_w_up"] = f(inputs["ffn2_w_up"])
    m["ffn1_w_down"] = f(inputs["ffn1_w_down"])
    m["ffn2_w_down"] = f(inputs["ffn2_w_down"])
    m["w_in"] = f(inputs["w_in"])
    m["w_proj_att"] = f(inputs["w_proj_att"])
    m["w_proj_dn"] = f(inputs["w_proj_dn"])
    m["w_out"] = f(inputs["w_out"])
    m["consts"] = make_consts()
    m["small"] = make_small(inputs)
    return m


_PROG = {}


def run(inputs, n_layers=DEPTH, stages=("ffn1", "mix", "ffn2"), cores=8, trace=False):
    key = (n_layers, tuple(stages))
    if key not in _PROG:
        _PROG[key] = Prog(n_layers, stages)
    p = _PROG[key]
    shared = prep_inputs(inputs, 0)
    in_maps = []
    for b in range(cores):
        m = dict(shared)
        m["x"] = np.ascontiguousarray(inputs["x"][b], dtype=np.float32)
        m["c"] = np.ascontiguousarray(inputs["c"][b].reshape(KC, 128).T, dtype=np.float32)
        in_maps.append(m)
    res = run_bass_kernel_spmd(p.nc, in_maps, core_ids=list(range(cores)), trace=trace)
    out = np.stack([r["out"] for r in res.results], axis=0)
    return out, res


def kernel(**inputs):
    out, _ = run(inputs)
    return out.astype(np.float32)
```

```python
import numpy as np
import concourse.bass as bass
import concourse.mybir as mybir
from concourse.bass_utils import run_bass_kernel_spmd

F32 = mybir.dt.float32
BF16 = mybir.dt.bfloat16
AF = mybir.ActivationFunctionType
ALU = mybir.AluOpType

D = 1024
T = 2048
DEPTH = 4
DFF = 2816
NFF = DFF // 128
KC = D // 128
NT = T // 512
NB = T // 128
N_IN = 8464
OFF_DN_QKV = 2304
OFF_DN_GATE = 5376
OFF_DN_A = 6400
OFF_DN_B = 6408
OFF_MERGE = 6416
EPS = 1e-6
SEM_LIMIT = 30000
STRICT_SAME_ENGINE = False


class Buf:
    __slots__ = ("name", "w", "rs", "dsem", "excl")

    def __init__(self, name, excl=False):
        self.name = name
        self.w = None
        self.rs = []
        self.dsem = None
        self.excl = excl


class Op:
    __slots__ = ("eng", "fn", "deps", "dma", "ndma", "sem", "val", "signals", "idx")


class DmaSem:
    def __init__(self, sched):
        self.s = sched
        self.sem = sched.new_sem()
        self.count = 0

    def take(self, n):
        if self.count + 16 * n > SEM_LIMIT:
            self.sem = self.s.new_sem()
            self.count = 0
        self.count += 16 * n
        return self.sem, self.count


class Sched:
    ENGS = ("pe", "act", "dve", "pool", "sp")

    def __init__(self, nc):
        self.nc = nc
        self.q = {e: [] for e in self.ENGS}
        self.nsem = 0
        self.dma_ops = []
        self.all_bufs = []

    def new_sem(self):
        self.nsem += 1
        return self.nc.alloc_semaphore("s%d" % self.nsem)

    def buf(self, name):
        b = Buf(name)
        return b

    def bufs(self, name, n, excl=False):
        return [Buf("%s%d" % (name, i), excl) for i in range(n)]

    def op(self, eng, fn, reads=(), writes=(), dma=False, ndma=1):
        o = Op()
        o.eng = eng
        o.fn = fn
        o.dma = dma
        o.ndma = ndma
        o.signals = dma
        o.sem = None
        o.val = 0
        deps = []
        for b in reads:
            if b.w is not None:
                deps.append((b.w, True))
            if b.excl:
                for r in b.rs:
                    if r.eng != eng:
                        deps.append((r, False))
        for b in writes:
            if b.w is not None:
                deps.append((b.w, False))
            for r in b.rs:
                deps.append((r, False))
        fd = {}
        for d, raw in deps:
            if d is o:
                continue
            if not d.dma and not dma and d.eng == eng:
                if eng == "pe":
                    continue
                if not raw and not STRICT_SAME_ENGINE:
                    continue
            if d.dma and dma and not raw:
                pass
            fd[id(d)] = d
        o.deps = list(fd.values())
        for b in reads:
            b.rs.append(o)
        for b in writes:
            b.w = o
            b.rs = []
        if dma:
            wb = writes[0]
            if wb.dsem is None:
                wb.dsem = DmaSem(self)
            o.sem, o.val = wb.dsem.take(ndma)
            self.dma_ops.append(o)
        o.idx = len(self.q[eng])
        self.q[eng].append(o)
        return o

    def barrier(self, engines=("pe", "act", "dve", "sp")):
        lasts = []
        for e in ("pe", "act", "dve", "pool"):
            for o in reversed(self.q[e]):
                if not o.dma and o.fn is not None:
                    lasts.append(o)
                    break
        dm = list(self.dma_ops)
        self.dma_ops = []
        for e in engines:
            o = Op()
            o.eng = e
            o.fn = None
            o.dma = False
            o.ndma = 0
            o.signals = False
            o.sem = None
            o.val = 0
            o.deps = [d for d in lasts if d.eng != e or e != "pe"] + dm
            o.idx = len(self.q[e])
            self.q[e].append(o)

    def emit(self):
        nc = self.nc
        for e in self.ENGS:
            for o in self.q[e]:
                for d in o.deps:
                    if not d.dma:
                        d.signals = True
        for e in self.ENGS:
            sem = None
            cnt = 0
            for o in self.q[e]:
                if o.dma or not o.signals:
                    continue
                if sem is None or cnt >= SEM_LIMIT:
                    sem = self.new_sem()
                    cnt = 0
                cnt += 1
                o.sem = sem
                o.val = cnt
        sched = self

        def run(e, eng):
            waited = {}
            for o in sched.q[e]:
                for d in o.deps:
                    key = id(d.sem)
                    if waited.get(key, 0) >= d.val:
                        continue
                    waited[key] = d.val
                    eng.wait_ge(d.sem, d.val)
                if o.fn is None:
                    continue
                r = o.fn(eng)
                if o.dma:
                    if not isinstance(r, (list, tuple)):
                        r = [r]
                    assert len(r) == o.ndma, (len(r), o.ndma)
                    for ins in r:
                        ins.then_inc(o.sem, 16)
                elif o.signals:
                    r.then_inc(o.sem, 1)

        with nc.Block() as block:
            @block.tensor
            def _(eng):
                run("pe", eng)

            @block.scalar
            def _(eng):
                run("act", eng)

            @block.vector
            def _(eng):
                run("dve", eng)

            @block.gpsimd
            def _(eng):
                run("pool", eng)

            @block.sync
            def _(eng):
                run("sp", eng)


class Prog:
    def __init__(self, n_layers=DEPTH, stages=("ffn1", "mix", "ffn2"), dbg=None):
        self.n_layers = n_layers
        self.stages = stages
        self.dbg = dbg
        nc = bass.Bass("TRN2", target_bir_lowering=False)
        self.nc = nc
        self.S = Sched(nc)
        self.decl_io()
        self.alloc()
        self.build()
        self.S.emit()

    def decl_io(self):
        nc = self.nc
        L = DEPTH

        def inp(name, shape):
            return nc.dram_tensor(name, list(shape), F32, kind="ExternalInput").ap()

        self.x_d = inp("x", (T, D))
        self.c_d = inp("c", (128, KC))
        self.ada_w = inp("ada_w", (L, D, 9 * D))
        self.ada_b = inp("ada_b", (L, 128, 72))
        self.norms = inp("norms", (L, 128, 3 * KC))
        self.w_up = [inp("ffn1_w_up", (L, D, 2 * DFF)), inp("ffn2_w_up", (L, D, 2 * DFF))]
        self.w_dn = [inp("ffn1_w_down", (L, DFF, D)), inp("ffn2_w_down", (L, DFF, D))]
        self.w_in = inp("w_in", (L, D, N_IN))
        self.cst = inp("consts", (128, 1024))
        self.small_d = inp("small", (L, 128, 512))
        self.w_pa = inp("w_proj_att", (L, 256, D))
        self.w_pd = inp("w_proj_dn", (L, D, D))
        self.w_o = inp("w_out", (L, D, D))
        self.odn_d = nc.dram_tensor("odn_scr", [8, 128, T], BF16, kind="Internal").ap()
        self.odn_b = self.S.bufs("odn_d", 8)
        self.att_d = nc.dram_tensor("att_scr", [4, 64, T], BF16, kind="Internal").ap()
        self.att_b = self.S.bufs("att_d", 4)
        self.out_d = nc.dram_tensor("out", [T, D], F32, kind="ExternalOutput").ap()

    def alloc(self):
        nc = self.nc
        S = self.S
        self.xT = nc.alloc_sbuf_tensor("xT", [128, KC, T], F32)
        self.x_b = S.bufs("x", KC)
        self.hT = nc.alloc_sbuf_tensor("hT", [128, KC, T], BF16)
        self.h_b = S.bufs("h", KC)
        self.NW = 4
        self.wbuf = [nc.alloc_sbuf_tensor("wbuf%d" % i, [128, KC, 256], BF16) for i in range(self.NW)]
        self.w_b = S.bufs("w", self.NW)
        self.wi = 0
        self.wd = nc.alloc_sbuf_tensor("wd", [128, 11, D], BF16)
        self.wd_b = S.bufs("wd", 11)
        self.cf = nc.alloc_sbuf_tensor("cf", [128, 1024], F32)
        self.cf_b = S.buf("cf")
        self.cb = nc.alloc_sbuf_tensor("cb", [128, 1024], BF16)
        self.cb_b = S.buf("cb")
        self.modv = nc.alloc_sbuf_tensor("modv", [128, 72], F32)
        self.modv_b = S.buf("modv")
        self.adab = nc.alloc_sbuf_tensor("adab", [128, 72], F32)
        self.adab_b = S.buf("adab")
        self.nrm = nc.alloc_sbuf_tensor("nrm", [128, 3 * KC], F32)
        self.nrm_b = S.buf("nrm")
        self.modc = nc.alloc_sbuf_tensor("modc", [128, 9 * KC], F32)
        self.modc_b = S.buf("modc")
        self.cact = nc.alloc_sbuf_tensor("cact", [128, KC], BF16)
        self.cact_b = S.buf("cact")
        self.c32 = nc.alloc_sbuf_tensor("c32", [128, 2 * KC], F32)
        self.c32_b = S.buf("c32")
        self.AR = getattr(Prog, 'AR_OVERRIDE', 33 * 1024)
        self.ar = nc.alloc_sbuf_tensor("arena", [128, self.AR], BF16)
        self.ps = nc.alloc_psum_tensor("ps", [128, 8, 512], F32)
        self.ps_b = S.bufs("ps", 8, excl=True)
        self.pi = 0

    def bank(self):
        i = self.pi
        self.pi = (self.pi + 1) % 7
        return i

    def next_w(self):
        i = self.wi
        self.wi = (self.wi + 1) % self.NW
        return i

    def ar_bf(self, off, n):
        return self.ar[:, off:off + n]

    def ar_f32(self, off, n):
        return self.ar[:, off:off + 2 * n].bitcast(F32)

    def build(self):
        S = self.S
        self.load_consts()
        self.load_x()
        for l in range(self.n_layers):
            self.mod_layer(l)
            if "ffn1" in self.stages:
                self.norm_mod(l, 0)
                self.ffn(l, 0)
            if "mix" in self.stages:
                self.norm_mod(l, 1)
                self.mixer(l)
            if "ffn2" in self.stages:
                self.norm_mod(l, 2)
                self.ffn(l, 1)
        self.store_x()

    def load_consts(self):
        S = self.S
        cf, cb = self.cf, self.cb
        S.op("sp", lambda e: e.dma_start(out=cf[:], in_=self.cst[:, :]), [], [self.cf_b], dma=True)
        S.op("dve", lambda e: e.tensor_copy(out=cb[:], in_=cf[:]), [self.cf_b], [self.cb_b])
        c32 = self.c32
        S.op("sp", lambda e: e.dma_start(out=c32[:, 0:KC], in_=self.c_d[:, :]), [], [self.c32_b], dma=True)
        S.op("act", lambda e: e.activation(out=c32[:, KC:2 * KC], in_=c32[:, 0:KC], func=AF.Exp, scale=-1.0),
             [self.c32_b], [self.c32_b])
        S.op("dve", lambda e: e.tensor_scalar_add(out=c32[:, KC:2 * KC], in0=c32[:, KC:2 * KC], scalar1=1.0),
             [self.c32_b], [self.c32_b])
        S.op("dve", lambda e: e.reciprocal(out=c32[:, KC:2 * KC], in_=c32[:, KC:2 * KC]),
             [self.c32_b], [self.c32_b])
        S.op("dve", lambda e: e.tensor_tensor(out=self.cact[:], in0=c32[:, 0:KC], in1=c32[:, KC:2 * KC], op=ALU.mult),
             [self.c32_b], [self.cact_b])

    def ident_f(self):
        return self.cf[:, 0:128]

    def load_x(self):
        S = self.S
        stg = [self.ar_f32(0, D), self.ar_f32(2 * D, D)]
        stg_b = S.bufs("xstg", 2)
        xT = self.xT
        for b in range(NB):
            s = b % 2
            S.op("sp", lambda e, b=b, s=s: e.dma_start(out=stg[s], in_=self.x_d[b * 128:(b + 1) * 128, :]),
                 [], [stg_b[s]], dma=True)
            for half in range(2):
                pb = self.bank()
                for q in range(4):
                    c = half * 4 + q
                    S.op("pe", lambda e, c=c, s=s, pb=pb, q=q: e.transpose(
                        self.ps[:, pb, q * 128:(q + 1) * 128], stg[s][:, c * 128:(c + 1) * 128], self.ident_f()),
                        [stg_b[s], self.cf_b], [self.ps_b[pb]])
                psv = self.ps[:, pb, :].rearrange("p (q i) -> p q i", i=128)
                dst = xT[:, half * 4:(half + 1) * 4, b * 128:(b + 1) * 128]
                xb = self.x_b[half * 4:(half + 1) * 4]
                if half == 0:
                    S.op("dve", lambda e, psv=psv, dst=dst: e.tensor_copy(out=dst, in_=psv), [self.ps_b[pb]], xb)
                else:
                    S.op("act", lambda e, psv=psv, dst=dst: e.copy(out=dst, in_=psv), [self.ps_b[pb]], xb)
        S.barrier()

    def store_x(self):
        S = self.S
        S.barrier()
        stg = [self.ar_f32(0, D), self.ar_f32(2 * D, D)]
        stg_b = [S.bufs("ostg%d_" % i, 2) for i in range(2)]
        xT = self.xT
        outs = []
        for b in range(NB):
            s = b % 2
            for half in range(2):
                pb = self.bank()
                for q in range(4):
                    c = half * 4 + q
                    S.op("pe", lambda e, c=c, b=b, pb=pb, q=q: e.transpose(
                        self.ps[:, pb, q * 128:(q + 1) * 128], xT[:, c, b * 128:(b + 1) * 128], self.ident_f()),
                        [self.x_b[c], self.cf_b], [self.ps_b[pb]])
                if half == 0:
                    S.op("dve", lambda e, s=s, pb=pb: e.tensor_copy(
                        out=stg[s][:, 0:512], in_=self.ps[:, pb, :]), [self.ps_b[pb]], [stg_b[s][0]])
                else:
                    S.op("act", lambda e, s=s, pb=pb: e.copy(
                        out=stg[s][:, 512:1024], in_=self.ps[:, pb, :]), [self.ps_b[pb]], [stg_b[s][1]])
            ob = S.buf("outd%d" % b)
            o = S.op("sp", lambda e, b=b, s=s: e.dma_start(out=self.out_d[b * 128:(b + 1) * 128, :], in_=stg[s]),
                     stg_b[s], [ob], dma=True)
            outs.append(ob)
        S.op("sp", None, outs, [])

    def mod_layer(self, l):
        S = self.S
        S.op("sp", lambda e: e.dma_start(out=self.adab[:], in_=self.ada_b[l]), [], [self.adab_b], dma=True)
        S.op("sp", lambda e: e.dma_start(out=self.nrm[:], in_=self.norms[l]), [], [self.nrm_b], dma=True)
        aw = self.ada_w[l].rearrange("(kc p) n -> p kc n", p=128)
        pb = self.bank()
        for blk in range(9 * D // 256):
            wi = self.next_w()
            wb = self.wbuf[wi]
            S.op("pool", lambda e, blk=blk, wb=wb: e.dma_start(out=wb[:], in_=aw[:, :, blk * 256:(blk + 1) * 256]),
                 [], [self.w_b[wi]], dma=True)
            for j in range(2):
                col = blk * 2 + j
                for kc in range(KC):
                    S.op("pe", lambda e, wb=wb, j=j, kc=kc, col=col: e.matmul(
                        self.ps[:, pb, col:col + 1], wb[:, kc, j * 128:(j + 1) * 128], self.cact[:, kc:kc + 1],
                        start=(kc == 0), stop=(kc == KC - 1)),
                        [self.w_b[wi], self.cact_b], [self.ps_b[pb]])
        S.op("dve", lambda e: e.tensor_tensor(out=self.modv[:], in0=self.ps[:, pb, 0:72], in1=self.adab[:], op=ALU.add),
             [self.ps_b[pb], self.adab_b], [self.modv_b])
        mv, mc, nr = self.modv, self.modc, self.nrm
        for j in range(3):
            S.op("dve", lambda e, j=j: e.scalar_tensor_tensor(
                out=mc[:, j * 24:j * 24 + 8], in0=mv[:, (3 * j + 1) * 8:(3 * j + 2) * 8], scalar=1.0,
                in1=nr[:, j * 8:(j + 1) * 8], op0=ALU.add, op1=ALU.mult),
                [self.modv_b, self.nrm_b], [self.modc_b])
            S.op("dve", lambda e, j=j: e.tensor_copy(out=mc[:, j * 24 + 8:j * 24 + 16], in_=mv[:, (3 * j) * 8:(3 * j + 1) * 8]),
                 [self.modv_b], [self.modc_b])
            gsc = 1.0 if j == 1 else 0.5
            S.op("dve", lambda e, j=j, gsc=gsc: e.tensor_scalar(
                out=mc[:, j * 24 + 16:j * 24 + 24], in0=mv[:, (3 * j + 2) * 8:(3 * j + 3) * 8], scalar1=gsc, scalar2=None,
                op0=ALU.mult),
                [self.modv_b], [self.modc_b])

    def norm_mod(self, l, j):
        S = self.S
        S.barrier()
        xT, hT = self.xT, self.hT
        sq = [self.ar_bf(0, T), self.ar_bf(T, T)]
        sq_b = S.bufs("sq", 2)
        rstd = self.ar_f32(2 * T, T)
        rstd_b = S.buf("rstd")
        tmp = [self.ar_bf(6 * T, T), self.ar_bf(7 * T, T)]
        tmp_b = S.bufs("ntmp", 2)
        ones = self.cb[:, 256:384]
        pbs = [self.bank() for _ in range(4)]
        for kc in range(KC):
            s = kc % 2
            S.op("act", lambda e, kc=kc, s=s: e.activation(out=sq[s], in_=xT[:, kc, :], func=AF.Square),
                 [self.x_b[kc]], [sq_b[s]])
            for t in range(NT):
                S.op("pe", lambda e, kc=kc, s=s, t=t: e.matmul(
                    self.ps[:, pbs[t], :], ones, sq[s][:, t * 512:(t + 1) * 512], start=(kc == 0), stop=(kc == KC - 1)),
                    [sq_b[s], self.cb_b], [self.ps_b[pbs[t]]])
        for t in range(NT):
            S.op("act", lambda e, t=t: e.activation(out=rstd[:, t * 512:(t + 1) * 512], in_=self.ps[:, pbs[t], :],
                                                    func=AF.Ln, scale=1.0 / D, bias=self.cf[:, 896:897]),
                 [self.ps_b[pbs[t]], self.cf_b], [rstd_b])
        S.op("act", lambda e: e.activation(out=rstd, in_=rstd, func=AF.Exp, scale=-0.5), [rstd_b], [rstd_b])
        mc = self.modc
        for kc in range(KC):
            s = kc % 2
            S.op("dve", lambda e, kc=kc, s=s: e.tensor_tensor(out=tmp[s], in0=xT[:, kc, :], in1=rstd, op=ALU.mult),
                 [self.x_b[kc], rstd_b], [tmp_b[s]])
            S.op("act", lambda e, kc=kc, s=s: e.activation(
                out=hT[:, kc, :], in_=tmp[s], func=AF.Identity,
                scale=mc[:, j * 24 + kc:j * 24 + kc + 1], bias=mc[:, j * 24 + 8 + kc:j * 24 + 8 + kc + 1]),
                [tmp_b[s], self.modc_b], [self.h_b[kc]])

    def ffn(self, l, which):
        S = self.S
        S.barrier()
        j = 0 if which == 0 else 2
        xT, hT = self.xT, self.hT
        wu = self.w_up[which][l].rearrange("(kc p) n -> p kc n", p=128)
        wdn = self.w_dn[which][l]
        act = self.ar[:, 0:11 * T].rearrange("p (j t) -> p j t", t=T)
        act_b = S.bufs("act", 11)
        sg = [self.ar_f32(11 * T, 512), self.ar_f32(11 * T + 1024, 512)]
        sg_b = S.bufs("sg", 2)
        mc = self.modc
        sgi = 0
        for half in range(2):
            def load_wd():
                for jj in range(11):
                    r0 = (half * 11 + jj) * 128
                    S.op("pool", lambda e, jj=jj, r0=r0: e.dma_start(out=self.wd[:, jj, :], in_=wdn[r0:r0 + 128, :]),
                         [], [self.wd_b[jj]], dma=True)
            for blk in range(6):
                nj = 2 if blk < 5 else 1
                j0 = half * 11 + blk * 2
                wg_i = self.next_w()
                wu_i = self.next_w()
                wgb, wub = self.wbuf[wg_i], self.wbuf[wu_i]
                ncol = nj * 128
                S.op("pool", lambda e, wgb=wgb, j0=j0, ncol=ncol: e.dma_start(
                    out=wgb[:, :, 0:ncol], in_=wu[:, :, j0 * 128:j0 * 128 + ncol]), [], [self.w_b[wg_i]], dma=True)
                S.op("pool", lambda e, wub=wub, j0=j0, ncol=ncol: e.dma_start(
                    out=wub[:, :, 0:ncol], in_=wu[:, :, DFF + j0 * 128:DFF + j0 * 128 + ncol]), [], [self.w_b[wu_i]], dma=True)
                if blk == 2:
                    load_wd()
                for jj in range(nj):
                    ja = blk * 2 + jj
                    for t in range(NT):
                        pg, pu = self.bank(), self.bank()
                        for kc in range(KC):
                            S.op("pe", lambda e, kc=kc, jj=jj, t=t, pg=pg, wgb=wgb: e.matmul(
                                self.ps[:, pg, :], wgb[:, kc, jj * 128:(jj + 1) * 128], hT[:, kc, t * 512:(t + 1) * 512],
                                start=(kc == 0), stop=(kc == KC - 1)),
                                [self.w_b[wg_i], self.h_b[kc]], [self.ps_b[pg]])
                        for kc in range(KC):
                            S.op("pe", lambda e, kc=kc, jj=jj, t=t, pu=pu, wub=wub: e.matmul(
                                self.ps[:, pu, :], wub[:, kc, jj * 128:(jj + 1) * 128], hT[:, kc, t * 512:(t + 1) * 512],
                                start=(kc == 0), stop=(kc == KC - 1)),
                                [self.w_b[wu_i], self.h_b[kc]], [self.ps_b[pu]])
                        s = sgi % 2
                        sgi += 1
                        S.op("act", lambda e, s=s, pg=pg: e.activation(out=sg[s], in_=self.ps[:, pg, :], func=AF.Silu),
                             [self.ps_b[pg]], [sg_b[s]])
                        S.op("dve", lambda e, s=s, pu=pu, ja=ja, t=t: e.tensor_tensor(
                            out=act[:, ja, t * 512:(t + 1) * 512], in0=sg[s], in1=self.ps[:, pu, :], op=ALU.mult),
                            [sg_b[s], self.ps_b[pu]], [act_b[ja]])
            for t in range(NT):
                for dc in range(KC):
                    pb = self.bank()
                    for jj in range(11):
                        S.op("pe", lambda e, jj=jj, dc=dc, t=t, pb=pb: e.matmul(
                            self.ps[:, pb, :], self.wd[:, jj, dc * 128:(dc + 1) * 128], act[:, jj, t * 512:(t + 1) * 512],
                            start=(jj == 0), stop=(jj == 10)),
                            [self.wd_b[jj], act_b[jj]], [self.ps_b[pb]])
                    S.op("dve", lambda e, dc=dc, t=t, pb=pb: e.scalar_tensor_tensor(
                        out=xT[:, dc, t * 512:(t + 1) * 512], in0=self.ps[:, pb, :],
                        scalar=mc[:, j * 24 + 16 + dc:j * 24 + 16 + dc + 1],
                        in1=xT[:, dc, t * 512:(t + 1) * 512], op0=ALU.mult, op1=ALU.add),
                        [self.ps_b[pb], self.modc_b, self.x_b[dc]], [self.x_b[dc]])

    def PE(self, out, lhsT, rhs, r, w, start=True, stop=True):
        return self.S.op("pe", lambda e: e.matmul(out, lhsT, rhs, start=start, stop=stop), r, w)

    def TR(self, out, in_, ident, r, w):
        return self.S.op("pe", lambda e: e.transpose(out, in_, ident), r, w)

    def ACT(self, out, in_, func, r, w, scale=None, bias=None, accum=None):
        kw = {}
        if scale is not None:
            kw["scale"] = scale
        if bias is not None:
            kw["bias"] = bias
        if accum is not None:
            kw["accum_out"] = accum
        return self.S.op("act", lambda e: e.activation(out=out, in_=in_, func=func, **kw), r, w)

    def TT(self, out, in0, in1, op, r, w):
        return self.S.op("dve", lambda e: e.tensor_tensor(out=out, in0=in0, in1=in1, op=op), r, w)

    def TS(self, out, in0, s1, op0, r, w, s2=None, op1=None):
        if op1 is None:
            return self.S.op("dve", lambda e: e.tensor_scalar(out=out, in0=in0, scalar1=s1, scalar2=None, op0=op0), r, w)
        return self.S.op("dve", lambda e: e.tensor_scalar(out=out, in0=in0, scalar1=s1, scalar2=s2, op0=op0, op1=op1), r, w)

    def STT(self, out, in0, sc, in1, op0, op1, r, w):
        return self.S.op("dve", lambda e: e.scalar_tensor_tensor(out=out, in0=in0, scalar=sc, in1=in1, op0=op0, op1=op1), r, w)

    def CPY(self, eng, out, in_, r, w):
        if eng == "dve":
            return self.S.op("dve", lambda e: e.tensor_copy(out=out, in_=in_), r, w)
        return self.S.op("act", lambda e: e.copy(out=out, in_=in_), r, w)

    def tmp(self, name, n, dt=BF16):
        if name in self.tmps:
            return self.tmps[name]
        ne = n if dt == BF16 else 2 * n
        if self.aoff % 2:
            self.aoff += 1
        off = self.aoff
        self.aoff += ne
        assert self.aoff <= self.AR + 11 * D, ("arena overflow", name, self.aoff)
        if off + ne <= self.AR:
            base = self.ar[:, off:off + ne]
        else:
            if off < self.AR:
                off = self.AR
                self.aoff = off + ne
            o2 = off - self.AR
            base = self.wd[:].rearrange("p a b -> p (a b)")[:, o2:o2 + ne]
        ap = base if dt == BF16 else base.bitcast(F32)
        if not hasattr(self, "toff"):
            self.toff = {}
        self.toff[name] = (off, ne, dt == F32)
        b = self.S.buf(name)
        self.tmps[name] = (ap, b)
        return ap, b

    def mixer(self, l):
        S = self.S
        S.barrier(engines=("pe", "act", "dve", "sp", "pool"))
        self.tmps = {}
        self.aoff = 0
        self.mix_setup(l)
        if "nodn" not in self.stages:
            self.dn_branch(l)
        if "noatt" not in self.stages:
            self.att_branch(l)
        if "nofinal" not in self.stages:
            self.mix_final(l)
        S.barrier(engines=("pe", "act", "dve", "sp", "pool"))

    def mix_setup(self, l):
        S = self.S
        sm, sm_b = self.tmp("small", 512, F32)
        self.sm, self.sm_b = sm, sm_b
        S.op("sp", lambda e: e.dma_start(out=sm, in_=self.small_d[l]), [], [sm_b], dma=True)

    def dn_branch(self, l):
        S = self.S
        hT = self.hT
        sm, sm_b = self.sm, self.sm_b
        cb, cf = self.cb, self.cf
        ident_b = cb[:, 0:128]
        U_b = cb[:, 128:256]
        ones_b = cb[:, 256:384]
        SU_f = cf[:, 384:512]
        SL_b = cb[:, 512:640]
        ones_f = cf[:, 256:384]
        ident_f = cf[:, 0:128]
        tri_f = cf[:, 128:256]
        win = self.w_in[l].rearrange("(kc p) n -> p kc n", p=128)
        CB = [self.cb_b]
        CF = [self.cf_b]
        wi = self.next_w()
        wab = self.wbuf[wi]
        S.op("pool", lambda e: e.dma_start(out=wab[:, :, 0:16], in_=win[:, :, OFF_DN_A:OFF_DN_A + 16]), [], [self.w_b[wi]], dma=True)
        pab = self.bank()
        for blk in range(NB):
            for kc in range(KC):
                self.PE(self.ps[:, pab, blk * 16:(blk + 1) * 16], hT[:, kc, blk * 128:(blk + 1) * 128], wab[:, kc, 0:16],
                        [self.h_b[kc], self.w_b[wi]], [self.ps_b[pab]], start=(kc == 0), stop=(kc == KC - 1))
        abv = self.ps[:, pab, 0:256].rearrange("p (b c) -> p b c", c=16)
        g_t, g_b = self.tmp("g_tok", 128, F32)
        be_t, be_b = self.tmp("be_tok", 128, F32)
        gc_t, gc_b = self.tmp("gc_tok", 128, F32)
        t1, t1_b = self.tmp("ab_t1", 128, F32)
        g3 = g_t.rearrange("p (b c) -> p b c", c=8)
        be3 = be_t.rearrange("p (b c) -> p b c", c=8)
        t13 = t1.rearrange("p (b c) -> p b c", c=8)
        dtb = sm[:, 256:384].rearrange("p (b c) -> p b c", c=8)
        self.TT(t13, abv[:, :, 0:8], dtb, ALU.add, [self.ps_b[pab], sm_b], [t1_b])
        self.ACT(t1, t1, AF.Exp, [t1_b], [t1_b])
        self.ACT(t1, t1, AF.Ln, [t1_b], [t1_b], bias=cf[:, 898:899])
        ea, ea_b = self.tmp("expalog", 128, F32)
        self.ACT(ea, sm[:, 384:512], AF.Exp, [sm_b], [ea_b])
        self.STT(g_t, t1, -1.0, ea, ALU.mult, ALU.mult, [t1_b, ea_b], [g_b])
        self.ACT(be3, abv[:, :, 8:16], AF.Exp, [self.ps_b[pab]], [be_b], scale=-1.0)
        self.TS(be_t, be_t, 1.0, ALU.add, [be_b], [be_b])
        S.op("dve", lambda e: e.reciprocal(out=be_t, in_=be_t), [be_b], [be_b])
        pgc = self.bank()
        self.PE(self.ps[:, pgc, 0:128], tri_f, g_t, CF + [g_b], [self.ps_b[pgc]])
        self.CPY("dve", gc_t, self.ps[:, pgc, 0:128], [self.ps_b[pgc]], [gc_b])
        nbe_t, nbe_b = self.tmp("nbe_tok", 128, F32)
        begc_t, begc_b = self.tmp("begc_tok", 128, F32)
        self.TS(nbe_t, be_t, -1.0, ALU.mult, [be_b], [nbe_b])
        self.ACT(begc_t, gc_t, AF.Exp, [gc_b], [begc_b])
        self.TT(begc_t, begc_t, be_t, ALU.mult, [begc_b, be_b], [begc_b])

        zc = [self.tmp("zc%d" % f, 2052) for f in range(3)]
        sgate, sgate_b = self.tmp("sgate", T)
        fT = [self.tmp("fT%d" % f, T) for f in range(3)]
        ktok, ktok_b = self.tmp("ktok", T)
        vtok, vtok_b = self.tmp("vtok", T)
        ktok3 = ktok.rearrange("p (b d) -> p b d", d=128)
        vtok3 = vtok.rearrange("p (b d) -> p b d", d=128)
        sq, sq_b = self.tmp("dsq", 512)
        rs, rs_b = self.tmp("drs", 512, F32)
        dg = [self.tmp("diag%d" % i, 128) for i in range(12)]
        S32, S32_b = self.tmp("S32", 128, F32)
        S16, S16_b = self.tmp("S16", 128)
        ost = [self.tmp("ost0", T)] * 2
        for f in range(3):
            S.op("dve", lambda e, f=f: e.memset(zc[f][0][:, 0:4], 0.0), [], [zc[f][1]])

        for h in range(8):
            cols = [OFF_DN_QKV + h * 128, OFF_DN_QKV + 1024 + h * 128, OFF_DN_QKV + 2048 + h * 128, OFF_DN_GATE + h * 128]
            w1 = self.next_w()
            w2 = self.next_w()
            wb1, wb2 = self.wbuf[w1], self.wbuf[w2]
            S.op("pool", lambda e, wb1=wb1, cols=cols: [e.dma_start(out=wb1[:, :, 0:128], in_=win[:, :, cols[0]:cols[0] + 128]),
                                                          e.dma_start(out=wb1[:, :, 128:256], in_=win[:, :, cols[1]:cols[1] + 128])],
                 [], [self.w_b[w1]], dma=True, ndma=2)
            S.op("pool", lambda e, wb2=wb2, cols=cols: [e.dma_start(out=wb2[:, :, 0:128], in_=win[:, :, cols[2]:cols[2] + 128]),
                                                          e.dma_start(out=wb2[:, :, 128:256], in_=win[:, :, cols[3]:cols[3] + 128])],
                 [], [self.w_b[w2]], dma=True, ndma=2)
            for f in range(3):
                for i in range(4):
                    c = f * 8 + h
                    self.TS(dg[f * 4 + i][0], ident_b, sm[:, c * 4 + i:c * 4 + i + 1], ALU.mult, CB + [sm_b], [dg[f * 4 + i][1]])
            for f in range(4):
                wb, wbb = (wb1, self.w_b[w1]) if f < 2 else (wb2, self.w_b[w2])
                co = (f % 2) * 128
                for t in range(NT):
                    pb = self.bank()
                    for kc in range(KC):
                        self.PE(self.ps[:, pb, :], wb[:, kc, co:co + 128], hT[:, kc, t * 512:(t + 1) * 512],
                                [wbb, self.h_b[kc]], [self.ps_b[pb]], start=(kc == 0), stop=(kc == KC - 1))
                    if f < 3:
                        self.CPY("dve" if t % 2 == 0 else "act", zc[f][0][:, 4 + t * 512:4 + (t + 1) * 512], self.ps[:, pb, :],
                                 [self.ps_b[pb]], [zc[f][1]])
                    else:
                        self.ACT(sgate[:, t * 512:(t + 1) * 512], self.ps[:, pb, :], AF.Silu, [self.ps_b[pb]], [sgate_b])
            for f in range(3):
                for t in range(NT):
                    pb = self.bank()
                    for i in range(4):
                        self.PE(self.ps[:, pb, :], dg[f * 4 + i][0], zc[f][0][:, 1 + t * 512 + i:1 + t * 512 + i + 512],
                                [dg[f * 4 + i][1], zc[f][1]], [self.ps_b[pb]], start=(i == 0), stop=(i == 3))
                    self.ACT(fT[f][0][:, t * 512:(t + 1) * 512], self.ps[:, pb, :], AF.Silu, [self.ps_b[pb]], [fT[f][1]])
            for f in range(2):
                for t in range(NT):
                    sl = slice(t * 512, (t + 1) * 512)
                    self.TT(sq, fT[f][0][:, sl], fT[f][0][:, sl], ALU.mult, [fT[f][1]], [sq_b])
                    pb = self.bank()
                    self.PE(self.ps[:, pb, :], ones_b, sq, CB + [sq_b], [self.ps_b[pb]])
                    if f == 0:
                        self.ACT(rs, self.ps[:, pb, :], AF.Ln, [self.ps_b[pb]] + CF, [rs_b], scale=128.0, bias=cf[:, 897:898])
                    else:
                        self.ACT(rs, self.ps[:, pb, :], AF.Ln, [self.ps_b[pb]] + CF, [rs_b], scale=1.0, bias=cf[:, 896:897])
                    self.ACT(rs, rs, AF.Exp, [rs_b], [rs_b], scale=-0.5)
                    self.TT(fT[f][0][:, sl], fT[f][0][:, sl], rs, ALU.mult, [fT[f][1], rs_b], [fT[f][1]])
            for (src, dst3, dst_b) in ((fT[1], ktok3, ktok_b), (fT[2], vtok3, vtok_b)):
                for g4 in range(4):
                    pb = self.bank()
                    pbv = self.ps[:, pb, :].bitcast(BF16)
                    for q in range(4):
                        blk = g4 * 4 + q
                        self.TR(pbv[:, q * 128:(q + 1) * 128], src[0][:, blk * 128:(blk + 1) * 128], ident_b,
                                [src[1]] + CB, [self.ps_b[pb]])
                    self.CPY("dve" if g4 % 2 == 0 else "act", dst3[:, g4 * 4:(g4 + 1) * 4, :],
                             pbv[:, 0:512].rearrange("p (q d) -> p q d", d=128), [self.ps_b[pb]], [dst_b])
            S.op("dve", lambda e: e.memset(S32, 0.0), [], [S32_b])
            S.op("dve", lambda e: e.memset(S16, 0.0), [], [S16_b])
            oT, oT_b = ost[h % 2]
            qT, kT = fT[0], fT[1]
            def alias(name, base, off, n, dt=F32):
                bap, bb = base
                ne = n if dt == BF16 else 2 * n
                ap = bap[:, off:off + ne]
                return (ap if dt == BF16 else ap.bitcast(F32)), bb

            def v4(ap):
                return ap.rearrange("p (c i) -> p c i", i=128)

            def bm(ap):
                return ap.unsqueeze(1).to_broadcast([128, 4, 128])

            def bc(ap4):
                return ap4.unsqueeze(2).to_broadcast([128, 4, 128])

            Xs = [alias("Xa", zc[0], 4, 512), alias("Xb", zc[0], 1028, 512)]
            Xts = [alias("Xta", zc[1], 4, 512), alias("Xtb", zc[1], 1028, 512)]
            Qs = [alias("Qa", zc[2], 4, 512), alias("Qb", zc[2], 1028, 512)]
            rr, rr_b = self.tmp("rr", 1024, F32)
            df, df_b = self.tmp("df", 512, F32)
            LtM, LtM_b = self.tmp("LtM", 512, F32)
            LM, LM_b = self.tmp("LM", 512, F32)
            nRb, nRb_b = self.tmp("nRb", 512, F32)
            Rex, Rex_b = self.tmp("Rex", 512, F32)
            Bf, Bf_b = self.tmp("Bf", 512, F32)
            Bo, Bo_b = self.tmp("Bo", 512, F32)
            R, R_b = self.tmp("Rr", 1024, F32)
            cp_, cp_b = self.tmp("cpr", 1024, F32)
            y16, y16_b = self.tmp("y16", 1024)
            Aq, Aq_b = self.tmp("Aq", 512)
            qd, qd_b = self.tmp("qd", 512)
            kd, kd_b = self.tmp("kd", 512)
            wT4, wT4_b = self.tmp("wT4", 512)
            on16, on16_b = self.tmp("on16", 512)
            vns = [self.tmp("vn%d" % i, 128) for i in range(2)]
            sc4, sc4_b = self.tmp("sc4", 16, F32)
            z_, z_b = rr, rr_b
            R3 = R.rearrange("p (c i) -> p c i", i=256)
            z3 = z_.rearrange("p (c i) -> p c i", i=256)
            c3 = cp_.rearrange("p (c i) -> p c i", i=256)
            y3 = y16.rearrange("p (c i) -> p c i", i=256)
            POSL = cf[:, 512:640]
            BD = cf[:, 768:896]
            PO = 7
            for qi in range(4):
                n0 = qi * 4
                ts = slice(n0 * 128, n0 * 128 + 512)

                def cols(t_):
                    return t_[:, n0 * 8 + h:n0 * 8 + h + 25:8]
                self.TT(v4(rr[:, 0:512]), bm(tri_f), bc(cols(g_t)), ALU.mult, CF + [g_b], [rr_b])
                self.TT(v4(rr[:, 512:1024]), bm(ident_f), bc(cols(be_t)), ALU.mult, CF + [be_b], [rr_b])
                pRg, pRb = self.bank(), self.bank()
                self.PE(self.ps[:, pRg, :], ones_f, rr[:, 0:512], CF + [rr_b], [self.ps_b[pRg]])
                self.PE(self.ps[:, pRb, :], ones_f, rr[:, 512:1024], CF + [rr_b], [self.ps_b[pRb]])
                self.TT(v4(df), v4(self.ps[:, pRg, :]), bc(cols(gc_t)), ALU.subtract, [self.ps_b[pRg], gc_b], [df_b])
                self.TS(LtM, df, 0.0, ALU.min, [df_b], [LtM_b])
                self.STT(v4(LM), v4(df), 0.0, bm(POSL), ALU.max, ALU.add, [df_b] + CF, [LM_b])
                self.STT(v4(nRb), v4(self.ps[:, pRb, :]), -1.0, bm(SU_f), ALU.mult, ALU.mult, [self.ps_b[pRb]] + CF, [nRb_b])
                self.TT(sc4[:, 0:4], self.ps[:, pRg, 127:512:128], cols(gc_t), ALU.subtract, [self.ps_b[pRg], gc_b], [sc4_b])
                self.ACT(Rex, self.ps[:, pRg, :], AF.Exp, [self.ps_b[pRg]], [Rex_b])
                self.ACT(LtM, LtM, AF.Exp, [LtM_b], [LtM_b])
                self.ACT(LM, LM, AF.Exp, [LM_b], [LM_b], scale=-1.0)
                self.ACT(sc4[:, 0:4], sc4[:, 0:4], AF.Exp, [sc4_b], [sc4_b])
                pK, pQ = self.bank(), self.bank()
                for c in range(4):
                    bs = slice((n0 + c) * 128, (n0 + c + 1) * 128)
                    cs = slice(c * 128, (c + 1) * 128)
                    self.PE(self.ps[:, pK, cs], kT[0][:, bs], kT[0][:, bs], [kT[1]], [self.ps_b[pK]])
                    self.PE(self.ps[:, pQ, cs], kT[0][:, bs], qT[0][:, bs], [kT[1], qT[1]], [self.ps_b[pQ]])
                X0, X0_b = Xs[0]
                Xt0, Xt0_b = Xts[0]
                self.TT(Bf, self.ps[:, pK, :], LtM, ALU.mult, [self.ps_b[pK], LtM_b], [Bf_b])
                self.TT(Bf, Bf, nRb, ALU.mult, [Bf_b, nRb_b], [Bf_b])
                self.TT(v4(X0), v4(Bf), bm(BD), ALU.mult, [Bf_b] + CF, [X0_b])
                self.TT(Bo, Bf, X0, ALU.subtract, [Bf_b, X0_b], [Bo_b])
                self.TT(v4(df), v4(self.ps[:, pK, :]), bc(cols(nbe_t)), ALU.mult, [self.ps_b[pK], nbe_b], [df_b])
                self.TT(Xt0, df, LM, ALU.mult, [df_b, LM_b], [Xt0_b])
                self.TT(nRb, self.ps[:, pQ, :], LtM, ALU.mult, [self.ps_b[pQ], LtM_b, Bf_b], [nRb_b])
                self.TT(v4(Aq), v4(nRb), bm(tri_f), ALU.mult, [nRb_b] + CF, [Aq_b])
                self.TT(qd, qT[0][:, ts], Rex, ALU.mult, [qT[1], Rex_b], [qd_b])
                self.TT(v4(kd), ktok3[:, n0:n0 + 4, :], bc(sc4[:, 0:4]), ALU.mult, [ktok_b, sc4_b], [kd_b])
                self.TT(R3[:, :, 0:128], vtok3[:, n0:n0 + 4, :], bc(cols(be_t)), ALU.mult, [vtok_b, be_b], [R_b])
                self.TT(R3[:, :, 128:256], ktok3[:, n0:n0 + 4, :], bc(cols(begc_t)), ALU.mult, [ktok_b, begc_b], [R_b])
                qc = 0
                self.TT(v4(Qs[0][0]), v4(X0), bm(ident_f), ALU.add, [X0_b] + CF, [Qs[0][1]])
                xc = 0
                for m in range(5):
                    Xc, Xc_b = Xs[xc]
                    Xn, Xn_b = Xs[1 - xc]
                    Xtc, Xtc_b = Xts[xc]
                    Xtn, Xtn_b = Xts[1 - xc]
                    if m < 4:
                        pX = self.bank()
                        for c in range(4):
                            cs = slice(c * 128, (c + 1) * 128)
                            self.PE(self.ps[:, pX, cs], Xtc[:, cs], Xc[:, cs], [Xtc_b, Xc_b], [self.ps_b[pX]])
                    pXt = self.bank()
                    for c in range(4):
                        cs = slice(c * 128, (c + 1) * 128)
                        self.PE(self.ps[:, pXt, cs], Xc[:, cs], Xtc[:, cs], [Xtc_b, Xc_b], [self.ps_b[pXt]])
                    if m < 4:
                        self.CPY("act", Xn, self.ps[:, pX, :], [self.ps_b[pX]], [Xn_b])
                    self.CPY("act", Xtn, self.ps[:, pXt, :], [self.ps_b[pXt]], [Xtn_b])
                    pQq = self.bank()
                    Qc, Qc_b = Qs[qc]
                    Qn, Qn_b = Qs[1 - qc]
                    for c in range(4):
                        cs = slice(c * 128, (c + 1) * 128)
                        self.PE(self.ps[:, pQq, cs], Xtn[:, cs], Qc[:, cs], [Xtn_b, Qc_b], [self.ps_b[pQq]])
                    self.TT(Qn, self.ps[:, pQq, :], Qc, ALU.add, [self.ps_b[pQq], Qc_b], [Qn_b])
                    qc = 1 - qc
                    xc = 1 - xc
                Qf, Qf_b = Qs[qc]
                pz = [self.bank(), self.bank()]
                for c in range(4):
                    self.PE(self.ps[:, pz[c // 2], (c % 2) * 256:(c % 2) * 256 + 256], Qf[:, c * 128:(c + 1) * 128], R3[:, c, :],
                            [Qf_b, R_b], [self.ps_b[pz[c // 2]]])
                for k2 in range(2):
                    self.CPY("act", z_[:, k2 * 512:(k2 + 1) * 512], self.ps[:, pz[k2], :], [self.ps_b[pz[k2]]], [z_b])
                pc = [self.bank(), self.bank()]
                for c in range(4):
                    self.PE(self.ps[:, pc[c // 2], (c % 2) * 256:(c % 2) * 256 + 256], Bo[:, c * 128:(c + 1) * 128], z3[:, c, :],
                            [Bo_b, z_b], [self.ps_b[pc[c // 2]]])
                for k2 in range(2):
                    self.CPY("act", cp_[:, k2 * 512:(k2 + 1) * 512], self.ps[:, pc[k2], :], [self.ps_b[pc[k2]]], [cp_b])
                py = [self.bank(), self.bank()]
                for c in range(4):
                    self.PE(self.ps[:, py[c // 2], (c % 2) * 256:(c % 2) * 256 + 256], Qf[:, c * 128:(c + 1) * 128], c3[:, c, :],
                            [Qf_b, cp_b], [self.ps_b[py[c // 2]]])
                for k2 in range(2):
                    self.TT(y16[:, k2 * 512:(k2 + 1) * 512], self.ps[:, py[k2], :], z_[:, k2 * 512:(k2 + 1) * 512], ALU.add,
                            [self.ps_b[py[k2]], z_b], [y16_b])
                pb = self.bank()
                pbv = self.ps[:, pb, :].bitcast(BF16)
                for c in range(4):
                    self.TR(pbv[:, c * 128:(c + 1) * 128], y3[:, c, 128:256], ident_b, [y16_b] + CB, [self.ps_b[pb]])
                self.CPY("act", wT4, pbv[:, 0:512], [self.ps_b[pb]], [wT4_b])
                for c in range(4):
                    cs = slice(c * 128, (c + 1) * 128)
                    vn, vn_b = vns[c % 2]
                    pa = self.bank()
                    self.PE(self.ps[:, pa, 0:128], wT4[:, cs], S16, [wT4_b, S16_b], [self.ps_b[pa]])
                    self.PE(self.ps[:, PO, cs], qd[:, cs], S16, [qd_b, S16_b], [self.ps_b[PO]], start=True, stop=False)
                    self.TT(vn, y3[:, c, 0:128], self.ps[:, pa, 0:128], ALU.subtract, [y16_b, self.ps_b[pa]], [vn_b])
                    self.PE(self.ps[:, PO, cs], Aq[:, cs], vn, [Aq_b, vn_b], [self.ps_b[PO]], start=False, stop=True)
                    pd = self.bank()
                    self.PE(self.ps[:, pd, 0:128], kd[:, cs], vn, [kd_b, vn_b], [self.ps_b[pd]])
                    gl = Rex[:, c * 128 + 127:c * 128 + 128]
                    self.STT(S16, S32, gl, self.ps[:, pd, 0:128], ALU.mult, ALU.add, [S32_b, Rex_b, self.ps_b[pd]], [S16_b])
                    self.STT(S32, S32, gl, self.ps[:, pd, 0:128], ALU.mult, ALU.add, [S32_b, Rex_b, self.ps_b[pd]], [S32_b])
                for c in range(4):
                    cs = slice(c * 128, (c + 1) * 128)
                    self.ACT(df[:, cs], self.ps[:, PO, cs], AF.Square, [self.ps_b[PO]], [df_b, sc4_b], accum=sc4[:, 4 + c:5 + c])
                self.ACT(sc4[:, 8:12], sc4[:, 4:8], AF.Ln, [sc4_b] + CF, [sc4_b], scale=1.0 / 128.0, bias=cf[:, 896:897])
                self.ACT(sc4[:, 8:12], sc4[:, 8:12], AF.Exp, [sc4_b], [sc4_b], scale=-0.5)
                self.TT(v4(nRb), v4(self.ps[:, PO, :]), bc(sc4[:, 8:12]), ALU.mult, [self.ps_b[PO], sc4_b], [nRb_b])
                self.TT(v4(on16), v4(nRb), bm(sm[:, 128:256]), ALU.mult, [nRb_b, sm_b], [on16_b])
                pb = self.bank()
                pbv = self.ps[:, pb, :].bitcast(BF16)
                for c in range(4):
                    cs = slice(c * 128, (c + 1) * 128)
                    self.TR(pbv[:, cs], on16[:, cs], ident_b, [on16_b] + CB, [self.ps_b[pb]])
                self.TT(oT[:, ts], pbv[:, 0:512], sgate[:, ts], ALU.mult, [self.ps_b[pb], sgate_b], [oT_b])
            S.op("sp", lambda e, h=h, oT=oT: e.dma_start(out=self.odn_d[h], in_=oT), [oT_b], [self.odn_b[h]], dma=True)

    def att_branch(self, l):
        S = self.S
        S.barrier()
        self.tmps = {}
        self.aoff = 0
        hT = self.hT
        cb, cf = self.cb, self.cf
        CB, CF = [self.cb_b], [self.cf_b]
        sm, sm_b = self.tmp("small", 512, F32)
        sm_b = self.sm_b
        win = self.w_in[l].rearrange("(kc p) n -> p kc n", p=128)
        attn, attn_b = self.tmp("attn", 4 * T)
        self.attn, self.attn_b = attn, attn_b
        mrg, mrg_b = self.tmp("mrg", 8 * T)
        self.mrg, self.mrg_b = mrg, mrg_b
        acc, acc_b = self.tmp("attacc", 2 * T, F32)
        acc3 = acc.rearrange("p (h t) -> p h t", t=T)
        attn3 = attn.rearrange("p (h t) -> p h t", t=T)
        BD_b = cb[:, 768:896]
        msk = cb[:, 640:768]
        U_b = cb[:, 128:256]
        self.aoff = self.AR
        qn, qn_b = self.tmp("qn", T)
        kn, kn_b = self.tmp("kn", T)
        z16, z16_b = self.tmp("z16", 512)
        sq, sq_b = self.tmp("asq", 512)
        rs, rs_b = self.tmp("ars", 512, F32)
        vt, vt_b = self.tmp("vt", 16 * 2 * 66)
        vt4 = vt.rearrange("p (b h d) -> p b h d", h=2, d=66)
        pt = [self.tmp("pt%d" % i, 256) for i in range(2)]
        S.op("dve", lambda e: e.memset(vt, 1.0), [], [vt_b])
        groups = ((128, 1), (512, 4), (2048, 16))
        pti = 0
        for c in (0, 2, 4, 1, 3, 5):
            g = c // 2
            d = groups[g][1]
            nblk = (T // d) // 128
            w1 = self.next_w()
            w2 = self.next_w()
            wb1, wb2 = self.wbuf[w1], self.wbuf[w2]
            S.op("pool", lambda e, wb1=wb1, c=c: [e.dma_start(out=wb1[:, :, 0:128], in_=win[:, :, c * 128:(c + 1) * 128]),
                                                  e.dma_start(out=wb1[:, :, 128:256], in_=win[:, :, 768 + c * 128:768 + (c + 1) * 128])],
                 [], [self.w_b[w1]], dma=True, ndma=2)
            S.op("pool", lambda e, wb2=wb2, c=c: e.dma_start(out=wb2[:, :, 0:128], in_=win[:, :, 1536 + c * 128:1536 + (c + 1) * 128]),
                 [], [self.w_b[w2]], dma=True)
            for qi, (dst, dst_b) in enumerate(((qn, qn_b), (kn, kn_b))):
                for t in range(NT):
                    sl = slice(t * 512, (t + 1) * 512)
                    pb = self.bank()
                    for kc in range(KC):
                        self.PE(self.ps[:, pb, :], wb1[:, kc, qi * 128:(qi + 1) * 128], hT[:, kc, sl],
                                [self.w_b[w1], self.h_b[kc]], [self.ps_b[pb]], start=(kc == 0), stop=(kc == KC - 1))
                    self.ACT(sq, self.ps[:, pb, :], AF.Square, [self.ps_b[pb]], [sq_b])
                    self.CPY("dve", z16, self.ps[:, pb, :], [self.ps_b[pb]], [z16_b])
                    p2 = self.bank()
                    self.PE(self.ps[:, p2, :], BD_b, sq, CB + [sq_b], [self.ps_b[p2]])
                    if qi == 0:
                        self.ACT(rs, self.ps[:, p2, :], AF.Ln, [self.ps_b[p2]] + CF, [rs_b], scale=1.0, bias=cf[:, 899:900])
                    else:
                        self.ACT(rs, self.ps[:, p2, :], AF.Ln, [self.ps_b[p2]] + CF, [rs_b], scale=1.0 / 64.0, bias=cf[:, 896:897])
                    self.ACT(rs, rs, AF.Exp, [rs_b], [rs_b], scale=-0.5)
                    self.STT(dst[:, sl], z16, sm[:, 96 + qi:97 + qi], rs, ALU.mult, ALU.mult, [z16_b, sm_b, rs_b], [dst_b])
            blocks = [(r, nb) for r in range(d) for nb in range(nblk)]
            for bi, (r, nb) in enumerate(blocks):
                st = nb * 128 * d + r
                tsl = slice(st, st + 127 * d + 1, d)
                pb = self.bank()
                for kc in range(KC):
                    self.PE(self.ps[:, pb, 0:128], hT[:, kc, tsl], wb2[:, kc, 0:128], [self.h_b[kc], self.w_b[w2]], [self.ps_b[pb]],
                            start=(kc == 0), stop=(kc == KC - 1))
                self.CPY("act", vt4[:, bi, :, 0:64], self.ps[:, pb, 0:128].rearrange("p (h d) -> p h d", d=64), [self.ps_b[pb]], [vt_b])
            for hh2 in range(2):
                hh = (c % 2) * 2 + hh2
                ps_ = slice(hh2 * 64, hh2 * 64 + 64)
                for bi, (r, nb) in enumerate(blocks):
                    st = nb * 128 * d + r
                    qsl = slice(st, st + 127 * d + 1, d)
                    pS = self.bank()
                    has_prev = nb > 0
                    if has_prev:
                        sp_ = (nb - 1) * 128 * d + r
                        ksl = slice(sp_, sp_ + 127 * d + 1, d)
                        self.PE(self.ps[:, pS, 0:128], kn[ps_, ksl], qn[ps_, qsl], [kn_b, qn_b], [self.ps_b[pS]])
                    self.PE(self.ps[:, pS, 128:256], kn[ps_, qsl], qn[ps_, qsl], [kn_b, qn_b], [self.ps_b[pS]])
                    P, P_b = pt[pti % 2]
                    pti += 1
                    lo = 0 if has_prev else 128
                    self.ACT(P[:, lo:256], self.ps[:, pS, lo:256], AF.Exp, [self.ps_b[pS]], [P_b])
                    if has_prev:
                        self.TT(P[:, 0:128], P[:, 0:128], msk, ALU.mult, [P_b] + CB, [P_b])
                    self.TT(P[:, 128:256], P[:, 128:256], U_b, ALU.mult, [P_b] + CB, [P_b])
                    pO = self.bank()
                    if has_prev:
                        self.PE(self.ps[0:65, pO, 0:128], vt4[:, bi - 1, hh2, 0:65], P[:, 0:128], [vt_b, P_b], [self.ps_b[pO]], start=True, stop=False)
                    self.PE(self.ps[0:65, pO, 0:128], vt4[:, bi, hh2, 0:65], P[:, 128:256], [vt_b, P_b], [self.ps_b[pO]],
                            start=(not has_prev), stop=True)
                    if g == 0:
                        self.CPY("dve", acc3[0:65, hh2, qsl], self.ps[0:65, pO, 0:128], [self.ps_b[pO]], [acc_b])
                    else:
                        self.TT(acc3[0:65, hh2, qsl], self.ps[0:65, pO, 0:128], acc3[0:65, hh2, qsl], ALU.add, [self.ps_b[pO], acc_b], [acc_b])
            if g == 2:
                self.ACT(acc3[64:65, :, :], acc3[64:65, :, :], AF.Ln, [acc_b], [acc_b])
                self.ACT(acc3[64:65, :, :], acc3[64:65, :, :], AF.Exp, [acc_b], [acc_b], scale=-1.0)
                for hh2 in range(2):
                    hh = (c % 2) * 2 + hh2
                    for t in range(NT):
                        sl = slice(t * 512, (t + 1) * 512)
                        pb = self.bank()
                        self.PE(self.ps[0:64, pb, :], cf[64:65, 256:320], acc3[64:65, hh2, sl], CF + [acc_b], [self.ps_b[pb]])
                        self.TT(attn3[0:64, hh, sl], acc3[0:64, hh2, sl], self.ps[0:64, pb, :], ALU.mult, [acc_b, self.ps_b[pb]], [attn_b])

    def mix_final(self, l):
        S = self.S
        S.barrier(engines=("pe", "act", "dve", "sp", "pool"))
        hT, xT = self.hT, self.xT
        cb, cf = self.cb, self.cf
        attn3 = self.attn.rearrange("p (h t) -> p h t", t=T)
        mrg3 = self.mrg.rearrange("p (c t) -> p c t", t=T)
        mrg_b = self.mrg_b
        odn_all = self.tmps["attacc"][0].bitcast(BF16)
        odn_t = [odn_all[:, i * 4096:(i + 1) * 4096].rearrange("p (h t) -> p h t", t=512) for i in range(2)]
        odn_tb = S.bufs("odn_re", 2)
        odn_src = self.odn_d.rearrange("h p t -> p h t")
        oi = 0
        win = self.w_in[l].rearrange("(kc p) n -> p kc n", p=128)
        wpd = self.w_pd[l].rearrange("(kc p) n -> p kc n", p=128)
        wo = self.w_o[l].rearrange("(kc p) n -> p kc n", p=128)
        wpa_d = self.w_pa[l].rearrange("(hh p) n -> p hh n", p=64)
        self.aoff = self.AR
        self.tmps.pop("qn", None)
        wpa, wpa_b = self.tmp("wpa", 4 * D)
        wpa3 = wpa.rearrange("p (h n) -> p h n", n=D)
        S.op("pool", lambda e: e.dma_start(out=wpa3[0:64, :, :], in_=wpa_d), [], [wpa_b], dma=True)
        ge = [self.tmp("ge%d" % i, 512, F32) for i in range(2)]
        m1, m1_b = self.tmp("m1", 512, F32)
        mc = self.modc
        for dc in range(KC):
            w1 = self.next_w()
            w2 = self.next_w()
            wb1, wb2 = self.wbuf[w1], self.wbuf[w2]
            S.op("pool", lambda e, wb1=wb1, dc=dc: [e.dma_start(out=wb1[:, :, 0:128], in_=win[:, :, OFF_MERGE + dc * 128:OFF_MERGE + (dc + 1) * 128]),
                                                   e.dma_start(out=wb1[:, :, 128:256], in_=win[:, :, OFF_MERGE + D + dc * 128:OFF_MERGE + D + (dc + 1) * 128])],
                 [], [self.w_b[w1]], dma=True, ndma=2)
            S.op("pool", lambda e, wb2=wb2, dc=dc: e.dma_start(out=wb2[:, :, 0:128], in_=wpd[:, :, dc * 128:(dc + 1) * 128]),
                 [], [self.w_b[w2]], dma=True)
            for t in range(NT):
                sl = slice(t * 512, (t + 1) * 512)
                od3, odn_b = odn_t[oi % 2], odn_tb[oi % 2]
                oi += 1
                S.op("sp", lambda e, od3=od3, sl=sl: e.dma_start(out=od3, in_=odn_src[:, :, sl]), self.odn_b, [odn_b], dma=True)
                pga, pgd, pya, pyd = self.bank(), self.bank(), self.bank(), self.bank()
                for gi, pg in enumerate((pga, pgd)):
                    for kc in range(KC):
                        self.PE(self.ps[:, pg, :], wb1[:, kc, gi * 128:(gi + 1) * 128], hT[:, kc, sl], [self.w_b[w1], self.h_b[kc]],
                                [self.ps_b[pg]], start=(kc == 0), stop=(kc == KC - 1))
                for hh in range(4):
                    self.PE(self.ps[:, pya, :], wpa3[0:64, hh, dc * 128:(dc + 1) * 128], attn3[0:64, hh, sl], [wpa_b, self.attn_b],
                            [self.ps_b[pya]], start=(hh == 0), stop=(hh == 3))
                for h8 in range(8):
                    self.PE(self.ps[:, pyd, :], wb2[:, h8, 0:128], od3[:, h8, :], [self.w_b[w2], odn_b], [self.ps_b[pyd]],
                            start=(h8 == 0), stop=(h8 == 7))
                for gi, (pg, py) in enumerate(((pga, pya), (pgd, pyd))):
                    gt_, gt_b = ge[gi]
                    self.ACT(gt_, self.ps[:, pg, :], AF.Sigmoid, [self.ps_b[pg]], [gt_b])
                    self.TT(gt_, gt_, self.ps[:, py, :], ALU.mult, [gt_b, self.ps_b[py]], [gt_b])
                self.TT(mrg3[:, dc, sl], ge[0][0], ge[1][0], ALU.add, [ge[0][1], ge[1][1]], [mrg_b])
        for half in range(4):
            w1 = self.next_w()
            wb1 = self.wbuf[w1]
            S.op("pool", lambda e, wb1=wb1, half=half: e.dma_start(out=wb1[:, :, 0:256], in_=wo[:, :, half * 256:(half + 1) * 256]),
                 [], [self.w_b[w1]], dma=True)
            for j2 in range(2):
                dc = half * 2 + j2
                for t in range(NT):
                    sl = slice(t * 512, (t + 1) * 512)
                    pb = self.bank()
                    for kc in range(KC):
                        self.PE(self.ps[:, pb, :], wb1[:, kc, j2 * 128:(j2 + 1) * 128], mrg3[:, kc, sl], [self.w_b[w1], mrg_b],
                                [self.ps_b[pb]], start=(kc == 0), stop=(kc == KC - 1))
                    self.STT(xT[:, dc, sl], self.ps[:, pb, :], mc[:, 24 + 16 + dc:24 + 16 + dc + 1], xT[:, dc, sl], ALU.mult, ALU.add,
                             [self.ps_b[pb], self.modc_b, self.x_b[dc]], [self.x_b[dc]])

def make_consts():
    c = np.zeros((128, 1024), np.float32)
    c[:, 0:128] = np.eye(128, dtype=np.float32)
    jj = np.arange(128)[:, None]
    ii = np.arange(128)[None, :]
    c[:, 128:256] = (jj <= ii).astype(np.float32)
    c[:, 256:384] = 1.0
    c[:, 384:512] = (jj < ii).astype(np.float32)
    c[:, 512:640] = (jj > ii).astype(np.float32)
    c[:, 640:768] = (jj >= ii).astype(np.float32)
    c[:, 768:896] = ((jj < 64) == (ii < 64)).astype(np.float32)
    c[:, 512:640] = np.where((jj > ii) & ((jj < 64) == (ii < 64)), 0.0, 30000.0)
    c[:, 896] = EPS
    c[:, 897] = 128.0 * EPS
    c[:, 898] = 1.0
    c[:, 899] = 64.0 * EPS
    return c


def make_small(inputs):
    L = DEPTH
    sm = np.zeros((L, 128, 512), np.float32)
    cw = np.asarray(inputs["conv_w"], np.float32)
    sm[:, :, 0:96] = cw.reshape(L, 4, 24, 128).transpose(0, 3, 2, 1).reshape(L, 128, 96)
    p = np.arange(128) % 64
    sm[:, :, 96] = np.asarray(inputs["q_norm"], np.float32)[:, p]
    sm[:, :, 97] = np.asarray(inputs["k_norm"], np.float32)[:, p]
    sm[:, :, 104:112] = np.asarray(inputs["a_log"], np.float32)[:, None, :]
    sm[:, :, 112:120] = np.asarray(inputs["dt_bias"], np.float32)[:, None, :]
    sm[:, :, 128:256] = np.asarray(inputs["dn_norm"], np.float32)[:, None, :]
    sm[:, :, 256:384] = np.tile(np.asarray(inputs["dt_bias"], np.float32), (1, 16))[:, None, :]
    sm[:, :, 384:512] = np.tile(np.asarray(inputs["a_log"], np.float32), (1, 16))[:, None, :]
    return sm


def prep_inputs(inputs, b):
    f = lambda a: np.ascontiguousarray(a, dtype=np.float32)
    m = {}
    m["x"] = f(inputs["x"][b])
    m["c"] = f(inputs["c"][b].reshape(KC, 128).T)
    m["ada_w"] = f(inputs["ada_w"])
    m["ada_b"] = f(inputs["ada_b"].reshape(DEPTH, 72, 128).transpose(0, 2, 1))
    nr = np.stack([inputs["norm_ff1"], inputs["norm_mix"], inputs["norm_ff2"]], axis=1)
    m["norms"] = f(nr.reshape(DEPTH, 3 * KC, 128).transpose(0, 2, 1))
    m["ffn1_w_up"] = f(inputs["ffn1_w_up"])
    m["ffn2_w_up"] = f(inputs["ffn2_w_up"])
    m["ffn1_w_down"] = f(inputs["ffn1_w_down"])
    m["ffn2_w_down"] = f(inputs["ffn2_w_down"])
    m["w_in"] = f(inputs["w_in"])
    m["w_proj_att"] = f(inputs["w_proj_att"])
    m["w_proj_dn"] = f(inputs["w_proj_dn"])
    m["w_out"] = f(inputs["w_out"])
    m["consts"] = make_consts()
    m["small"] = make_small(inputs)
    return m


_PROG = {}


def run(inputs, n_layers=DEPTH, stages=("ffn1", "mix", "ffn2"), cores=8, trace=False):
    key = (n_layers, tuple(stages))
    if key not in _PROG:
        _PROG[key] = Prog(n_layers, stages)
    p = _PROG[key]
    shared = prep_inputs(inputs, 0)
    in_maps = []
    for b in range(cores):
        m = dict(shared)
        m["x"] = np.ascontiguousarray(inputs["x"][b], dtype=np.float32)
        m["c"] = np.ascontiguousarray(inputs["c"][b].reshape(KC, 128).T, dtype=np.float32)
        in_maps.append(m)
    res = run_bass_kernel_spmd(p.nc, in_maps, core_ids=list(range(cores)), trace=trace)
    out = np.stack([r["out"] for r in res.results], axis=0)
    return out, res


def kernel(**inputs):
    out, _ = run(inputs)
    return out.astype(np.float32)
```

```python
import numpy as np
import concourse.bass as bass
import concourse.mybir as mybir
from concourse.bass_utils import run_bass_kernel_spmd

F32 = mybir.dt.float32
BF16 = mybir.dt.bfloat16
AF = mybir.ActivationFunctionType
ALU = mybir.AluOpType

D = 1024
T = 2048
DEPTH = 4
DFF = 2816
NFF = DFF // 128
KC = D // 128
NT = T // 512
NB = T // 128
N_IN = 8464
OFF_DN_QKV = 2304
OFF_DN_GATE = 5376
OFF_DN_A = 6400
OFF_DN_B = 6408
OFF_MERGE = 6416
EPS = 1e-6
SEM_LIMIT = 30000
STRICT_SAME_ENGINE = False


class Buf:
    __slots__ = ("name", "w", "rs", "dsem", "excl")

    def __init__(self, name, excl=False):
        self.name = name
        self.w = None
        self.rs = []
        self.dsem = None
        self.excl = excl


class Op:
    __slots__ = ("eng", "fn", "deps", "dma", "ndma", "sem", "val", "signals", "idx")


class DmaSem:
    def __init__(self, sched):
        self.s = sched
        self.sem = sched.new_sem()
        self.count = 0

    def take(self, n):
        if self.count + 16 * n > SEM_LIMIT:
            self.sem = self.s.new_sem()
            self.count = 0
        self.count += 16 * n
        return self.sem, self.count


class Sched:
    ENGS = ("pe", "act", "dve", "pool", "sp")

    def __init__(self, nc):
        self.nc = nc
        self.q = {e: [] for e in self.ENGS}
        self.nsem = 0
        self.dma_ops = []
        self.all_bufs = []

    def new_sem(self):
        self.nsem += 1
        return self.nc.alloc_semaphore("s%d" % self.nsem)

    def buf(self, name):
        b = Buf(name)
        return b

    def bufs(self, name, n, excl=False):
        return [Buf("%s%d" % (name, i), excl) for i in range(n)]

    def op(self, eng, fn, reads=(), writes=(), dma=False, ndma=1):
        o = Op()
        o.eng = eng
        o.fn = fn
        o.dma = dma
        o.ndma = ndma
        o.signals = dma
        o.sem = None
        o.val = 0
        deps = []
        for b in reads:
            if b.w is not None:
                deps.append((b.w, True))
            if b.excl:
                for r in b.rs:
                    if r.eng != eng:
                        deps.append((r, False))
        for b in writes:
            if b.w is not None:
                deps.append((b.w, False))
            for r in b.rs:
                deps.append((r, False))
        fd = {}
        for d, raw in deps:
            if d is o:
                continue
            if not d.dma and not dma and d.eng == eng:
                if eng == "pe":
                    continue
                if not raw and not STRICT_SAME_ENGINE:
                    continue
            if d.dma and dma and not raw:
                pass
            fd[id(d)] = d
        o.deps = list(fd.values())
        for b in reads:
            b.rs.append(o)
        for b in writes:
            b.w = o
            b.rs = []
        if dma:
            wb = writes[0]
            if wb.dsem is None:
                wb.dsem = DmaSem(self)
            o.sem, o.val = wb.dsem.take(ndma)
            self.dma_ops.append(o)
        o.idx = len(self.q[eng])
        self.q[eng].append(o)
        return o

    def barrier(self, engines=("pe", "act", "dve", "sp")):
        lasts = []
        for e in ("pe", "act", "dve", "pool"):
            for o in reversed(self.q[e]):
                if not o.dma and o.fn is not None:
                    lasts.append(o)
                    break
        dm = list(self.dma_ops)
        self.dma_ops = []
        for e in engines:
            o = Op()
            o.eng = e
            o.fn = None
            o.dma = False
            o.ndma = 0
            o.signals = False
            o.sem = None
            o.val = 0
            o.deps = [d for d in lasts if d.eng != e or e != "pe"] + dm
            o.idx = len(self.q[e])
            self.q[e].append(o)

    def emit(self):
        nc = self.nc
        for e in self.ENGS:
            for o in self.q[e]:
                for d in o.deps:
                    if not d.dma:
                        d.signals = True
        for e in self.ENGS:
            sem = None
            cnt = 0
            for o in self.q[e]:
                if o.dma or not o.signals:
                    continue
                if sem is None or cnt >= SEM_LIMIT:
                    sem = self.new_sem()
                    cnt = 0
                cnt += 1
                o.sem = sem
                o.val = cnt
        sched = self

        def run(e, eng):
            waited = {}
            for o in sched.q[e]:
                for d in o.deps:
                    key = id(d.sem)
                    if waited.get(key, 0) >= d.val:
                        continue
                    waited[key] = d.val
                    eng.wait_ge(d.sem, d.val)
                if o.fn is None:
                    continue
                r = o.fn(eng)
                if o.dma:
                    if not isinstance(r, (list, tuple)):
                        r = [r]
                    assert len(r) == o.ndma, (len(r), o.ndma)
                    for ins in r:
                        ins.then_inc(o.sem, 16)
                elif o.signals:
                    r.then_inc(o.sem, 1)

        with nc.Block() as block:
            @block.tensor
            def _(eng):
                run("pe", eng)

            @block.scalar
            def _(eng):
                run("act", eng)

            @block.vector
            def _(eng):
                run("dve", eng)

            @block.gpsimd
            def _(eng):
                run("pool", eng)

            @block.sync
            def _(eng):
                run("sp", eng)


class Prog:
    def __init__(self, n_layers=DEPTH, stages=("ffn1", "mix", "ffn2"), dbg=None):
        self.n_layers = n_layers
        self.stages = stages
        self.dbg = dbg
        nc = bass.Bass("TRN2", target_bir_lowering=False)
        self.nc = nc
        self.S = Sched(nc)
        self.decl_io()
        self.alloc()
        self.build()
        self.S.emit()

    def decl_io(self):
        nc = self.nc
        L = DEPTH

        def inp(name, shape):
            return nc.dram_tensor(name, list(shape), F32, kind="ExternalInput").ap()

        self.x_d = inp("x", (T, D))
        self.c_d = inp("c", (128, KC))
        self.ada_w = inp("ada_w", (L, D, 9 * D))
        self.ada_b = inp("ada_b", (L, 128, 72))
        self.norms = inp("norms", (L, 128, 3 * KC))
        self.w_up = [inp("ffn1_w_up", (L, D, 2 * DFF)), inp("ffn2_w_up", (L, D, 2 * DFF))]
        self.w_dn = [inp("ffn1_w_down", (L, DFF, D)), inp("ffn2_w_down", (L, DFF, D))]
        self.w_in = inp("w_in", (L, D, N_IN))
        self.cst = inp("consts", (128, 1024))
        self.small_d = inp("small", (L, 128, 512))
        self.w_pa = inp("w_proj_att", (L, 256, D))
        self.w_pd = inp("w_proj_dn", (L, D, D))
        self.w_o = inp("w_out", (L, D, D))
        self.odn_d = nc.dram_tensor("odn_scr", [8, 128, T], BF16, kind="Internal").ap()
        self.odn_b = self.S.bufs("odn_d", 8)
        self.att_d = nc.dram_tensor("att_scr", [4, 64, T], BF16, kind="Internal").ap()
        self.att_b = self.S.bufs("att_d", 4)
        self.out_d = nc.dram_tensor("out", [T, D], F32, kind="ExternalOutput").ap()

    def alloc(self):
        nc = self.nc
        S = self.S
        self.xT = nc.alloc_sbuf_tensor("xT", [128, KC, T], F32)
        self.x_b = S.bufs("x", KC)
        self.hT = nc.alloc_sbuf_tensor("hT", [128, KC, T], BF16)
        self.h_b = S.bufs("h", KC)
        self.NW = 4
        self.wbuf = [nc.alloc_sbuf_tensor("wbuf%d" % i, [128, KC, 256], BF16) for i in range(self.NW)]
        self.w_b = S.bufs("w", self.NW)
        self.wi = 0
        self.wd = nc.alloc_sbuf_tensor("wd", [128, 11, D], BF16)
        self.wd_b = S.bufs("wd", 11)
        self.cf = nc.alloc_sbuf_tensor("cf", [128, 1024], F32)
        self.cf_b = S.buf("cf")
        self.cb = nc.alloc_sbuf_tensor("cb", [128, 1024], BF16)
        self.cb_b = S.buf("cb")
        self.modv = nc.alloc_sbuf_tensor("modv", [128, 72], F32)
        self.modv_b = S.buf("modv")
        self.adab = nc.alloc_sbuf_tensor("adab", [128, 72], F32)
        self.adab_b = S.buf("adab")
        self.nrm = nc.alloc_sbuf_tensor("nrm", [128, 3 * KC], F32)
        self.nrm_b = S.buf("nrm")
        self.modc = nc.alloc_sbuf_tensor("modc", [128, 9 * KC], F32)
        self.modc_b = S.buf("modc")
        self.cact = nc.alloc_sbuf_tensor("cact", [128, KC], BF16)
        self.cact_b = S.buf("cact")
        self.c32 = nc.alloc_sbuf_tensor("c32", [128, 2 * KC], F32)
        self.c32_b = S.buf("c32")
        self.AR = getattr(Prog, 'AR_OVERRIDE', 33 * 1024)
        self.ar = nc.alloc_sbuf_tensor("arena", [128, self.AR], BF16)
        self.ps = nc.alloc_psum_tensor("ps", [128, 8, 512], F32)
        self.ps_b = S.bufs("ps", 8, excl=True)
        self.pi = 0

    def bank(self):
        i = self.pi
        self.pi = (self.pi + 1) % 7
        return i

    def next_w(self):
        i = self.wi
        self.wi = (self.wi + 1) % self.NW
        return i

    def ar_bf(self, off, n):
        return self.ar[:, off:off + n]

    def ar_f32(self, off, n):
        return self.ar[:, off:off + 2 * n].bitcast(F32)

    def build(self):
        S = self.S
        self.load_consts()
        self.load_x()
        for l in range(self.n_layers):
            self.mod_layer(l)
            if "ffn1" in self.stages:
                self.norm_mod(l, 0)
                self.ffn(l, 0)
            if "mix" in self.stages:
                self.norm_mod(l, 1)
                self.mixer(l)
            if "ffn2" in self.stages:
                self.norm_mod(l, 2)
                self.ffn(l, 1)
        self.store_x()

    def load_consts(self):
        S = self.S
        cf, cb = self.cf, self.cb
        S.op("sp", lambda e: e.dma_start(out=cf[:], in_=self.cst[:, :]), [], [self.cf_b], dma=True)
        S.op("dve", lambda e: e.tensor_copy(out=cb[:], in_=cf[:]), [self.cf_b], [self.cb_b])
        c32 = self.c32
        S.op("sp", lambda e: e.dma_start(out=c32[:, 0:KC], in_=self.c_d[:, :]), [], [self.c32_b], dma=True)
        S.op("act", lambda e: e.activation(out=c32[:, KC:2 * KC], in_=c32[:, 0:KC], func=AF.Exp, scale=-1.0),
             [self.c32_b], [self.c32_b])
        S.op("dve", lambda e: e.tensor_scalar_add(out=c32[:, KC:2 * KC], in0=c32[:, KC:2 * KC], scalar1=1.0),
             [self.c32_b], [self.c32_b])
        S.op("dve", lambda e: e.reciprocal(out=c32[:, KC:2 * KC], in_=c32[:, KC:2 * KC]),
             [self.c32_b], [self.c32_b])
        S.op("dve", lambda e: e.tensor_tensor(out=self.cact[:], in0=c32[:, 0:KC], in1=c32[:, KC:2 * KC], op=ALU.mult),
             [self.c32_b], [self.cact_b])

    def ident_f(self):
        return self.cf[:, 0:128]

    def load_x(self):
        S = self.S
        stg = [self.ar_f32(0, D), self.ar_f32(2 * D, D)]
        stg_b = S.bufs("xstg", 2)
        xT = self.xT
        for b in range(NB):
            s = b % 2
            S.op("sp", lambda e, b=b, s=s: e.dma_start(out=stg[s], in_=self.x_d[b * 128:(b + 1) * 128, :]),
                 [], [stg_b[s]], dma=True)
            for half in range(2):
                pb = self.bank()
                for q in range(4):
                    c = half * 4 + q
                    S.op("pe", lambda e, c=c, s=s, pb=pb, q=q: e.transpose(
                        self.ps[:, pb, q * 128:(q + 1) * 128], stg[s][:, c * 128:(c + 1) * 128], self.ident_f()),
                        [stg_b[s], self.cf_b], [self.ps_b[pb]])
                psv = self.ps[:, pb, :].rearrange("p (q i) -> p q i", i=128)
                dst = xT[:, half * 4:(half + 1) * 4, b * 128:(b + 1) * 128]
                xb = self.x_b[half * 4:(half + 1) * 4]
                if half == 0:
                    S.op("dve", lambda e, psv=psv, dst=dst: e.tensor_copy(out=dst, in_=psv), [self.ps_b[pb]], xb)
                else:
                    S.op("act", lambda e, psv=psv, dst=dst: e.copy(out=dst, in_=psv), [self.ps_b[pb]], xb)
        S.barrier()

    def store_x(self):
        S = self.S
        S.barrier()
        stg = [self.ar_f32(0, D), self.ar_f32(2 * D, D)]
        stg_b = [S.bufs("ostg%d_" % i, 2) for i in range(2)]
        xT = self.xT
        outs = []
        for b in range(NB):
            s = b % 2
            for half in range(2):
                pb = self.bank()
                for q in range(4):
                    c = half * 4 + q
                    S.op("pe", lambda e, c=c, b=b, pb=pb, q=q: e.transpose(
                        self.ps[:, pb, q * 128:(q + 1) * 128], xT[:, c, b * 128:(b + 1) * 128], self.ident_f()),
                        [self.x_b[c], self.cf_b], [self.ps_b[pb]])
                if half == 0:
                    S.op("dve", lambda e, s=s, pb=pb: e.tensor_copy(
                        out=stg[s][:, 0:512], in_=self.ps[:, pb, :]), [self.ps_b[pb]], [stg_b[s][0]])
                else:
                    S.op("act", lambda e, s=s, pb=pb: e.copy(
                        out=stg[s][:, 512:1024], in_=self.ps[:, pb, :]), [self.ps_b[pb]], [stg_b[s][1]])
            ob = S.buf("outd%d" % b)
            o = S.op("sp", lambda e, b=b, s=s: e.dma_start(out=self.out_d[b * 128:(b + 1) * 128, :], in_=stg[s]),
                     stg_b[s], [ob], dma=True)
            outs.append(ob)
        S.op("sp", None, outs, [])

    def mod_layer(self, l):
        S = self.S
        S.op("sp", lambda e: e.dma_start(out=self.adab[:], in_=self.ada_b[l]), [], [self.adab_b], dma=True)
        S.op("sp", lambda e: e.dma_start(out=self.nrm[:], in_=self.norms[l]), [], [self.nrm_b], dma=True)
        aw = self.ada_w[l].rearrange("(kc p) n -> p kc n", p=128)
        pb = self.bank()
        for blk in range(9 * D // 256):
            wi = self.next_w()
            wb = self.wbuf[wi]
            S.op("pool", lambda e, blk=blk, wb=wb: e.dma_start(out=wb[:], in_=aw[:, :, blk * 256:(blk + 1) * 256]),
                 [], [self.w_b[wi]], dma=True)
            for j in range(2):
                col = blk * 2 + j
                for kc in range(KC):
                    S.op("pe", lambda e, wb=wb, j=j, kc=kc, col=col: e.matmul(
                        self.ps[:, pb, col:col + 1], wb[:, kc, j * 128:(j + 1) * 128], self.cact[:, kc:kc + 1],
                        start=(kc == 0), stop=(kc == KC - 1)),
                        [self.w_b[wi], self.cact_b], [self.ps_b[pb]])
        S.op("dve", lambda e: e.tensor_tensor(out=self.modv[:], in0=self.ps[:, pb, 0:72], in1=self.adab[:], op=ALU.add),
             [self.ps_b[pb], self.adab_b], [self.modv_b])
        mv, mc, nr = self.modv, self.modc, self.nrm
        for j in range(3):
            S.op("dve", lambda e, j=j: e.scalar_tensor_tensor(
                out=mc[:, j * 24:j * 24 + 8], in0=mv[:, (3 * j + 1) * 8:(3 * j + 2) * 8], scalar=1.0,
                in1=nr[:, j * 8:(j + 1) * 8], op0=ALU.add, op1=ALU.mult),
                [self.modv_b, self.nrm_b], [self.modc_b])
            S.op("dve", lambda e, j=j: e.tensor_copy(out=mc[:, j * 24 + 8:j * 24 + 16], in_=mv[:, (3 * j) * 8:(3 * j + 1) * 8]),
                 [self.modv_b], [self.modc_b])
            gsc = 1.0 if j == 1 else 0.5
            S.op("dve", lambda e, j=j, gsc=gsc: e.tensor_scalar(
                out=mc[:, j * 24 + 16:j * 24 + 24], in0=mv[:, (3 * j + 2) * 8:(3 * j + 3) * 8], scalar1=gsc, scalar2=None,
                op0=ALU.mult),
                [self.modv_b], [self.modc_b])

    def norm_mod(self, l, j):
        S = self.S
        S.barrier()
        xT, hT = self.xT, self.hT
        sq = [self.ar_bf(0, T), self.ar_bf(T, T)]
        sq_b = S.bufs("sq", 2)
        rstd = self.ar_f32(2 * T, T)
        rstd_b = S.buf("rstd")
        tmp = [self.ar_bf(6 * T, T), self.ar_bf(7 * T, T)]
        tmp_b = S.bufs("ntmp", 2)
        ones = self.cb[:, 256:384]
        pbs = [self.bank() for _ in range(4)]
        for kc in range(KC):
            s = kc % 2
            S.op("act", lambda e, kc=kc, s=s: e.activation(out=sq[s], in_=xT[:, kc, :], func=AF.Square),
                 [self.x_b[kc]], [sq_b[s]])
            for t in range(NT):
                S.op("pe", lambda e, kc=kc, s=s, t=t: e.matmul(
                    self.ps[:, pbs[t], :], ones, sq[s][:, t * 512:(t + 1) * 512], start=(kc == 0), stop=(kc == KC - 1)),
                    [sq_b[s], self.cb_b], [self.ps_b[pbs[t]]])
        for t in range(NT):
            S.op("act", lambda e, t=t: e.activation(out=rstd[:, t * 512:(t + 1) * 512], in_=self.ps[:, pbs[t], :],
                                                    func=AF.Ln, scale=1.0 / D, bias=self.cf[:, 896:897]),
                 [self.ps_b[pbs[t]], self.cf_b], [rstd_b])
        S.op("act", lambda e: e.activation(out=rstd, in_=rstd, func=AF.Exp, scale=-0.5), [rstd_b], [rstd_b])
        mc = self.modc
        for kc in range(KC):
            s = kc % 2
            S.op("dve", lambda e, kc=kc, s=s: e.tensor_tensor(out=tmp[s], in0=xT[:, kc, :], in1=rstd, op=ALU.mult),
                 [self.x_b[kc], rstd_b], [tmp_b[s]])
            S.op("act", lambda e, kc=kc, s=s: e.activation(
                out=hT[:, kc, :], in_=tmp[s], func=AF.Identity,
                scale=mc[:, j * 24 + kc:j * 24 + kc + 1], bias=mc[:, j * 24 + 8 + kc:j * 24 + 8 + kc + 1]),
                [tmp_b[s], self.modc_b], [self.h_b[kc]])

    def ffn(self, l, which):
        S = self.S
        S.barrier()
        j = 0 if which == 0 else 2
        xT, hT = self.xT, self.hT
        wu = self.w_up[which][l].rearrange("(kc p) n -> p kc n", p=128)
        wdn = self.w_dn[which][l]
        act = self.ar[:, 0:11 * T].rearrange("p (j t) -> p j t", t=T)
        act_b = S.bufs("act", 11)
        sg = [self.ar_f32(11 * T, 512), self.ar_f32(11 * T + 1024, 512)]
        sg_b = S.bufs("sg", 2)
        mc = self.modc
        sgi = 0
        for half in range(2):
            def load_wd():
                for jj in range(11):
                    r0 = (half * 11 + jj) * 128
                    S.op("pool", lambda e, jj=jj, r0=r0: e.dma_start(out=self.wd[:, jj, :], in_=wdn[r0:r0 + 128, :]),
                         [], [self.wd_b[jj]], dma=True)
            for blk in range(6):
                nj = 2 if blk < 5 else 1
                j0 = half * 11 + blk * 2
                wg_i = self.next_w()
                wu_i = self.next_w()
                wgb, wub = self.wbuf[wg_i], self.wbuf[wu_i]
                ncol = nj * 128
                S.op("pool", lambda e, wgb=wgb, j0=j0, ncol=ncol: e.dma_start(
                    out=wgb[:, :, 0:ncol], in_=wu[:, :, j0 * 128:j0 * 128 + ncol]), [], [self.w_b[wg_i]], dma=True)
                S.op("pool", lambda e, wub=wub, j0=j0, ncol=ncol: e.dma_start(
                    out=wub[:, :, 0:ncol], in_=wu[:, :, DFF + j0 * 128:DFF + j0 * 128 + ncol]), [], [self.w_b[wu_i]], dma=True)
                if blk == 2:
                    load_wd()
                for jj in range(nj):
                    ja = blk * 2 + jj
                    for t in range(NT):
                        pg, pu = self.bank(), self.bank()
                        for kc in range(KC):
                            S.op("pe", lambda e, kc=kc, jj=jj, t=t, pg=pg, wgb=wgb: e.matmul(
                                self.ps[:, pg, :], wgb[:, kc, jj * 128:(jj + 1) * 128], hT[:, kc, t * 512:(t + 1) * 512],
                                start=(kc == 0), stop=(kc == KC - 1)),
                                [self.w_b[wg_i], self.h_b[kc]], [self.ps_b[pg]])
                        for kc in range(KC):
                            S.op("pe", lambda e, kc=kc, jj=jj, t=t, pu=pu, wub=wub: e.matmul(
                                self.ps[:, pu, :], wub[:, kc, jj * 128:(jj + 1) * 128], hT[:, kc, t * 512:(t + 1) * 512],
                                start=(kc == 0), stop=(kc == KC - 1)),
                                [self.w_b[wu_i], self.h_b[kc]], [self.ps_b[pu]])
                        s = sgi % 2
                        sgi += 1
                        S.op("act", lambda e, s=s, pg=pg: e.activation(out=sg[s], in_=self.ps[:, pg, :], func=AF.Silu),
                             [self.ps_b[pg]], [sg_b[s]])
                        S.op("dve", lambda e, s=s, pu=pu, ja=ja, t=t: e.tensor_tensor(
                            out=act[:, ja, t * 512:(t + 1) * 512], in0=sg[s], in1=self.ps[:, pu, :], op=ALU.mult),
                            [sg_b[s], self.ps_b[pu]], [act_b[ja]])
            for t in range(NT):
                for dc in range(KC):
                    pb = self.bank()
                    for jj in range(11):
                        S.op("pe", lambda e, jj=jj, dc=dc, t=t, pb=pb: e.matmul(
                            self.ps[:, pb, :], self.wd[:, jj, dc * 128:(dc + 1) * 128], act[:, jj, t * 512:(t + 1) * 512],
                            start=(jj == 0), stop=(jj == 10)),
                            [self.wd_b[jj], act_b[jj]], [self.ps_b[pb]])
                    S.op("dve", lambda e, dc=dc, t=t, pb=pb: e.scalar_tensor_tensor(
                        out=xT[:, dc, t * 512:(t + 1) * 512], in0=self.ps[:, pb, :],
                        scalar=mc[:, j * 24 + 16 + dc:j * 24 + 16 + dc + 1],
                        in1=xT[:, dc, t * 512:(t + 1) * 512], op0=ALU.mult, op1=ALU.add),
                        [self.ps_b[pb], self.modc_b, self.x_b[dc]], [self.x_b[dc]])

    def PE(self, out, lhsT, rhs, r, w, start=True, stop=True):
        return self.S.op("pe", lambda e: e.matmul(out, lhsT, rhs, start=start, stop=stop), r, w)

    def TR(self, out, in_, ident, r, w):
        return self.S.op("pe", lambda e: e.transpose(out, in_, ident), r, w)

    def ACT(self, out, in_, func, r, w, scale=None, bias=None, accum=None):
        kw = {}
        if scale is not None:
            kw["scale"] = scale
        if bias is not None:
            kw["bias"] = bias
        if accum is not None:
            kw["accum_out"] = accum
        return self.S.op("act", lambda e: e.activation(out=out, in_=in_, func=func, **kw), r, w)

    def TT(self, out, in0, in1, op, r, w):
        return self.S.op("dve", lambda e: e.tensor_tensor(out=out, in0=in0, in1=in1, op=op), r, w)

    def TS(self, out, in0, s1, op0, r, w, s2=None, op1=None):
        if op1 is None:
            return self.S.op("dve", lambda e: e.tensor_scalar(out=out, in0=in0, scalar1=s1, scalar2=None, op0=op0), r, w)
        return self.S.op("dve", lambda e: e.tensor_scalar(out=out, in0=in0, scalar1=s1, scalar2=s2, op0=op0, op1=op1), r, w)

    def STT(self, out, in0, sc, in1, op0, op1, r, w):
        return self.S.op("dve", lambda e: e.scalar_tensor_tensor(out=out, in0=in0, scalar=sc, in1=in1, op0=op0, op1=op1), r, w)

    def CPY(self, eng, out, in_, r, w):
        if eng == "dve":
            return self.S.op("dve", lambda e: e.tensor_copy(out=out, in_=in_), r, w)
        return self.S.op("act", lambda e: e.copy(out=out, in_=in_), r, w)

    def tmp(self, name, n, dt=BF16):
        if name in self.tmps:
            return self.tmps[name]
        ne = n if dt == BF16 else 2 * n
        if self.aoff % 2:
            self.aoff += 1
        off = self.aoff
        self.aoff += ne
        assert self.aoff <= self.AR + 11 * D, ("arena overflow", name, self.aoff)
        if off + ne <= self.AR:
            base = self.ar[:, off:off + ne]
        else:
            if off < self.AR:
                off = self.AR
                self.aoff = off + ne
            o2 = off - self.AR
            base = self.wd[:].rearrange("p a b -> p (a b)")[:, o2:o2 + ne]
        ap = base if dt == BF16 else base.bitcast(F32)
        if not hasattr(self, "toff"):
            self.toff = {}
        self.toff[name] = (off, ne, dt == F32)
        b = self.S.buf(name)
        self.tmps[name] = (ap, b)
        return ap, b

    def mixer(self, l):
        S = self.S
        S.barrier(engines=("pe", "act", "dve", "sp", "pool"))
        self.tmps = {}
        self.aoff = 0
        self.mix_setup(l)
        if "nodn" not in self.stages:
            self.dn_branch(l)
        if "noatt" not in self.stages:
            self.att_branch(l)
        if "nofinal" not in self.stages:
            self.mix_final(l)
        S.barrier(engines=("pe", "act", "dve", "sp", "pool"))

    def mix_setup(self, l):
        S = self.S
        sm, sm_b = self.tmp("small", 512, F32)
        self.sm, self.sm_b = sm, sm_b
        S.op("sp", lambda e: e.dma_start(out=sm, in_=self.small_d[l]), [], [sm_b], dma=True)

    def dn_branch(self, l):
        S = self.S
        hT = self.hT
        sm, sm_b = self.sm, self.sm_b
        cb, cf = self.cb, self.cf
        ident_b = cb[:, 0:128]
        U_b = cb[:, 128:256]
        ones_b = cb[:, 256:384]
        SU_f = cf[:, 384:512]
        SL_b = cb[:, 512:640]
        ones_f = cf[:, 256:384]
        ident_f = cf[:, 0:128]
        tri_f = cf[:, 128:256]
        win = self.w_in[l].rearrange("(kc p) n -> p kc n", p=128)
        CB = [self.cb_b]
        CF = [self.cf_b]
        wi = self.next_w()
        wab = self.wbuf[wi]
        S.op("pool", lambda e: e.dma_start(out=wab[:, :, 0:16], in_=win[:, :, OFF_DN_A:OFF_DN_A + 16]), [], [self.w_b[wi]], dma=True)
        pab = self.bank()
        for blk in range(NB):
            for kc in range(KC):
                self.PE(self.ps[:, pab, blk * 16:(blk + 1) * 16], hT[:, kc, blk * 128:(blk + 1) * 128], wab[:, kc, 0:16],
                        [self.h_b[kc], self.w_b[wi]], [self.ps_b[pab]], start=(kc == 0), stop=(kc == KC - 1))
        abv = self.ps[:, pab, 0:256].rearrange("p (b c) -> p b c", c=16)
        g_t, g_b = self.tmp("g_tok", 128, F32)
        be_t, be_b = self.tmp("be_tok", 128, F32)
        gc_t, gc_b = self.tmp("gc_tok", 128, F32)
        t1, t1_b = self.tmp("ab_t1", 128, F32)
        g3 = g_t.rearrange("p (b c) -> p b c", c=8)
        be3 = be_t.rearrange("p (b c) -> p b c", c=8)
        t13 = t1.rearrange("p (b c) -> p b c", c=8)
        dtb = sm[:, 256:384].rearrange("p (b c) -> p b c", c=8)
        self.TT(t13, abv[:, :, 0:8], dtb, ALU.add, [self.ps_b[pab], sm_b], [t1_b])
        self.ACT(t1, t1, AF.Exp, [t1_b], [t1_b])
        self.ACT(t1, t1, AF.Ln, [t1_b], [t1_b], bias=cf[:, 898:899])
        ea, ea_b = self.tmp("expalog", 128, F32)
        self.ACT(ea, sm[:, 384:512], AF.Exp, [sm_b], [ea_b])
        self.STT(g_t, t1, -1.0, ea, ALU.mult, ALU.mult, [t1_b, ea_b], [g_b])
        self.ACT(be3, abv[:, :, 8:16], AF.Exp, [self.ps_b[pab]], [be_b], scale=-1.0)
        self.TS(be_t, be_t, 1.0, ALU.add, [be_b], [be_b])
        S.op("dve", lambda e: e.reciprocal(out=be_t, in_=be_t), [be_b], [be_b])
        pgc = self.bank()
        self.PE(self.ps[:, pgc, 0:128], tri_f, g_t, CF + [g_b], [self.ps_b[pgc]])
        self.CPY("dve", gc_t, self.ps[:, pgc, 0:128], [self.ps_b[pgc]], [gc_b])
        nbe_t, nbe_b = self.tmp("nbe_tok", 128, F32)
        begc_t, begc_b = self.tmp("begc_tok", 128, F32)
        self.TS(nbe_t, be_t, -1.0, ALU.mult, [be_b], [nbe_b])
        self.ACT(begc_t, gc_t, AF.Exp, [gc_b], [begc_b])
        self.TT(begc_t, begc_t, be_t, ALU.mult, [begc_b, be_b], [begc_b])

        zc = [self.tmp("zc%d" % f, 2052) for f in range(3)]
        sgate, sgate_b = self.tmp("sgate", T)
        fT = [self.tmp("fT%d" % f, T) for f in range(3)]
        ktok, ktok_b = self.tmp("ktok", T)
        vtok, vtok_b = self.tmp("vtok", T)
        ktok3 = ktok.rearrange("p (b d) -> p b d", d=128)
        vtok3 = vtok.rearrange("p (b d) -> p b d", d=128)
        sq, sq_b = self.tmp("dsq", 512)
        rs, rs_b = self.tmp("drs", 512, F32)
        dg = [self.tmp("diag%d" % i, 128) for i in range(12)]
        S32, S32_b = self.tmp("S32", 128, F32)
        S16, S16_b = self.tmp("S16", 128)
        ost = [self.tmp("ost0", T)] * 2
        for f in range(3):
            S.op("dve", lambda e, f=f: e.memset(zc[f][0][:, 0:4], 0.0), [], [zc[f][1]])

        for h in range(8):
            S.barrier()
            cols = [OFF_DN_QKV + h * 128, OFF_DN_QKV + 1024 + h * 128, OFF_DN_QKV + 2048 + h * 128, OFF_DN_GATE + h * 128]
            w1 = self.next_w()
            w2 = self.next_w()
            wb1, wb2 = self.wbuf[w1], self.wbuf[w2]
            S.op("pool", lambda e, wb1=wb1, cols=cols: [e.dma_start(out=wb1[:, :, 0:128], in_=win[:, :, cols[0]:cols[0] + 128]),
                                                          e.dma_start(out=wb1[:, :, 128:256], in_=win[:, :, cols[1]:cols[1] + 128])],
                 [], [self.w_b[w1]], dma=True, ndma=2)
            S.op("pool", lambda e, wb2=wb2, cols=cols: [e.dma_start(out=wb2[:, :, 0:128], in_=win[:, :, cols[2]:cols[2] + 128]),
                                                          e.dma_start(out=wb2[:, :, 128:256], in_=win[:, :, cols[3]:cols[3] + 128])],
                 [], [self.w_b[w2]], dma=True, ndma=2)
            for f in range(3):
                for i in range(4):
                    c = f * 8 + h
                    self.TS(dg[f * 4 + i][0], ident_b, sm[:, c * 4 + i:c * 4 + i + 1], ALU.mult, CB + [sm_b], [dg[f * 4 + i][1]])
            for f in range(4):
                wb, wbb = (wb1, self.w_b[w1]) if f < 2 else (wb2, self.w_b[w2])
                co = (f % 2) * 128
                for t in range(NT):
                    pb = self.bank()
                    for kc in range(KC):
                        self.PE(self.ps[:, pb, :], wb[:, kc, co:co + 128], hT[:, kc, t * 512:(t + 1) * 512],
                                [wbb, self.h_b[kc]], [self.ps_b[pb]], start=(kc == 0), stop=(kc == KC - 1))
                    if f < 3:
                        self.CPY("dve" if t % 2 == 0 else "act", zc[f][0][:, 4 + t * 512:4 + (t + 1) * 512], self.ps[:, pb, :],
                                 [self.ps_b[pb]], [zc[f][1]])
                    else:
                        self.ACT(sgate[:, t * 512:(t + 1) * 512], self.ps[:, pb, :], AF.Silu, [self.ps_b[pb]], [sgate_b])
            for f in range(3):
                for t in range(NT):
                    pb = self.bank()
                    for i in range(4):
                        self.PE(self.ps[:, pb, :], dg[f * 4 + i][0], zc[f][0][:, 1 + t * 512 + i:1 + t * 512 + i + 512],
                                [dg[f * 4 + i][1], zc[f][1]], [self.ps_b[pb]], start=(i == 0), stop=(i == 3))
                    self.ACT(fT[f][0][:, t * 512:(t + 1) * 512], self.ps[:, pb, :], AF.Silu, [self.ps_b[pb]], [fT[f][1]])
            for f in range(2):
                for t in range(NT):
                    sl = slice(t * 512, (t + 1) * 512)
                    self.TT(sq, fT[f][0][:, sl], fT[f][0][:, sl], ALU.mult, [fT[f][1]], [sq_b])
                    pb = self.bank()
                    self.PE(self.ps[:, pb, :], ones_b, sq, CB + [sq_b], [self.ps_b[pb]])
                    if f == 0:
                        self.ACT(rs, self.ps[:, pb, :], AF.Ln, [self.ps_b[pb]] + CF, [rs_b], scale=128.0, bias=cf[:, 897:898])
                    else:
                        self.ACT(rs, self.ps[:, pb, :], AF.Ln, [self.ps_b[pb]] + CF, [rs_b], scale=1.0, bias=cf[:, 896:897])
                    self.ACT(rs, rs, AF.Exp, [rs_b], [rs_b], scale=-0.5)
                    self.TT(fT[f][0][:, sl], fT[f][0][:, sl], rs, ALU.mult, [fT[f][1], rs_b], [fT[f][1]])
            for (src, dst3, dst_b) in ((fT[1], ktok3, ktok_b), (fT[2], vtok3, vtok_b)):
                for g4 in range(4):
                    pb = self.bank()
                    pbv = self.ps[:, pb, :].bitcast(BF16)
                    for q in range(4):
                        blk = g4 * 4 + q
                        self.TR(pbv[:, q * 128:(q + 1) * 128], src[0][:, blk * 128:(blk + 1) * 128], ident_b,
                                [src[1]] + CB, [self.ps_b[pb]])
                    self.CPY("dve" if g4 % 2 == 0 else "act", dst3[:, g4 * 4:(g4 + 1) * 4, :],
                             pbv[:, 0:512].rearrange("p (q d) -> p q d", d=128), [self.ps_b[pb]], [dst_b])
            S.op("dve", lambda e: e.memset(S32, 0.0), [], [S32_b])
            S.op("dve", lambda e: e.memset(S16, 0.0), [], [S16_b])
            oT, oT_b = ost[h % 2]
            qT, kT = fT[0], fT[1]
            def alias(name, base, off, n, dt=F32):
                bap, bb = base
                ne = n if dt == BF16 else 2 * n
                ap = bap[:, off:off + ne]
                return (ap if dt == BF16 else ap.bitcast(F32))

            def v2(ap):
                return ap.rearrange("p (c i) -> p c i", i=128)

            def bm2(ap):
                return ap.unsqueeze(1).to_broadcast([128, 2, 128])

            def bc2(ap2):
                return ap2.unsqueeze(2).to_broadcast([128, 2, 128])

            Xs = [alias("Xa", zc[0], 4, 512), alias("Xb", zc[0], 1028, 512)]
            Xts = [alias("Xta", zc[1], 4, 512), alias("Xtb", zc[1], 1028, 512)]
            Qs = [alias("Qa", zc[2], 4, 512), alias("Qb", zc[2], 1028, 512)]
            rr = self.tmp("rr", 1024, F32)[0]
            df = self.tmp("df", 512, F32)[0]
            LtM = self.tmp("LtM", 512, F32)[0]
            LM = self.tmp("LM", 512, F32)[0]
            nRb = self.tmp("nRb", 512, F32)[0]
            Rex = self.tmp("Rex", 512, F32)[0]
            Bf = self.tmp("Bf", 512, F32)[0]
            Bo = self.tmp("Bo16", 512)[0]
            R = self.tmp("Rr16", 1024)[0]
            cp_ = self.tmp("c16", 1024)[0]
            z16b = self.tmp("z16b", 1024)[0]
            Q16 = self.tmp("Q16", 512)[0]
            y16 = self.tmp("y16", 1024)[0]
            Aq = self.tmp("Aq", 512)[0]
            qd = self.tmp("qd", 512)[0]
            kd = self.tmp("kd", 512)[0]
            wT4 = self.tmp("wT4", 512)[0]
            on16, on16_b = self.tmp("on16", 512)
            vns = [self.tmp("vn%d" % i, 128) for i in range(2)]
            sc4 = self.tmp("sc4", 16, F32)[0]
            if not hasattr(self, "hb"):
                self.hb = {}
            for nm in ("rr", "df", "LtM", "LM", "nRb", "Rex", "Bf", "Bo", "Rr", "cpr", "y16", "Aq", "qd", "kd", "wT4", "sc4",
                       "Xa", "Xb", "Xta", "Xtb", "Qa", "Qb", "z16b", "Q16"):
                if nm not in self.hb:
                    self.hb[nm] = S.bufs("h_" + nm, 2)
            hb = self.hb

            def BB(nm):
                return list(hb[nm])
            y3 = y16.rearrange("p (c i) -> p c i", i=256)
            POSL = cf[:, 512:640]
            BD = cf[:, 768:896]
            PO = 7

            def prep(hf, qi):
                na = qi * 4 + 2 * hf
                cbase = na * 8 + h
                hs = slice(hf * 256, hf * 256 + 256)
                ws = slice(hf * 512, hf * 512 + 512)
                tsl = slice(na * 128, na * 128 + 256)

                def cols(t_):
                    return t_[:, cbase:cbase + 9:8]

                def B(nm):
                    return hb[nm][hf]
                rrh = rr[:, ws]
                self.TT(v2(rrh[:, 0:256]), bm2(tri_f), bc2(cols(g_t)), ALU.mult, CF + [g_b], [B("rr")]); yield
                self.TT(v2(rrh[:, 256:512]), bm2(ident_f), bc2(cols(be_t)), ALU.mult, CF + [be_b], [B("rr")]); yield
                pR = self.bank()
                self.PE(self.ps[:, pR, :], ones_f, rrh, CF + [B("rr")], [self.ps_b[pR]]); yield
                Rg = self.ps[:, pR, 0:256]
                Rb = self.ps[:, pR, 256:512]
                dfh, LtMh, LMh, nRbh, Rexh, Bfh, Boh = df[:, hs], LtM[:, hs], LM[:, hs], nRb[:, hs], Rex[:, hs], Bf[:, hs], Bo[:, hs]
                sck = sc4[:, 2 * hf:2 * hf + 2]
                self.TT(v2(dfh), v2(Rg), bc2(cols(gc_t)), ALU.subtract, [self.ps_b[pR], gc_b], [B("df")]); yield
                self.ACT(Rexh, Rg, AF.Exp, [self.ps_b[pR]], [B("Rex")]); yield
                self.TT(sck, self.ps[:, pR, 127:256:128], cols(gc_t), ALU.subtract, [self.ps_b[pR], gc_b], [B("sc4")]); yield
                self.STT(v2(nRbh), v2(Rb), -1.0, bm2(SU_f), ALU.mult, ALU.mult, [self.ps_b[pR]] + CF, [B("nRb")]); yield
                self.TS(LtMh, dfh, 0.0, ALU.min, [B("df")], [B("LtM")]); yield
                self.STT(v2(LMh), v2(dfh), 0.0, bm2(POSL), ALU.max, ALU.add, [B("df")] + CF, [B("LM")]); yield
                self.ACT(LtMh, LtMh, AF.Exp, [B("LtM")], [B("LtM")]); yield
                self.ACT(LMh, LMh, AF.Exp, [B("LM")], [B("LM")], scale=-1.0); yield
                self.ACT(sck, sck, AF.Exp, [B("sc4")], [B("sc4")]); yield
                pK = self.bank()
                for j in range(2):
                    bs = slice((na + j) * 128, (na + j + 1) * 128)
                    self.PE(self.ps[:, pK, j * 128:(j + 1) * 128], kT[0][:, bs], kT[0][:, bs], [kT[1]], [self.ps_b[pK]])
                    self.PE(self.ps[:, pK, 256 + j * 128:256 + (j + 1) * 128], kT[0][:, bs], qT[0][:, bs], [kT[1], qT[1]], [self.ps_b[pK]])
                yield
                KK = self.ps[:, pK, 0:256]
                QKt = self.ps[:, pK, 256:512]
                X0h = Xs[0][:, hs]
                Xt0h = Xts[0][:, hs]
                self.TT(Bfh, KK, LtMh, ALU.mult, [self.ps_b[pK], B("LtM")], [B("Bf")]); yield
                self.TT(v2(dfh), v2(KK), bc2(cols(nbe_t)), ALU.mult, [self.ps_b[pK], nbe_b, B("LM"), B("LtM")], [B("df")]); yield
                self.TT(Bfh, Bfh, nRbh, ALU.mult, [B("Bf"), B("nRb")], [B("Bf")]); yield
                self.TT(Xt0h, dfh, LMh, ALU.mult, [B("df"), B("LM")], [B("Xta")]); yield
                self.TT(v2(X0h), v2(Bfh), bm2(BD), ALU.mult, [B("Bf")] + CF, [B("Xa")]); yield
                self.TT(v2(Qs[0][:, hs]), v2(X0h), bm2(ident_f), ALU.add, [B("Xa")] + CF, [B("Qa")]); yield
                self.TT(Boh, Bfh, X0h, ALU.subtract, [B("Bf"), B("Xa")], [B("Bo")]); yield
                self.TT(nRbh, QKt, LtMh, ALU.mult, [self.ps_b[pK], B("LtM"), B("Bf")], [B("nRb")]); yield
                self.TT(v2(Aq[:, hs]), v2(nRbh), bm2(tri_f), ALU.mult, [B("nRb")] + CF, [B("Aq")]); yield
                R3h = R[:, ws].rearrange("p (c i) -> p c i", i=256)
                self.TT(R3h[:, :, 0:128], vtok3[:, na:na + 2, :], bc2(cols(be_t)), ALU.mult, [vtok_b, be_b], [B("Rr")]); yield
                self.TT(R3h[:, :, 128:256], ktok3[:, na:na + 2, :], bc2(cols(begc_t)), ALU.mult, [ktok_b, begc_b], [B("Rr")]); yield
                qc = 0
                xc = 0
                xn_ = ("Xa", "Xb")
                xtn_ = ("Xta", "Xtb")
                qn_ = ("Qa", "Qb")
                for m in range(5):
                    Xc, Xn = Xs[xc][:, hs], Xs[1 - xc][:, hs]
                    Xtc, Xtn = Xts[xc][:, hs], Xts[1 - xc][:, hs]
                    bXc, bXn, bXtc, bXtn = B(xn_[xc]), B(xn_[1 - xc]), B(xtn_[xc]), B(xtn_[1 - xc])
                    pX = self.bank()
                    for j in range(2):
                        cs = slice(j * 128, (j + 1) * 128)
                        self.PE(self.ps[:, pX, j * 128:(j + 1) * 128], Xc[:, cs], Xtc[:, cs], [bXtc, bXc], [self.ps_b[pX]])
                    if m < 4:
                        for j in range(2):
                            cs = slice(j * 128, (j + 1) * 128)
                            self.PE(self.ps[:, pX, 256 + j * 128:256 + (j + 1) * 128], Xtc[:, cs], Xc[:, cs], [bXtc, bXc], [self.ps_b[pX]])
                    yield
                    self.CPY("act", Xtn, self.ps[:, pX, 0:256], [self.ps_b[pX]], [bXtn]); yield
                    pQq = self.bank()
                    Qc, Qn = Qs[qc][:, hs], Qs[1 - qc][:, hs]
                    bQc, bQn = B(qn_[qc]), B(qn_[1 - qc])
                    for j in range(2):
                        cs = slice(j * 128, (j + 1) * 128)
                        self.PE(self.ps[:, pQq, cs], Xtn[:, cs], Qc[:, cs], [bXtn, bQc], [self.ps_b[pQq]])
                    yield
                    if m < 4:
                        self.CPY("act", Xn, self.ps[:, pX, 256:512], [self.ps_b[pX]], [bXn]); yield
                    if m < 4:
                        self.TT(Qn, self.ps[:, pQq, 0:256], Qc, ALU.add, [self.ps_b[pQq], bQc], [bQn]); yield
                    else:
                        self.TT(Q16[:, hs], self.ps[:, pQq, 0:256], Qc, ALU.add, [self.ps_b[pQq], bQc], [B("Q16")]); yield
                    qc = 1 - qc
                    xc = 1 - xc
                Qf, bQf = Q16[:, hs], B("Q16")
                self.TT(qd[:, hs], qT[0][:, tsl], Rexh, ALU.mult, [qT[1], B("Rex")], [B("qd")]); yield
                self.TT(v2(kd[:, hs]), ktok3[:, na:na + 2, :], bc2(sck), ALU.mult, [ktok_b, B("sc4")], [B("kd")]); yield
                zf = z16b[:, ws]
                z3h = zf.rearrange("p (c i) -> p c i", i=256)
                cf_ = cp_[:, ws]
                c3h = cf_.rearrange("p (c i) -> p c i", i=256)
                pz = self.bank()
                for j in range(2):
                    self.PE(self.ps[:, pz, j * 256:(j + 1) * 256], Qf[:, j * 128:(j + 1) * 128], R3h[:, j, :], [bQf, B("Rr")], [self.ps_b[pz]])
                yield
                self.CPY("act", zf, self.ps[:, pz, :], [self.ps_b[pz]], [B("z16b")]); yield
                pc = self.bank()
                for j in range(2):
                    self.PE(self.ps[:, pc, j * 256:(j + 1) * 256], Boh[:, j * 128:(j + 1) * 128], z3h[:, j, :], [B("Bo"), B("z16b")], [self.ps_b[pc]])
                yield
                self.CPY("act", cf_, self.ps[:, pc, :], [self.ps_b[pc]], [B("cpr")]); yield
                py = self.bank()
                for j in range(2):
                    self.PE(self.ps[:, py, j * 256:(j + 1) * 256], Qf[:, j * 128:(j + 1) * 128], c3h[:, j, :], [bQf, B("cpr")], [self.ps_b[py]])
                yield
                self.TT(y16[:, ws], self.ps[:, py, :], zf, ALU.add, [self.ps_b[py], B("z16b")], [B("y16")]); yield
                pb = self.bank()
                pbv = self.ps[:, pb, :].bitcast(BF16)
                for j in range(2):
                    self.TR(pbv[:, j * 128:(j + 1) * 128], y3[:, 2 * hf + j, 128:256], ident_b, [B("y16")] + CB, [self.ps_b[pb]])
                yield
                self.CPY("act", wT4[:, hs], pbv[:, 0:256], [self.ps_b[pb]], [B("wT4")]); yield

            for qi in range(4):
                n0 = qi * 4
                ts = slice(n0 * 128, n0 * 128 + 512)
                gens = [prep(0, qi), prep(1, qi)]
                while gens:
                    for g_ in list(gens):
                        try:
                            next(g_)
                        except StopIteration:
                            gens.remove(g_)
                for c in range(4):
                    cs = slice(c * 128, (c + 1) * 128)
                    hf = c // 2
                    vn, vn_b = vns[c % 2]
                    pa = self.bank()
                    self.PE(self.ps[:, pa, 0:128], wT4[:, cs], S16, [hb["wT4"][hf], S16_b], [self.ps_b[pa]])
                    self.PE(self.ps[:, PO, cs], qd[:, cs], S16, [hb["qd"][hf], S16_b], [self.ps_b[PO]], start=True, stop=False)
                    self.TT(vn, y3[:, c, 0:128], self.ps[:, pa, 0:128], ALU.subtract, [hb["y16"][hf], self.ps_b[pa]], [vn_b])
                    self.PE(self.ps[:, PO, cs], Aq[:, cs], vn, [hb["Aq"][hf], vn_b], [self.ps_b[PO]], start=False, stop=True)
                    pd = self.bank()
                    self.PE(self.ps[:, pd, 0:128], kd[:, cs], vn, [hb["kd"][hf], vn_b], [self.ps_b[pd]])
                    gl = Rex[:, c * 128 + 127:c * 128 + 128]
                    self.STT(S16, S32, gl, self.ps[:, pd, 0:128], ALU.mult, ALU.add, [S32_b, hb["Rex"][hf], self.ps_b[pd]], [S16_b])
                    self.STT(S32, S32, gl, self.ps[:, pd, 0:128], ALU.mult, ALU.add, [S32_b, hb["Rex"][hf], self.ps_b[pd]], [S32_b])
                for c in range(4):
                    cs = slice(c * 128, (c + 1) * 128)
                    self.ACT(df[:, cs], self.ps[:, PO, cs], AF.Square, [self.ps_b[PO]], BB("df") + BB("sc4"), accum=sc4[:, 4 + c:5 + c])
                self.ACT(sc4[:, 8:12], sc4[:, 4:8], AF.Ln, BB("sc4") + CF, BB("sc4"), scale=1.0 / 128.0, bias=cf[:, 896:897])
                self.ACT(sc4[:, 8:12], sc4[:, 8:12], AF.Exp, BB("sc4"), BB("sc4"), scale=-0.5)
                nR4 = nRb.rearrange("p (c i) -> p c i", i=128)
                self.TT(nR4, self.ps[:, PO, :].rearrange("p (c i) -> p c i", i=128), sc4[:, 8:12].unsqueeze(2).to_broadcast([128, 4, 128]),
                        ALU.mult, [self.ps_b[PO]] + BB("sc4") + BB("Aq"), BB("nRb"))
                self.TT(on16.rearrange("p (c i) -> p c i", i=128), nR4, sm[:, 128:256].unsqueeze(1).to_broadcast([128, 4, 128]),
                        ALU.mult, BB("nRb") + [sm_b], [on16_b])
                pb = self.bank()
                pbv = self.ps[:, pb, :].bitcast(BF16)
                for c in range(4):
                    cs = slice(c * 128, (c + 1) * 128)
                    self.TR(pbv[:, cs], on16[:, cs], ident_b, [on16_b] + CB, [self.ps_b[pb]])
                self.TT(oT[:, ts], pbv[:, 0:512], sgate[:, ts], ALU.mult, [self.ps_b[pb], sgate_b], [oT_b])
            S.op("sp", lambda e, h=h, oT=oT: e.dma_start(out=self.odn_d[h], in_=oT), [oT_b], [self.odn_b[h]], dma=True)

    def att_branch(self, l):
        S = self.S
        S.barrier()
        self.tmps = {}
        self.aoff = 0
        hT = self.hT
        cb, cf = self.cb, self.cf
        CB, CF = [self.cb_b], [self.cf_b]
        sm, sm_b = self.tmp("small", 512, F32)
        sm_b = self.sm_b
        win = self.w_in[l].rearrange("(kc p) n -> p kc n", p=128)
        attn, attn_b = self.tmp("attn", 4 * T)
        self.attn, self.attn_b = attn, attn_b
        mrg, mrg_b = self.tmp("mrg", 8 * T)
        self.mrg, self.mrg_b = mrg, mrg_b
        acc, acc_b = self.tmp("attacc", 2 * T, F32)
        acc3 = acc.rearrange("p (h t) -> p h t", t=T)
        attn3 = attn.rearrange("p (h t) -> p h t", t=T)
        BD_b = cb[:, 768:896]
        msk = cb[:, 640:768]
        U_b = cb[:, 128:256]
        self.aoff = self.AR
        qn, qn_b = self.tmp("qn", T)
        kn, kn_b = self.tmp("kn", T)
        sqs = [self.tmp("asq_%d" % i, 512) for i in range(2)]
        rss = [self.tmp("ars_%d" % i, 512, F32) for i in range(2)]
        vt, vt_b = self.tmp("vt", 16 * 2 * 66)
        vt4 = vt.rearrange("p (b h d) -> p b h d", h=2, d=66)
        pt = [self.tmp("pt%d" % i, 256) for i in range(4)]
        negm, negm_b = self.tmp("negm", 256)
        S.op("dve", lambda e: e.memset(vt, 1.0), [], [vt_b])
        self.TS(negm[:, 0:128], msk, -1.0, ALU.add, CB, [negm_b], s2=30000.0, op1=ALU.mult)
        self.TS(negm[:, 128:256], U_b, -1.0, ALU.add, CB, [negm_b], s2=30000.0, op1=ALU.mult)
        groups = ((128, 1), (512, 4), (2048, 16))
        import os
        for c in [int(v) for v in os.environ.get('ATT_CHUNKS', '0,2,4,1,3,5').split(',')]:
            g = c // 2
            d = groups[g][1]
            nblk = (T // d) // 128
            w1 = self.next_w()
            w2 = self.next_w()
            wb1, wb2 = self.wbuf[w1], self.wbuf[w2]
            S.op("pool", lambda e, wb1=wb1, c=c: [e.dma_start(out=wb1[:, :, 0:128], in_=win[:, :, c * 128:(c + 1) * 128]),
                                                  e.dma_start(out=wb1[:, :, 128:256], in_=win[:, :, 768 + c * 128:768 + (c + 1) * 128])],
                 [], [self.w_b[w1]], dma=True, ndma=2)
            S.op("pool", lambda e, wb2=wb2, c=c: e.dma_start(out=wb2[:, :, 0:128], in_=win[:, :, 1536 + c * 128:1536 + (c + 1) * 128]),
                 [], [self.w_b[w2]], dma=True)
            tiles = [(qi, t) for qi in range(2) for t in range(NT)]
            dsts = ((qn, qn_b), (kn, kn_b))
            pbs_ = {}

            def norm_a(i):
                qi, t = tiles[i]
                sl = slice(t * 512, (t + 1) * 512)
                sq, sq_b = sqs[i % 2]
                pb = self.bank()
                pbs_[i] = pb
                for kc in range(KC):
                    self.PE(self.ps[:, pb, :], wb1[:, kc, qi * 128:(qi + 1) * 128], hT[:, kc, sl],
                            [self.w_b[w1], self.h_b[kc]], [self.ps_b[pb]], start=(kc == 0), stop=(kc == KC - 1))
                self.ACT(sq, self.ps[:, pb, :], AF.Square, [self.ps_b[pb]], [sq_b])

            def norm_b(i):
                qi, t = tiles[i]
                sl = slice(t * 512, (t + 1) * 512)
                dst, dst_b = dsts[qi]
                sq, sq_b = sqs[i % 2]
                rs, rs_b = rss[i % 2]
                pb = pbs_[i]
                p2 = self.bank()
                self.PE(self.ps[:, p2, :], BD_b, sq, CB + [sq_b], [self.ps_b[p2]])
                if qi == 0:
                    self.ACT(rs, self.ps[:, p2, :], AF.Ln, [self.ps_b[p2]] + CF, [rs_b], scale=1.0, bias=cf[:, 899:900])
                else:
                    self.ACT(rs, self.ps[:, p2, :], AF.Ln, [self.ps_b[p2]] + CF, [rs_b], scale=1.0 / 64.0, bias=cf[:, 896:897])
                self.ACT(rs, rs, AF.Exp, [rs_b], [rs_b], scale=-0.5)
                self.STT(dst[:, sl], self.ps[:, pb, :], sm[:, 96 + qi:97 + qi], rs, ALU.mult, ALU.mult, [self.ps_b[pb], sm_b, rs_b], [dst_b])
            for i in range(len(tiles) + 1):
                if i < len(tiles):
                    norm_a(i)
                if i >= 1:
                    norm_b(i - 1)
            blocks = [(r, nb) for r in range(d) for nb in range(nblk)]
            for bi, (r, nb) in enumerate(blocks):
                st = nb * 128 * d + r
                tsl = slice(st, st + 127 * d + 1, d)
                pb = self.bank()
                for kc in range(KC):
                    self.PE(self.ps[:, pb, 0:128], hT[:, kc, tsl], wb2[:, kc, 0:128], [self.h_b[kc], self.w_b[w2]], [self.ps_b[pb]],
                            start=(kc == 0), stop=(kc == KC - 1))
                self.CPY("act", vt4[:, bi, :, 0:64], self.ps[:, pb, 0:128].rearrange("p (h d) -> p h d", d=64), [self.ps_b[pb]], [vt_b])
            items = [(hh2, bi, r, nb) for hh2 in range(2) for bi, (r, nb) in enumerate(blocks)]

            def st_scores(i):
                hh2, bi, r, nb = items[i]
                ps_ = slice(hh2 * 64, hh2 * 64 + 64)
                st = nb * 128 * d + r
                qsl = slice(st, st + 127 * d + 1, d)
                pS = self.bank()
                has_prev = nb > 0
                if has_prev:
                    sp_ = (nb - 1) * 128 * d + r
                    ksl = slice(sp_, sp_ + 127 * d + 1, d)
                    self.PE(self.ps[:, pS, 0:128], kn[ps_, ksl], qn[ps_, qsl], [kn_b, qn_b], [self.ps_b[pS]], start=True, stop=False)
                self.PE(self.ps[:, pS, 128:256], kn[ps_, qsl], qn[ps_, qsl], [kn_b, qn_b], [self.ps_b[pS]], start=(not has_prev), stop=False)
                lo = 0 if has_prev else 128
                self.PE(self.ps[:, pS, lo:256], cb[:, 0:128], negm[:, lo:256], CB + [negm_b], [self.ps_b[pS]], start=False, stop=True)
                P, P_b = pt[i % 4]
                self.ACT(P[:, lo:256], self.ps[:, pS, lo:256], AF.Exp, [self.ps_b[pS]], [P_b])

            def st_pv(i):
                hh2, bi, r, nb = items[i]
                st = nb * 128 * d + r
                qsl = slice(st, st + 127 * d + 1, d)
                has_prev = nb > 0
                P, P_b = pt[i % 4]
                pO = self.bank()
                if has_prev:
                    self.PE(self.ps[0:65, pO, 0:128], vt4[:, bi - 1, hh2, 0:65], P[:, 0:128], [vt_b, P_b], [self.ps_b[pO]], start=True, stop=False)
                self.PE(self.ps[0:65, pO, 0:128], vt4[:, bi, hh2, 0:65], P[:, 128:256], [vt_b, P_b], [self.ps_b[pO]],
                        start=(not has_prev), stop=True)
                if g == 0:
                    self.CPY("dve", acc3[0:65, hh2, qsl], self.ps[0:65, pO, 0:128], [self.ps_b[pO]], [acc_b])
                else:
                    self.TT(acc3[0:65, hh2, qsl], self.ps[0:65, pO, 0:128], acc3[0:65, hh2, qsl], ALU.add, [self.ps_b[pO], acc_b], [acc_b])
            SK = 2
            for i in range(len(items) + SK):
                if i < len(items):
                    st_scores(i)
                if i >= SK:
                    st_pv(i - SK)
            if g == 2:
                self.ACT(acc3[64:65, :, :], acc3[64:65, :, :], AF.Ln, [acc_b], [acc_b])
                self.ACT(acc3[64:65, :, :], acc3[64:65, :, :], AF.Exp, [acc_b], [acc_b], scale=-1.0)
                for hh2 in range(2):
                    hh = (c % 2) * 2 + hh2
                    for t in range(NT):
                        sl = slice(t * 512, (t + 1) * 512)
                        pb = self.bank()
                        self.PE(self.ps[0:64, pb, :], cf[64:65, 256:320], acc3[64:65, hh2, sl], CF + [acc_b], [self.ps_b[pb]])
                        self.TT(attn3[0:64, hh, sl], acc3[0:64, hh2, sl], self.ps[0:64, pb, :], ALU.mult, [acc_b, self.ps_b[pb]], [attn_b])

    def mix_final(self, l):
        S = self.S
        S.barrier(engines=("pe", "act", "dve", "sp", "pool"))
        hT, xT = self.hT, self.xT
        cb, cf = self.cb, self.cf
        attn3 = self.attn.rearrange("p (h t) -> p h t", t=T)
        mrg3 = self.mrg.rearrange("p (c t) -> p c t", t=T)
        mrg_b = self.mrg_b
        odn_all = self.tmps["attacc"][0].bitcast(BF16)
        odn_t = [odn_all[:, i * 4096:(i + 1) * 4096].rearrange("p (h t) -> p h t", t=512) for i in range(2)]
        odn_tb = S.bufs("odn_re", 2)
        odn_src = self.odn_d.rearrange("h p t -> p h t")
        oi = 0
        win = self.w_in[l].rearrange("(kc p) n -> p kc n", p=128)
        wpd = self.w_pd[l].rearrange("(kc p) n -> p kc n", p=128)
        wo = self.w_o[l].rearrange("(kc p) n -> p kc n", p=128)
        wpa_d = self.w_pa[l].rearrange("(hh p) n -> p hh n", p=64)
        self.aoff = self.AR
        self.tmps.pop("qn", None)
        wpa, wpa_b = self.tmp("wpa", 4 * D)
        wpa3 = wpa.rearrange("p (h n) -> p h n", n=D)
        S.op("pool", lambda e: e.dma_start(out=wpa3[0:64, :, :], in_=wpa_d), [], [wpa_b], dma=True)
        ge = [self.tmp("ge%d" % i, 512, F32) for i in range(2)]
        m1, m1_b = self.tmp("m1", 512, F32)
        mc = self.modc
        for dc in range(KC):
            w1 = self.next_w()
            w2 = self.next_w()
            wb1, wb2 = self.wbuf[w1], self.wbuf[w2]
            S.op("pool", lambda e, wb1=wb1, dc=dc: [e.dma_start(out=wb1[:, :, 0:128], in_=win[:, :, OFF_MERGE + dc * 128:OFF_MERGE + (dc + 1) * 128]),
                                                   e.dma_start(out=wb1[:, :, 128:256], in_=win[:, :, OFF_MERGE + D + dc * 128:OFF_MERGE + D + (dc + 1) * 128])],
                 [], [self.w_b[w1]], dma=True, ndma=2)
            S.op("pool", lambda e, wb2=wb2, dc=dc: e.dma_start(out=wb2[:, :, 0:128], in_=wpd[:, :, dc * 128:(dc + 1) * 128]),
                 [], [self.w_b[w2]], dma=True)
            for t in range(NT):
                sl = slice(t * 512, (t + 1) * 512)
                od3, odn_b = odn_t[oi % 2], odn_tb[oi % 2]
                oi += 1
                S.op("sp", lambda e, od3=od3, sl=sl: e.dma_start(out=od3, in_=odn_src[:, :, sl]), self.odn_b, [odn_b], dma=True)
                pga, pgd, pya, pyd = self.bank(), self.bank(), self.bank(), self.bank()
                for gi, pg in enumerate((pga, pgd)):
                    for kc in range(KC):
                        self.PE(self.ps[:, pg, :], wb1[:, kc, gi * 128:(gi + 1) * 128], hT[:, kc, sl], [self.w_b[w1], self.h_b[kc]],
                                [self.ps_b[pg]], start=(kc == 0), stop=(kc == KC - 1))
                for hh in range(4):
                    self.PE(self.ps[:, pya, :], wpa3[0:64, hh, dc * 128:(dc + 1) * 128], attn3[0:64, hh, sl], [wpa_b, self.attn_b],
                            [self.ps_b[pya]], start=(hh == 0), stop=(hh == 3))
                for h8 in range(8):
                    self.PE(self.ps[:, pyd, :], wb2[:, h8, 0:128], od3[:, h8, :], [self.w_b[w2], odn_b], [self.ps_b[pyd]],
                            start=(h8 == 0), stop=(h8 == 7))
                for gi, (pg, py) in enumerate(((pga, pya), (pgd, pyd))):
                    gt_, gt_b = ge[gi]
                    self.ACT(gt_, self.ps[:, pg, :], AF.Sigmoid, [self.ps_b[pg]], [gt_b])
                    self.TT(gt_, gt_, self.ps[:, py, :], ALU.mult, [gt_b, self.ps_b[py]], [gt_b])
                self.TT(mrg3[:, dc, sl], ge[0][0], ge[1][0], ALU.add, [ge[0][1], ge[1][1]], [mrg_b])
        for half in range(4):
            w1 = self.next_w()
            wb1 = self.wbuf[w1]
            S.op("pool", lambda e, wb1=wb1, half=half: e.dma_start(out=wb1[:, :, 0:256], in_=wo[:, :, half * 256:(half + 1) * 256]),
                 [], [self.w_b[w1]], dma=True)
            for j2 in range(2):
                dc = half * 2 + j2
                for t in range(NT):
                    sl = slice(t * 512, (t + 1) * 512)
                    pb = self.bank()
                    for kc in range(KC):
                        self.PE(self.ps[:, pb, :], wb1[:, kc, j2 * 128:(j2 + 1) * 128], mrg3[:, kc, sl], [self.w_b[w1], mrg_b],
                                [self.ps_b[pb]], start=(kc == 0), stop=(kc == KC - 1))
                    self.STT(xT[:, dc, sl], self.ps[:, pb, :], mc[:, 24 + 16 + dc:24 + 16 + dc + 1], xT[:, dc, sl], ALU.mult, ALU.add,
                             [self.ps_b[pb], self.modc_b, self.x_b[dc]], [self.x_b[dc]])

def make_consts():
    c = np.zeros((128, 1024), np.float32)
    c[:, 0:128] = np.eye(128, dtype=np.float32)
    jj = np.arange(128)[:, None]
    ii = np.arange(128)[None, :]
    c[:, 128:256] = (jj <= ii).astype(np.float32)
    c[:, 256:384] = 1.0
    c[:, 384:512] = (jj < ii).astype(np.float32)
    c[:, 512:640] = (jj > ii).astype(np.float32)
    c[:, 640:768] = (jj >= ii).astype(np.float32)
    c[:, 768:896] = ((jj < 64) == (ii < 64)).astype(np.float32)
    c[:, 512:640] = np.where((jj > ii) & ((jj < 64) == (ii < 64)), 0.0, 30000.0)
    c[:, 896] = EPS
    c[:, 897] = 128.0 * EPS
    c[:, 898] = 1.0
    c[:, 899] = 64.0 * EPS
    return c


def make_small(inputs):
    L = DEPTH
    sm = np.zeros((L, 128, 512), np.float32)
    cw = np.asarray(inputs["conv_w"], np.float32)
    sm[:, :, 0:96] = cw.reshape(L, 4, 24, 128).transpose(0, 3, 2, 1).reshape(L, 128, 96)
    p = np.arange(128) % 64
    sm[:, :, 96] = np.asarray(inputs["q_norm"], np.float32)[:, p]
    sm[:, :, 97] = np.asarray(inputs["k_norm"], np.float32)[:, p]
    sm[:, :, 104:112] = np.asarray(inputs["a_log"], np.float32)[:, None, :]
    sm[:, :, 112:120] = np.asarray(inputs["dt_bias"], np.float32)[:, None, :]
    sm[:, :, 128:256] = np.asarray(inputs["dn_norm"], np.float32)[:, None, :]
    sm[:, :, 256:384] = np.tile(np.asarray(inputs["dt_bias"], np.float32), (1, 16))[:, None, :]
    sm[:, :, 384:512] = np.tile(np.asarray(inputs["a_log"], np.float32), (1, 16))[:, None, :]
    return sm


def prep_inputs(inputs, b):
    f = lambda a: np.ascontiguousarray(a, dtype=np.float32)
    m = {}
    m["x"] = f(inputs["x"][b])
    m["c"] = f(inputs["c"][b].reshape(KC, 128).T)
    m["ada_w"] = f(inputs["ada_w"])
    m["ada_b"] = f(inputs["ada_b"].reshape(DEPTH, 72, 128).transpose(0, 2, 1))
    nr = np.stack([inputs["norm_ff1"], inputs["norm_mix"], inputs["norm_ff2"]], axis=1)
    m["norms"] = f(nr.reshape(DEPTH, 3 * KC, 128).transpose(0, 2, 1))
    m["ffn1_w_up"] = f(inputs["ffn1_w_up"])
    m["ffn2_w_up"] = f(inputs["ffn2_w_up"])
    m["ffn1_w_down"] = f(inputs["ffn1_w_down"])
    m["ffn2_w_down"] = f(inputs["ffn2_w_down"])
    m["w_in"] = f(inputs["w_in"])
    m["w_proj_att"] = f(inputs["w_proj_att"])
    m["w_proj_dn"] = f(inputs["w_proj_dn"])
    m["w_out"] = f(inputs["w_out"])
    m["consts"] = make_consts()
    m["small"] = make_small(inputs)
    return m


_PROG = {}


def run(inputs, n_layers=DEPTH, stages=("ffn1", "mix", "ffn2"), cores=8, trace=False):
    key = (n_layers, tuple(stages))
    if key not in _PROG:
        _PROG[key] = Prog(n_layers, stages)
    p = _PROG[key]
    shared = prep_inputs(inputs, 0)
    in_maps = []
    for b in range(cores):
        m = dict(shared)
        m["x"] = np.ascontiguousarray(inputs["x"][b], dtype=np.float32)
        m["c"] = np.ascontiguousarray(inputs["c"][b].reshape(KC, 128).T, dtype=np.float32)
        in_maps.append(m)
    res = run_bass_kernel_spmd(p.nc, in_maps, core_ids=list(range(cores)), trace=trace)
    out = np.stack([r["out"] for r in res.results], axis=0)
    return out, res


def kernel(**inputs):
    out, _ = run(inputs)
    return out.astype(np.float32)
```

```python
import numpy as np
import concourse.bass as bass
import concourse.mybir as mybir
from concourse.bass_utils import run_bass_kernel_spmd

F32 = mybir.dt.float32
BF16 = mybir.dt.bfloat16
AF = mybir.ActivationFunctionType
ALU = mybir.AluOpType

D = 1024
T = 2048
DEPTH = 4
DFF = 2816
NFF = DFF // 128
KC = D // 128
NT = T // 512
NB = T // 128
N_IN = 8464
OFF_DN_QKV = 2304
OFF_DN_GATE = 5376
OFF_DN_A = 6400
OFF_DN_B = 6408
OFF_MERGE = 6416
EPS = 1e-6
SEM_LIMIT = 30000
STRICT_SAME_ENGINE = False


class Buf:
    __slots__ = ("name", "w", "rs", "dsem", "excl")

    def __init__(self, name, excl=False):
        self.name = name
        self.w = None
        self.rs = []
        self.dsem = None
        self.excl = excl


class Op:
    __slots__ = ("eng", "fn", "deps", "dma", "ndma", "sem", "val", "signals", "idx")


class DmaSem:
    def __init__(self, sched):
        self.s = sched
        self.sem = sched.new_sem()
        self.count = 0

    def take(self, n):
        if self.count + 16 * n > SEM_LIMIT:
            self.sem = self.s.new_sem()
            self.count = 0
        self.count += 16 * n
        return self.sem, self.count


class Sched:
    ENGS = ("pe", "act", "dve", "pool", "sp")

    def __init__(self, nc):
        self.nc = nc
        self.q = {e: [] for e in self.ENGS}
        self.nsem = 0
        self.dma_ops = []
        self.all_bufs = []

    def new_sem(self):
        self.nsem += 1
        return self.nc.alloc_semaphore("s%d" % self.nsem)

    def buf(self, name):
        b = Buf(name)
        return b

    def bufs(self, name, n, excl=False):
        return [Buf("%s%d" % (name, i), excl) for i in range(n)]

    def op(self, eng, fn, reads=(), writes=(), dma=False, ndma=1):
        o = Op()
        o.eng = eng
        o.fn = fn
        o.dma = dma
        o.ndma = ndma
        o.signals = dma
        o.sem = None
        o.val = 0
        deps = []
        for b in reads:
            if b.w is not None:
                deps.append((b.w, True))
            if b.excl:
                for r in b.rs:
                    if r.eng != eng:
                        deps.append((r, False))
        for b in writes:
            if b.w is not None:
                deps.append((b.w, False))
            for r in b.rs:
                deps.append((r, False))
        fd = {}
        for d, raw in deps:
            if d is o:
                continue
            if not d.dma and not dma and d.eng == eng:
                if eng == "pe":
                    continue
                if not raw and not STRICT_SAME_ENGINE:
                    continue
            if d.dma and dma and not raw:
                pass
            fd[id(d)] = d
        o.deps = list(fd.values())
        for b in reads:
            b.rs.append(o)
        for b in writes:
            b.w = o
            b.rs = []
        if dma:
            wb = writes[0]
            if wb.dsem is None:
                wb.dsem = DmaSem(self)
            o.sem, o.val = wb.dsem.take(ndma)
            self.dma_ops.append(o)
        o.idx = len(self.q[eng])
        self.q[eng].append(o)
        return o

    def barrier(self, engines=("pe", "act", "dve", "sp")):
        lasts = []
        for e in ("pe", "act", "dve", "pool"):
            for o in reversed(self.q[e]):
                if not o.dma and o.fn is not None:
                    lasts.append(o)
                    break
        dm = list(self.dma_ops)
        self.dma_ops = []
        for e in engines:
            o = Op()
            o.eng = e
            o.fn = None
            o.dma = False
            o.ndma = 0
            o.signals = False
            o.sem = None
            o.val = 0
            o.deps = [d for d in lasts if d.eng != e or e != "pe"] + dm
            o.idx = len(self.q[e])
            self.q[e].append(o)

    def emit(self):
        nc = self.nc
        for e in self.ENGS:
            for o in self.q[e]:
                for d in o.deps:
                    if not d.dma:
                        d.signals = True
        for e in self.ENGS:
            sem = None
            cnt = 0
            for o in self.q[e]:
                if o.dma or not o.signals:
                    continue
                if sem is None or cnt >= SEM_LIMIT:
                    sem = self.new_sem()
                    cnt = 0
                cnt += 1
                o.sem = sem
                o.val = cnt
        sched = self

        def run(e, eng):
            waited = {}
            for o in sched.q[e]:
                for d in o.deps:
                    key = id(d.sem)
                    if waited.get(key, 0) >= d.val:
                        continue
                    waited[key] = d.val
                    eng.wait_ge(d.sem, d.val)
                if o.fn is None:
                    continue
                r = o.fn(eng)
                if o.dma:
                    if not isinstance(r, (list, tuple)):
                        r = [r]
                    assert len(r) == o.ndma, (len(r), o.ndma)
                    for ins in r:
                        ins.then_inc(o.sem, 16)
                elif o.signals:
                    r.then_inc(o.sem, 1)

        with nc.Block() as block:
            @block.tensor
            def _(eng):
                run("pe", eng)

            @block.scalar
            def _(eng):
                run("act", eng)

            @block.vector
            def _(eng):
                run("dve", eng)

            @block.gpsimd
            def _(eng):
                run("pool", eng)

            @block.sync
            def _(eng):
                run("sp", eng)


class Prog:
    def __init__(self, n_layers=DEPTH, stages=("ffn1", "mix", "ffn2"), dbg=None):
        self.n_layers = n_layers
        self.stages = stages
        self.dbg = dbg
        nc = bass.Bass("TRN2", target_bir_lowering=False)
        self.nc = nc
        self.S = Sched(nc)
        self.decl_io()
        self.alloc()
        self.build()
        self.S.emit()

    def decl_io(self):
        nc = self.nc
        L = DEPTH

        def inp(name, shape):
            return nc.dram_tensor(name, list(shape), F32, kind="ExternalInput").ap()

        self.x_d = inp("x", (T, D))
        self.c_d = inp("c", (128, KC))
        self.ada_w = inp("ada_w", (L, D, 9 * D))
        self.ada_b = inp("ada_b", (L, 128, 72))
        self.norms = inp("norms", (L, 128, 3 * KC))
        self.w_up = [inp("ffn1_w_up", (L, D, 2 * DFF)), inp("ffn2_w_up", (L, D, 2 * DFF))]
        self.w_dn = [inp("ffn1_w_down", (L, DFF, D)), inp("ffn2_w_down", (L, DFF, D))]
        self.w_in = inp("w_in", (L, D, N_IN))
        self.cst = inp("consts", (128, 1024))
        self.small_d = inp("small", (L, 128, 512))
        self.w_pa = inp("w_proj_att", (L, 256, D))
        self.w_pd = inp("w_proj_dn", (L, D, D))
        self.w_o = inp("w_out", (L, D, D))
        self.odn_d = nc.dram_tensor("odn_scr", [8, 128, T], BF16, kind="Internal").ap()
        self.odn_b = self.S.bufs("odn_d", 8)
        self.att_d = nc.dram_tensor("att_scr", [4, 64, T], BF16, kind="Internal").ap()
        self.att_b = self.S.bufs("att_d", 4)
        self.out_d = nc.dram_tensor("out", [T, D], F32, kind="ExternalOutput").ap()

    def alloc(self):
        nc = self.nc
        S = self.S
        self.xT = nc.alloc_sbuf_tensor("xT", [128, KC, T], F32)
        self.x_b = S.bufs("x", KC)
        self.hT = nc.alloc_sbuf_tensor("hT", [128, KC, T], BF16)
        self.h_b = S.bufs("h", KC)
        self.NW = 4
        self.wbuf = [nc.alloc_sbuf_tensor("wbuf%d" % i, [128, KC, 256], BF16) for i in range(self.NW)]
        self.w_b = S.bufs("w", self.NW)
        self.wi = 0
        self.wd = nc.alloc_sbuf_tensor("wd", [128, 11, D], BF16)
        self.wd_b = S.bufs("wd", 11)
        self.cf = nc.alloc_sbuf_tensor("cf", [128, 1024], F32)
        self.cf_b = S.buf("cf")
        self.cb = nc.alloc_sbuf_tensor("cb", [128, 1024], BF16)
        self.cb_b = S.buf("cb")
        self.modv = nc.alloc_sbuf_tensor("modv", [128, 72], F32)
        self.modv_b = S.buf("modv")
        self.adab = nc.alloc_sbuf_tensor("adab", [128, 72], F32)
        self.adab_b = S.buf("adab")
        self.nrm = nc.alloc_sbuf_tensor("nrm", [128, 3 * KC], F32)
        self.nrm_b = S.buf("nrm")
        self.modc = nc.alloc_sbuf_tensor("modc", [128, 9 * KC], F32)
        self.modc_b = S.buf("modc")
        self.cact = nc.alloc_sbuf_tensor("cact", [128, KC], BF16)
        self.cact_b = S.buf("cact")
        self.c32 = nc.alloc_sbuf_tensor("c32", [128, 2 * KC], F32)
        self.c32_b = S.buf("c32")
        self.AR = getattr(Prog, 'AR_OVERRIDE', 33 * 1024)
        self.ar = nc.alloc_sbuf_tensor("arena", [128, self.AR], BF16)
        self.ps = nc.alloc_psum_tensor("ps", [128, 8, 512], F32)
        self.ps_b = S.bufs("ps", 8, excl=True)
        self.pi = 0

    def bank(self):
        i = self.pi
        self.pi = (self.pi + 1) % 7
        return i

    def next_w(self):
        i = self.wi
        self.wi = (self.wi + 1) % self.NW
        return i

    def ar_bf(self, off, n):
        return self.ar[:, off:off + n]

    def ar_f32(self, off, n):
        return self.ar[:, off:off + 2 * n].bitcast(F32)

    def build(self):
        S = self.S
        self.load_consts()
        self.load_x()
        for l in range(self.n_layers):
            self.mod_layer(l)
            if "ffn1" in self.stages:
                self.norm_mod(l, 0)
                self.ffn(l, 0)
            if "mix" in self.stages:
                self.norm_mod(l, 1)
                self.mixer(l)
            if "ffn2" in self.stages:
                self.norm_mod(l, 2)
                self.ffn(l, 1)
        self.store_x()

    def load_consts(self):
        S = self.S
        cf, cb = self.cf, self.cb
        S.op("sp", lambda e: e.dma_start(out=cf[:], in_=self.cst[:, :]), [], [self.cf_b], dma=True)
        S.op("dve", lambda e: e.tensor_copy(out=cb[:], in_=cf[:]), [self.cf_b], [self.cb_b])
        c32 = self.c32
        S.op("sp", lambda e: e.dma_start(out=c32[:, 0:KC], in_=self.c_d[:, :]), [], [self.c32_b], dma=True)
        S.op("act", lambda e: e.activation(out=c32[:, KC:2 * KC], in_=c32[:, 0:KC], func=AF.Exp, scale=-1.0),
             [self.c32_b], [self.c32_b])
        S.op("dve", lambda e: e.tensor_scalar_add(out=c32[:, KC:2 * KC], in0=c32[:, KC:2 * KC], scalar1=1.0),
             [self.c32_b], [self.c32_b])
        S.op("dve", lambda e: e.reciprocal(out=c32[:, KC:2 * KC], in_=c32[:, KC:2 * KC]),
             [self.c32_b], [self.c32_b])
        S.op("dve", lambda e: e.tensor_tensor(out=self.cact[:], in0=c32[:, 0:KC], in1=c32[:, KC:2 * KC], op=ALU.mult),
             [self.c32_b], [self.cact_b])

    def ident_f(self):
        return self.cf[:, 0:128]

    def load_x(self):
        S = self.S
        stg = [self.ar_f32(0, D), self.ar_f32(2 * D, D)]
        stg_b = S.bufs("xstg", 2)
        xT = self.xT
        for b in range(NB):
            s = b % 2
            S.op("sp", lambda e, b=b, s=s: e.dma_start(out=stg[s], in_=self.x_d[b * 128:(b + 1) * 128, :]),
                 [], [stg_b[s]], dma=True)
            for half in range(2):
                pb = self.bank()
                for q in range(4):
                    c = half * 4 + q
                    S.op("pe", lambda e, c=c, s=s, pb=pb, q=q: e.transpose(
                        self.ps[:, pb, q * 128:(q + 1) * 128], stg[s][:, c * 128:(c + 1) * 128], self.ident_f()),
                        [stg_b[s], self.cf_b], [self.ps_b[pb]])
                psv = self.ps[:, pb, :].rearrange("p (q i) -> p q i", i=128)
                dst = xT[:, half * 4:(half + 1) * 4, b * 128:(b + 1) * 128]
                xb = self.x_b[half * 4:(half + 1) * 4]
                if half == 0:
                    S.op("dve", lambda e, psv=psv, dst=dst: e.tensor_copy(out=dst, in_=psv), [self.ps_b[pb]], xb)
                else:
                    S.op("act", lambda e, psv=psv, dst=dst: e.copy(out=dst, in_=psv), [self.ps_b[pb]], xb)
        S.barrier()

    def store_x(self):
        S = self.S
        S.barrier()
        stg = [self.ar_f32(0, D), self.ar_f32(2 * D, D)]
        stg_b = [S.bufs("ostg%d_" % i, 2) for i in range(2)]
        xT = self.xT
        outs = []
        for b in range(NB):
            s = b % 2
            for half in range(2):
                pb = self.bank()
                for q in range(4):
                    c = half * 4 + q
                    S.op("pe", lambda e, c=c, b=b, pb=pb, q=q: e.transpose(
                        self.ps[:, pb, q * 128:(q + 1) * 128], xT[:, c, b * 128:(b + 1) * 128], self.ident_f()),
                        [self.x_b[c], self.cf_b], [self.ps_b[pb]])
                if half == 0:
                    S.op("dve", lambda e, s=s, pb=pb: e.tensor_copy(
                        out=stg[s][:, 0:512], in_=self.ps[:, pb, :]), [self.ps_b[pb]], [stg_b[s][0]])
                else:
                    S.op("act", lambda e, s=s, pb=pb: e.copy(
                        out=stg[s][:, 512:1024], in_=self.ps[:, pb, :]), [self.ps_b[pb]], [stg_b[s][1]])
            ob = S.buf("outd%d" % b)
            o = S.op("sp", lambda e, b=b, s=s: e.dma_start(out=self.out_d[b * 128:(b + 1) * 128, :], in_=stg[s]),
                     stg_b[s], [ob], dma=True)
            outs.append(ob)
        S.op("sp", None, outs, [])

    def mod_layer(self, l):
        S = self.S
        S.op("sp", lambda e: e.dma_start(out=self.adab[:], in_=self.ada_b[l]), [], [self.adab_b], dma=True)
        S.op("sp", lambda e: e.dma_start(out=self.nrm[:], in_=self.norms[l]), [], [self.nrm_b], dma=True)
        aw = self.ada_w[l].rearrange("(kc p) n -> p kc n", p=128)
        pb = self.bank()
        for blk in range(9 * D // 256):
            wi = self.next_w()
            wb = self.wbuf[wi]
            S.op("pool", lambda e, blk=blk, wb=wb: e.dma_start(out=wb[:], in_=aw[:, :, blk * 256:(blk + 1) * 256]),
                 [], [self.w_b[wi]], dma=True)
            for j in range(2):
                col = blk * 2 + j
                for kc in range(KC):
                    S.op("pe", lambda e, wb=wb, j=j, kc=kc, col=col: e.matmul(
                        self.ps[:, pb, col:col + 1], wb[:, kc, j * 128:(j + 1) * 128], self.cact[:, kc:kc + 1],
                        start=(kc == 0), stop=(kc == KC - 1)),
                        [self.w_b[wi], self.cact_b], [self.ps_b[pb]])
        S.op("dve", lambda e: e.tensor_tensor(out=self.modv[:], in0=self.ps[:, pb, 0:72], in1=self.adab[:], op=ALU.add),
             [self.ps_b[pb], self.adab_b], [self.modv_b])
        mv, mc, nr = self.modv, self.modc, self.nrm
        for j in range(3):
            S.op("dve", lambda e, j=j: e.scalar_tensor_tensor(
                out=mc[:, j * 24:j * 24 + 8], in0=mv[:, (3 * j + 1) * 8:(3 * j + 2) * 8], scalar=1.0,
                in1=nr[:, j * 8:(j + 1) * 8], op0=ALU.add, op1=ALU.mult),
                [self.modv_b, self.nrm_b], [self.modc_b])
            S.op("dve", lambda e, j=j: e.tensor_copy(out=mc[:, j * 24 + 8:j * 24 + 16], in_=mv[:, (3 * j) * 8:(3 * j + 1) * 8]),
                 [self.modv_b], [self.modc_b])
            gsc = 1.0 if j == 1 else 0.5
            S.op("dve", lambda e, j=j, gsc=gsc: e.tensor_scalar(
                out=mc[:, j * 24 + 16:j * 24 + 24], in0=mv[:, (3 * j + 2) * 8:(3 * j + 3) * 8], scalar1=gsc, scalar2=None,
                op0=ALU.mult),
                [self.modv_b], [self.modc_b])

    def norm_mod(self, l, j):
        S = self.S
        S.barrier()
        xT, hT = self.xT, self.hT
        sq = [self.ar_bf(0, T), self.ar_bf(T, T)]
        sq_b = S.bufs("sq", 2)
        rstd = self.ar_f32(2 * T, T)
        rstd_b = S.buf("rstd")
        tmp = [self.ar_bf(6 * T, T), self.ar_bf(7 * T, T)]
        tmp_b = S.bufs("ntmp", 2)
        ones = self.cb[:, 256:384]
        pbs = [self.bank() for _ in range(4)]
        for kc in range(KC):
            s = kc % 2
            S.op("act", lambda e, kc=kc, s=s: e.activation(out=sq[s], in_=xT[:, kc, :], func=AF.Square),
                 [self.x_b[kc]], [sq_b[s]])
            for t in range(NT):
                S.op("pe", lambda e, kc=kc, s=s, t=t: e.matmul(
                    self.ps[:, pbs[t], :], ones, sq[s][:, t * 512:(t + 1) * 512], start=(kc == 0), stop=(kc == KC - 1)),
                    [sq_b[s], self.cb_b], [self.ps_b[pbs[t]]])
        for t in range(NT):
            S.op("act", lambda e, t=t: e.activation(out=rstd[:, t * 512:(t + 1) * 512], in_=self.ps[:, pbs[t], :],
                                                    func=AF.Ln, scale=1.0 / D, bias=self.cf[:, 896:897]),
                 [self.ps_b[pbs[t]], self.cf_b], [rstd_b])
        S.op("act", lambda e: e.activation(out=rstd, in_=rstd, func=AF.Exp, scale=-0.5), [rstd_b], [rstd_b])
        mc = self.modc
        for kc in range(KC):
            s = kc % 2
            S.op("dve", lambda e, kc=kc, s=s: e.tensor_tensor(out=tmp[s], in0=xT[:, kc, :], in1=rstd, op=ALU.mult),
                 [self.x_b[kc], rstd_b], [tmp_b[s]])
            S.op("act", lambda e, kc=kc, s=s: e.activation(
                out=hT[:, kc, :], in_=tmp[s], func=AF.Identity,
                scale=mc[:, j * 24 + kc:j * 24 + kc + 1], bias=mc[:, j * 24 + 8 + kc:j * 24 + 8 + kc + 1]),
                [tmp_b[s], self.modc_b], [self.h_b[kc]])

    def ffn(self, l, which):
        S = self.S
        S.barrier()
        j = 0 if which == 0 else 2
        xT, hT = self.xT, self.hT
        wu = self.w_up[which][l].rearrange("(kc p) n -> p kc n", p=128)
        wdn = self.w_dn[which][l]
        act = self.ar[:, 0:11 * T].rearrange("p (j t) -> p j t", t=T)
        act_b = S.bufs("act", 11)
        sg = [self.ar_f32(11 * T, 512), self.ar_f32(11 * T + 1024, 512)]
        sg_b = S.bufs("sg", 2)
        mc = self.modc
        sgi = 0
        for half in range(2):
            def load_wd():
                for jj in range(11):
                    r0 = (half * 11 + jj) * 128
                    S.op("pool", lambda e, jj=jj, r0=r0: e.dma_start(out=self.wd[:, jj, :], in_=wdn[r0:r0 + 128, :]),
                         [], [self.wd_b[jj]], dma=True)
            for blk in range(6):
                nj = 2 if blk < 5 else 1
                j0 = half * 11 + blk * 2
                wg_i = self.next_w()
                wu_i = self.next_w()
                wgb, wub = self.wbuf[wg_i], self.wbuf[wu_i]
                ncol = nj * 128
                S.op("pool", lambda e, wgb=wgb, j0=j0, ncol=ncol: e.dma_start(
                    out=wgb[:, :, 0:ncol], in_=wu[:, :, j0 * 128:j0 * 128 + ncol]), [], [self.w_b[wg_i]], dma=True)
                S.op("pool", lambda e, wub=wub, j0=j0, ncol=ncol: e.dma_start(
                    out=wub[:, :, 0:ncol], in_=wu[:, :, DFF + j0 * 128:DFF + j0 * 128 + ncol]), [], [self.w_b[wu_i]], dma=True)
                if blk == 2:
                    load_wd()
                for jj in range(nj):
                    ja = blk * 2 + jj
                    for t in range(NT):
                        pg, pu = self.bank(), self.bank()
                        for kc in range(KC):
                            S.op("pe", lambda e, kc=kc, jj=jj, t=t, pg=pg, wgb=wgb: e.matmul(
                                self.ps[:, pg, :], wgb[:, kc, jj * 128:(jj + 1) * 128], hT[:, kc, t * 512:(t + 1) * 512],
                                start=(kc == 0), stop=(kc == KC - 1)),
                                [self.w_b[wg_i], self.h_b[kc]], [self.ps_b[pg]])
                        for kc in range(KC):
                            S.op("pe", lambda e, kc=kc, jj=jj, t=t, pu=pu, wub=wub: e.matmul(
                                self.ps[:, pu, :], wub[:, kc, jj * 128:(jj + 1) * 128], hT[:, kc, t * 512:(t + 1) * 512],
                                start=(kc == 0), stop=(kc == KC - 1)),
                                [self.w_b[wu_i], self.h_b[kc]], [self.ps_b[pu]])
                        s = sgi % 2
                        sgi += 1
                        S.op("act", lambda e, s=s, pg=pg: e.activation(out=sg[s], in_=self.ps[:, pg, :], func=AF.Silu),
                             [self.ps_b[pg]], [sg_b[s]])
                        S.op("dve", lambda e, s=s, pu=pu, ja=ja, t=t: e.tensor_tensor(
                            out=act[:, ja, t * 512:(t + 1) * 512], in0=sg[s], in1=self.ps[:, pu, :], op=ALU.mult),
                            [sg_b[s], self.ps_b[pu]], [act_b[ja]])
            for t in range(NT):
                for dc in range(KC):
                    pb = self.bank()
                    for jj in range(11):
                        S.op("pe", lambda e, jj=jj, dc=dc, t=t, pb=pb: e.matmul(
                            self.ps[:, pb, :], self.wd[:, jj, dc * 128:(dc + 1) * 128], act[:, jj, t * 512:(t + 1) * 512],
                            start=(jj == 0), stop=(jj == 10)),
                            [self.wd_b[jj], act_b[jj]], [self.ps_b[pb]])
                    S.op("dve", lambda e, dc=dc, t=t, pb=pb: e.scalar_tensor_tensor(
                        out=xT[:, dc, t * 512:(t + 1) * 512], in0=self.ps[:, pb, :],
                        scalar=mc[:, j * 24 + 16 + dc:j * 24 + 16 + dc + 1],
                        in1=xT[:, dc, t * 512:(t + 1) * 512], op0=ALU.mult, op1=ALU.add),
                        [self.ps_b[pb], self.modc_b, self.x_b[dc]], [self.x_b[dc]])

    def PE(self, out, lhsT, rhs, r, w, start=True, stop=True):
        return self.S.op("pe", lambda e: e.matmul(out, lhsT, rhs, start=start, stop=stop), r, w)

    def TR(self, out, in_, ident, r, w):
        return self.S.op("pe", lambda e: e.transpose(out, in_, ident), r, w)

    def ACT(self, out, in_, func, r, w, scale=None, bias=None, accum=None):
        kw = {}
        if scale is not None:
            kw["scale"] = scale
        if bias is not None:
            kw["bias"] = bias
        if accum is not None:
            kw["accum_out"] = accum
        return self.S.op("act", lambda e: e.activation(out=out, in_=in_, func=func, **kw), r, w)

    def TT(self, out, in0, in1, op, r, w):
        return self.S.op("dve", lambda e: e.tensor_tensor(out=out, in0=in0, in1=in1, op=op), r, w)

    def TS(self, out, in0, s1, op0, r, w, s2=None, op1=None):
        if op1 is None:
            return self.S.op("dve", lambda e: e.tensor_scalar(out=out, in0=in0, scalar1=s1, scalar2=None, op0=op0), r, w)
        return self.S.op("dve", lambda e: e.tensor_scalar(out=out, in0=in0, scalar1=s1, scalar2=s2, op0=op0, op1=op1), r, w)

    def STT(self, out, in0, sc, in1, op0, op1, r, w):
        return self.S.op("dve", lambda e: e.scalar_tensor_tensor(out=out, in0=in0, scalar=sc, in1=in1, op0=op0, op1=op1), r, w)

    def CPY(self, eng, out, in_, r, w):
        if eng == "dve":
            return self.S.op("dve", lambda e: e.tensor_copy(out=out, in_=in_), r, w)
        return self.S.op("act", lambda e: e.copy(out=out, in_=in_), r, w)

    def tmp(self, name, n, dt=BF16):
        if name in self.tmps:
            return self.tmps[name]
        ne = n if dt == BF16 else 2 * n
        if self.aoff % 2:
            self.aoff += 1
        off = self.aoff
        self.aoff += ne
        assert self.aoff <= self.AR + 11 * D, ("arena overflow", name, self.aoff)
        if off + ne <= self.AR:
            base = self.ar[:, off:off + ne]
        else:
            if off < self.AR:
                off = self.AR
                self.aoff = off + ne
            o2 = off - self.AR
            base = self.wd[:].rearrange("p a b -> p (a b)")[:, o2:o2 + ne]
        ap = base if dt == BF16 else base.bitcast(F32)
        if not hasattr(self, "toff"):
            self.toff = {}
        self.toff[name] = (off, ne, dt == F32)
        b = self.S.buf(name)
        self.tmps[name] = (ap, b)
        return ap, b

    def mixer(self, l):
        S = self.S
        S.barrier(engines=("pe", "act", "dve", "sp", "pool"))
        self.tmps = {}
        self.aoff = 0
        self.mix_setup(l)
        if "nodn" not in self.stages:
            self.dn_branch(l)
        if "noatt" not in self.stages:
            self.att_branch(l)
        if "nofinal" not in self.stages:
            self.mix_final(l)
        S.barrier(engines=("pe", "act", "dve", "sp", "pool"))

    def mix_setup(self, l):
        S = self.S
        sm, sm_b = self.tmp("small", 512, F32)
        self.sm, self.sm_b = sm, sm_b
        S.op("sp", lambda e: e.dma_start(out=sm, in_=self.small_d[l]), [], [sm_b], dma=True)

    def dn_branch(self, l):
        S = self.S
        hT = self.hT
        sm, sm_b = self.sm, self.sm_b
        cb, cf = self.cb, self.cf
        ident_b = cb[:, 0:128]
        U_b = cb[:, 128:256]
        ones_b = cb[:, 256:384]
        SU_f = cf[:, 384:512]
        SL_b = cb[:, 512:640]
        ones_f = cf[:, 256:384]
        ident_f = cf[:, 0:128]
        tri_f = cf[:, 128:256]
        win = self.w_in[l].rearrange("(kc p) n -> p kc n", p=128)
        CB = [self.cb_b]
        CF = [self.cf_b]
        wi = self.next_w()
        wab = self.wbuf[wi]
        S.op("pool", lambda e: e.dma_start(out=wab[:, :, 0:16], in_=win[:, :, OFF_DN_A:OFF_DN_A + 16]), [], [self.w_b[wi]], dma=True)
        pab = self.bank()
        for blk in range(NB):
            for kc in range(KC):
                self.PE(self.ps[:, pab, blk * 16:(blk + 1) * 16], hT[:, kc, blk * 128:(blk + 1) * 128], wab[:, kc, 0:16],
                        [self.h_b[kc], self.w_b[wi]], [self.ps_b[pab]], start=(kc == 0), stop=(kc == KC - 1))
        abv = self.ps[:, pab, 0:256].rearrange("p (b c) -> p b c", c=16)
        g_t, g_b = self.tmp("g_tok", 128, F32)
        be_t, be_b = self.tmp("be_tok", 128, F32)
        gc_t, gc_b = self.tmp("gc_tok", 128, F32)
        t1, t1_b = self.tmp("ab_t1", 128, F32)
        g3 = g_t.rearrange("p (b c) -> p b c", c=8)
        be3 = be_t.rearrange("p (b c) -> p b c", c=8)
        t13 = t1.rearrange("p (b c) -> p b c", c=8)
        dtb = sm[:, 256:384].rearrange("p (b c) -> p b c", c=8)
        self.TT(t13, abv[:, :, 0:8], dtb, ALU.add, [self.ps_b[pab], sm_b], [t1_b])
        self.ACT(t1, t1, AF.Exp, [t1_b], [t1_b])
        self.ACT(t1, t1, AF.Ln, [t1_b], [t1_b], bias=cf[:, 898:899])
        ea, ea_b = self.tmp("expalog", 128, F32)
        self.ACT(ea, sm[:, 384:512], AF.Exp, [sm_b], [ea_b])
        self.STT(g_t, t1, -1.0, ea, ALU.mult, ALU.mult, [t1_b, ea_b], [g_b])
        self.ACT(be3, abv[:, :, 8:16], AF.Exp, [self.ps_b[pab]], [be_b], scale=-1.0)
        self.TS(be_t, be_t, 1.0, ALU.add, [be_b], [be_b])
        S.op("dve", lambda e: e.reciprocal(out=be_t, in_=be_t), [be_b], [be_b])
        pgc = self.bank()
        self.PE(self.ps[:, pgc, 0:128], tri_f, g_t, CF + [g_b], [self.ps_b[pgc]])
        self.CPY("dve", gc_t, self.ps[:, pgc, 0:128], [self.ps_b[pgc]], [gc_b])
        nbe_t, nbe_b = self.tmp("nbe_tok", 128, F32)
        begc_t, begc_b = self.tmp("begc_tok", 128, F32)
        self.TS(nbe_t, be_t, -1.0, ALU.mult, [be_b], [nbe_b])
        self.ACT(begc_t, gc_t, AF.Exp, [gc_b], [begc_b])
        self.TT(begc_t, begc_t, be_t, ALU.mult, [begc_b, be_b], [begc_b])

        zc = [self.tmp("zc%d" % f, 2052) for f in range(3)]
        sgate, sgate_b = self.tmp("sgate", T)
        fT = [self.tmp("fT%d" % f, T) for f in range(3)]
        ktok, ktok_b = self.tmp("ktok", T)
        vtok, vtok_b = self.tmp("vtok", T)
        ktok3 = ktok.rearrange("p (b d) -> p b d", d=128)
        vtok3 = vtok.rearrange("p (b d) -> p b d", d=128)
        sq, sq_b = self.tmp("dsq", 512)
        rs, rs_b = self.tmp("drs", 512, F32)
        dg = [self.tmp("diag%d" % i, 128) for i in range(12)]
        S32, S32_b = self.tmp("S32", 128, F32)
        S16, S16_b = self.tmp("S16", 128)
        oTq = [self.tmp("oTq%d" % i, 512) for i in range(2)]
        for f in range(3):
            S.op("dve", lambda e, f=f: e.memset(zc[f][0][:, 0:4], 0.0), [], [zc[f][1]])

        if not hasattr(self, "hb"):
            self.hb = {}
        for nm in ("LtM", "Aq"):
            if nm not in self.hb:
                self.hb[nm] = S.bufs("h_" + nm, 2)
        sq2 = [(sq, [sq_b]), (self.tmp("Aq", 512)[0], list(self.hb["Aq"]))]
        rs2 = [(rs, [rs_b]), (self.tmp("LtM", 512, F32)[0], list(self.hb["LtM"]))]
        for h in range(8):
            S.barrier()
            cols = [OFF_DN_QKV + h * 128, OFF_DN_QKV + 1024 + h * 128, OFF_DN_QKV + 2048 + h * 128, OFF_DN_GATE + h * 128]
            w1 = self.next_w()
            w2 = self.next_w()
            wb1, wb2 = self.wbuf[w1], self.wbuf[w2]
            S.op("pool", lambda e, wb1=wb1, cols=cols: [e.dma_start(out=wb1[:, :, 0:128], in_=win[:, :, cols[0]:cols[0] + 128]),
                                                          e.dma_start(out=wb1[:, :, 128:256], in_=win[:, :, cols[1]:cols[1] + 128])],
                 [], [self.w_b[w1]], dma=True, ndma=2)
            S.op("pool", lambda e, wb2=wb2, cols=cols: [e.dma_start(out=wb2[:, :, 0:128], in_=win[:, :, cols[2]:cols[2] + 128]),
                                                          e.dma_start(out=wb2[:, :, 128:256], in_=win[:, :, cols[3]:cols[3] + 128])],
                 [], [self.w_b[w2]], dma=True, ndma=2)
            for f in range(3):
                for i in range(4):
                    c = f * 8 + h
                    self.TS(dg[f * 4 + i][0], ident_b, sm[:, c * 4 + i:c * 4 + i + 1], ALU.mult, CB + [sm_b], [dg[f * 4 + i][1]])
            for f in range(4):
                wb, wbb = (wb1, self.w_b[w1]) if f < 2 else (wb2, self.w_b[w2])
                co = (f % 2) * 128
                for t in range(NT):
                    pb = self.bank()
                    for kc in range(KC):
                        self.PE(self.ps[:, pb, :], wb[:, kc, co:co + 128], hT[:, kc, t * 512:(t + 1) * 512],
                                [wbb, self.h_b[kc]], [self.ps_b[pb]], start=(kc == 0), stop=(kc == KC - 1))
                    if f < 3:
                        self.CPY("dve" if t % 2 == 0 else "act", zc[f][0][:, 4 + t * 512:4 + (t + 1) * 512], self.ps[:, pb, :],
                                 [self.ps_b[pb]], [zc[f][1]])
                    else:
                        self.ACT(sgate[:, t * 512:(t + 1) * 512], self.ps[:, pb, :], AF.Silu, [self.ps_b[pb]], [sgate_b])
            for f in range(3):
                for t in range(NT):
                    pb = self.bank()
                    for i in range(4):
                        self.PE(self.ps[:, pb, :], dg[f * 4 + i][0], zc[f][0][:, 1 + t * 512 + i:1 + t * 512 + i + 512],
                                [dg[f * 4 + i][1], zc[f][1]], [self.ps_b[pb]], start=(i == 0), stop=(i == 3))
                    self.ACT(fT[f][0][:, t * 512:(t + 1) * 512], self.ps[:, pb, :], AF.Silu, [self.ps_b[pb]], [fT[f][1]])
            ntiles = [(f, t) for f in range(2) for t in range(NT)]
            nbank = {}

            def l2_a(i):
                f, t = ntiles[i]
                sl = slice(t * 512, (t + 1) * 512)
                sq_, sqb_ = sq2[i % 2]
                self.TT(sq_, fT[f][0][:, sl], fT[f][0][:, sl], ALU.mult, [fT[f][1]], sqb_)
                pb = self.bank()
                nbank[i] = pb
                self.PE(self.ps[:, pb, :], ones_b, sq_, CB + sqb_, [self.ps_b[pb]])

            def l2_b(i):
                f, t = ntiles[i]
                sl = slice(t * 512, (t + 1) * 512)
                rs_, rsb_ = rs2[i % 2]
                pb = nbank[i]
                if f == 0:
                    self.ACT(rs_, self.ps[:, pb, :], AF.Ln, [self.ps_b[pb]] + CF, rsb_, scale=128.0, bias=cf[:, 897:898])
                else:
                    self.ACT(rs_, self.ps[:, pb, :], AF.Ln, [self.ps_b[pb]] + CF, rsb_, scale=1.0, bias=cf[:, 896:897])
                self.ACT(rs_, rs_, AF.Exp, rsb_, rsb_, scale=-0.5)
                self.TT(fT[f][0][:, sl], fT[f][0][:, sl], rs_, ALU.mult, [fT[f][1]] + rsb_, [fT[f][1]])
            for i in range(len(ntiles) + 1):
                if i < len(ntiles):
                    l2_a(i)
                if i >= 1:
                    l2_b(i - 1)
            for (src, dst3, dst_b) in ((fT[1], ktok3, ktok_b), (fT[2], vtok3, vtok_b)):
                for g4 in range(4):
                    pb = self.bank()
                    pbv = self.ps[:, pb, :].bitcast(BF16)
                    for q in range(4):
                        blk = g4 * 4 + q
                        self.TR(pbv[:, q * 128:(q + 1) * 128], src[0][:, blk * 128:(blk + 1) * 128], ident_b,
                                [src[1]] + CB, [self.ps_b[pb]])
                    self.CPY("dve" if g4 % 2 == 0 else "act", dst3[:, g4 * 4:(g4 + 1) * 4, :],
                             pbv[:, 0:512].rearrange("p (q d) -> p q d", d=128), [self.ps_b[pb]], [dst_b])
            S.op("dve", lambda e: e.memset(S32, 0.0), [], [S32_b])
            S.op("dve", lambda e: e.memset(S16, 0.0), [], [S16_b])
            qT, kT = fT[0], fT[1]
            def alias(name, base, off, n, dt=F32):
                bap, bb = base
                ne = n if dt == BF16 else 2 * n
                ap = bap[:, off:off + ne]
                return (ap if dt == BF16 else ap.bitcast(F32))

            def v2(ap):
                return ap.rearrange("p (c i) -> p c i", i=128)

            def bm2(ap):
                return ap.unsqueeze(1).to_broadcast([128, 2, 128])

            def bc2(ap2):
                return ap2.unsqueeze(2).to_broadcast([128, 2, 128])

            Xs = [alias("Xa", zc[0], 4, 512), alias("Xb", zc[0], 1028, 512)]
            Xts = [alias("Xta", zc[1], 4, 512), alias("Xtb", zc[1], 1028, 512)]
            Qs = [alias("Qa", zc[2], 4, 512), alias("Qb", zc[2], 1028, 512)]
            rr = self.tmp("rr", 1024, F32)[0]
            df = self.tmp("df", 512, F32)[0]
            LtM = self.tmp("LtM", 512, F32)[0]
            LM = self.tmp("LM", 512, F32)[0]
            nRb = self.tmp("nRb", 512, F32)[0]
            Rex = self.tmp("Rex", 512, F32)[0]
            Bf = self.tmp("Bf", 512, F32)[0]
            Bo = self.tmp("Bo16", 512)[0]
            R = self.tmp("Rr16", 1024)[0]
            cp_ = self.tmp("c16", 1024)[0]
            z16b = self.tmp("z16b", 1024)[0]
            Q16 = self.tmp("Q16", 512)[0]
            y16 = self.tmp("y16", 1024)[0]
            Aq = self.tmp("Aq", 512)[0]
            qd = self.tmp("qd", 512)[0]
            kd = self.tmp("kd", 512)[0]
            wT4 = self.tmp("wT4", 512)[0]
            on16, on16_b = self.tmp("on16", 512)
            vns = [self.tmp("vn%d" % i, 128) for i in range(2)]
            sc4 = self.tmp("sc4", 16, F32)[0]
            sco, sco_b = self.tmp("sco", 8, F32)
            sqj, sqj_b = self.tmp("sqj", 128)
            fT2 = fT[2][0]
            y16s = [y16, fT2[:, 0:1024]]
            Aqs = [Aq, fT2[:, 1024:1536]]
            qds = [qd, fT2[:, 1536:2048]]
            kds = [kd, self.tmp("kd_1", 512)[0]]
            wT4s = [wT4, self.tmp("wT4_1", 512)[0]]
            gl4s = [self.tmp("gl4_%d" % i, 4, F32)[0] for i in range(2)]
            if not hasattr(self, "hb"):
                self.hb = {}
            for nm in ("rr", "df", "LtM", "LM", "nRb", "Rex", "Bf", "Bo", "Rr", "cpr", "y16", "Aq", "qd", "kd", "wT4", "sc4",
                       "Xa", "Xb", "Xta", "Xtb", "Qa", "Qb", "z16b", "Q16",
                       "y16_1", "Aq_1", "qd_1", "kd_1", "wT4_1", "gl4_0", "gl4_1"):
                if nm not in self.hb:
                    self.hb[nm] = S.bufs("h_" + nm, 2)
            hb = self.hb

            def BB(nm):
                return list(hb[nm])
            y3 = y16.rearrange("p (c i) -> p c i", i=256)
            POSL = cf[:, 512:640]
            BD = cf[:, 768:896]
            PO = 7

            def prep(hf, qi):
                na = qi * 4 + 2 * hf
                cbase = na * 8 + h
                hs = slice(hf * 256, hf * 256 + 256)
                ws = slice(hf * 512, hf * 512 + 512)
                tsl = slice(na * 128, na * 128 + 256)

                def cols(t_):
                    return t_[:, cbase:cbase + 9:8]

                def B(nm):
                    return hb[nm][hf]
                par = qi % 2

                def BP(nm):
                    return hb[nm + "_1"][hf] if par else hb[nm][hf]
                Aq, qd, kd, y16, wT4 = Aqs[par], qds[par], kds[par], y16s[par], wT4s[par]
                y3 = y16.rearrange("p (c i) -> p c i", i=256)
                rrh = rr[:, ws]
                self.TT(v2(rrh[:, 0:256]), bm2(tri_f), bc2(cols(g_t)), ALU.mult, CF + [g_b], [B("rr")]); yield
                self.TT(v2(rrh[:, 256:512]), bm2(ident_f), bc2(cols(be_t)), ALU.mult, CF + [be_b], [B("rr")]); yield
                pR = self.bank()
                self.PE(self.ps[:, pR, :], ones_f, rrh, CF + [B("rr")], [self.ps_b[pR]]); yield
                Rg = self.ps[:, pR, 0:256]
                Rb = self.ps[:, pR, 256:512]
                dfh, LtMh, LMh, nRbh, Rexh, Bfh, Boh = df[:, hs], LtM[:, hs], LM[:, hs], nRb[:, hs], Rex[:, hs], Bf[:, hs], Bo[:, hs]
                sck = sc4[:, 2 * hf:2 * hf + 2]
                self.TT(v2(dfh), v2(Rg), bc2(cols(gc_t)), ALU.subtract, [self.ps_b[pR], gc_b], [B("df")]); yield
                self.ACT(Rexh, Rg, AF.Exp, [self.ps_b[pR]], [B("Rex")]); yield
                self.TT(sck, self.ps[:, pR, 127:256:128], cols(gc_t), ALU.subtract, [self.ps_b[pR], gc_b], [B("sc4")]); yield
                self.STT(v2(nRbh), v2(Rb), -1.0, bm2(SU_f), ALU.mult, ALU.mult, [self.ps_b[pR]] + CF, [B("nRb")]); yield
                self.TS(LtMh, dfh, 0.0, ALU.min, [B("df")], [B("LtM")]); yield
                self.STT(v2(LMh), v2(dfh), 0.0, bm2(POSL), ALU.max, ALU.add, [B("df")] + CF, [B("LM")]); yield
                self.ACT(LtMh, LtMh, AF.Exp, [B("LtM")], [B("LtM")]); yield
                self.ACT(LMh, LMh, AF.Exp, [B("LM")], [B("LM")], scale=-1.0); yield
                self.ACT(sck, sck, AF.Exp, [B("sc4")], [B("sc4")]); yield
                pK = self.bank()
                for j in range(2):
                    bs = slice((na + j) * 128, (na + j + 1) * 128)
                    self.PE(self.ps[:, pK, j * 128:(j + 1) * 128], kT[0][:, bs], kT[0][:, bs], [kT[1]], [self.ps_b[pK]])
                    self.PE(self.ps[:, pK, 256 + j * 128:256 + (j + 1) * 128], kT[0][:, bs], qT[0][:, bs], [kT[1], qT[1]], [self.ps_b[pK]])
                yield
                KK = self.ps[:, pK, 0:256]
                QKt = self.ps[:, pK, 256:512]
                X0h = Xs[0][:, hs]
                Xt0h = Xts[0][:, hs]
                self.TT(Bfh, KK, LtMh, ALU.mult, [self.ps_b[pK], B("LtM")], [B("Bf")]); yield
                self.TT(v2(dfh), v2(KK), bc2(cols(nbe_t)), ALU.mult, [self.ps_b[pK], nbe_b, B("LM"), B("LtM")], [B("df")]); yield
                self.TT(Bfh, Bfh, nRbh, ALU.mult, [B("Bf"), B("nRb")], [B("Bf")]); yield
                self.TT(Xt0h, dfh, LMh, ALU.mult, [B("df"), B("LM")], [B("Xta")]); yield
                self.TT(v2(X0h), v2(Bfh), bm2(BD), ALU.mult, [B("Bf")] + CF, [B("Xa")]); yield
                self.TT(v2(Qs[0][:, hs]), v2(X0h), bm2(ident_f), ALU.add, [B("Xa")] + CF, [B("Qa")]); yield
                self.TT(Boh, Bfh, X0h, ALU.subtract, [B("Bf"), B("Xa")], [B("Bo")]); yield
                self.TT(nRbh, QKt, LtMh, ALU.mult, [self.ps_b[pK], B("LtM"), B("Bf")], [B("nRb")]); yield
                self.TT(v2(Aq[:, hs]), v2(nRbh), bm2(tri_f), ALU.mult, [B("nRb")] + CF, [BP("Aq")]); yield
                R3h = R[:, ws].rearrange("p (c i) -> p c i", i=256)
                self.TT(R3h[:, :, 0:128], vtok3[:, na:na + 2, :], bc2(cols(be_t)), ALU.mult, [vtok_b, be_b], [B("Rr")]); yield
                self.TT(R3h[:, :, 128:256], ktok3[:, na:na + 2, :], bc2(cols(begc_t)), ALU.mult, [ktok_b, begc_b], [B("Rr")]); yield
                qc = 0
                xc = 0
                xn_ = ("Xa", "Xb")
                xtn_ = ("Xta", "Xtb")
                qn_ = ("Qa", "Qb")
                for m in range(5):
                    Xc, Xn = Xs[xc][:, hs], Xs[1 - xc][:, hs]
                    Xtc, Xtn = Xts[xc][:, hs], Xts[1 - xc][:, hs]
                    bXc, bXn, bXtc, bXtn = B(xn_[xc]), B(xn_[1 - xc]), B(xtn_[xc]), B(xtn_[1 - xc])
                    pX = self.bank()
                    for j in range(2):
                        cs = slice(j * 128, (j + 1) * 128)
                        self.PE(self.ps[:, pX, j * 128:(j + 1) * 128], Xc[:, cs], Xtc[:, cs], [bXtc, bXc], [self.ps_b[pX]])
                    if m < 4:
                        for j in range(2):
                            cs = slice(j * 128, (j + 1) * 128)
                            self.PE(self.ps[:, pX, 256 + j * 128:256 + (j + 1) * 128], Xtc[:, cs], Xc[:, cs], [bXtc, bXc], [self.ps_b[pX]])
                    yield
                    self.CPY("act", Xtn, self.ps[:, pX, 0:256], [self.ps_b[pX]], [bXtn]); yield
                    pQq = self.bank()
                    Qc, Qn = Qs[qc][:, hs], Qs[1 - qc][:, hs]
                    bQc, bQn = B(qn_[qc]), B(qn_[1 - qc])
                    for j in range(2):
                        cs = slice(j * 128, (j + 1) * 128)
                        self.PE(self.ps[:, pQq, cs], Xtn[:, cs], Qc[:, cs], [bXtn, bQc], [self.ps_b[pQq]])
                    yield
                    if m < 4:
                        self.CPY("act", Xn, self.ps[:, pX, 256:512], [self.ps_b[pX]], [bXn]); yield
                    if m < 4:
                        self.TT(Qn, self.ps[:, pQq, 0:256], Qc, ALU.add, [self.ps_b[pQq], bQc], [bQn]); yield
                    else:
                        self.TT(Q16[:, hs], self.ps[:, pQq, 0:256], Qc, ALU.add, [self.ps_b[pQq], bQc], [B("Q16")]); yield
                    qc = 1 - qc
                    xc = 1 - xc
                Qf, bQf = Q16[:, hs], B("Q16")
                self.TT(qd[:, hs], qT[0][:, tsl], Rexh, ALU.mult, [qT[1], B("Rex")], [BP("qd")]); yield
                self.TT(v2(kd[:, hs]), ktok3[:, na:na + 2, :], bc2(sck), ALU.mult, [ktok_b, B("sc4")], [BP("kd")]); yield
                self.CPY("act", gl4s[par][:, 2 * hf:2 * hf + 2], Rexh[:, 127:256:128], [B("Rex")], [hb["gl4_%d" % par][hf]]); yield
                zf = z16b[:, ws]
                z3h = zf.rearrange("p (c i) -> p c i", i=256)
                cf_ = cp_[:, ws]
                c3h = cf_.rearrange("p (c i) -> p c i", i=256)
                pz = self.bank()
                for j in range(2):
                    self.PE(self.ps[:, pz, j * 256:(j + 1) * 256], Qf[:, j * 128:(j + 1) * 128], R3h[:, j, :], [bQf, B("Rr")], [self.ps_b[pz]])
                yield
                self.CPY("act", zf, self.ps[:, pz, :], [self.ps_b[pz]], [B("z16b")]); yield
                pc = self.bank()
                for j in range(2):
                    self.PE(self.ps[:, pc, j * 256:(j + 1) * 256], Boh[:, j * 128:(j + 1) * 128], z3h[:, j, :], [B("Bo"), B("z16b")], [self.ps_b[pc]])
                yield
                self.CPY("act", cf_, self.ps[:, pc, :], [self.ps_b[pc]], [B("cpr")]); yield
                py = self.bank()
                for j in range(2):
                    self.PE(self.ps[:, py, j * 256:(j + 1) * 256], Qf[:, j * 128:(j + 1) * 128], c3h[:, j, :], [bQf, B("cpr")], [self.ps_b[py]])
                yield
                self.TT(y16[:, ws], self.ps[:, py, :], zf, ALU.add, [self.ps_b[py], B("z16b")], [BP("y16")]); yield
                pb = self.bank()
                pbv = self.ps[:, pb, :].bitcast(BF16)
                for j in range(2):
                    self.TR(pbv[:, j * 128:(j + 1) * 128], y3[:, 2 * hf + j, 128:256], ident_b, [BP("y16")] + CB, [self.ps_b[pb]])
                yield
                self.CPY("act", wT4[:, hs], pbv[:, 0:256], [self.ps_b[pb]], [BP("wT4")]); yield

            def scan_out(qi):
                par = qi % 2
                n0 = qi * 4
                ts = slice(n0 * 128, n0 * 128 + 512)
                Aq, qd, kd, y16, wT4, gl4 = Aqs[par], qds[par], kds[par], y16s[par], wT4s[par], gl4s[par]
                y3 = y16.rearrange("p (c i) -> p c i", i=256)

                def HB(nm, hf):
                    return hb[nm + "_1"][hf] if par else hb[nm][hf]
                for c in range(4):
                    cs = slice(c * 128, (c + 1) * 128)
                    hf = c // 2
                    vn, vn_b = vns[c % 2]
                    glb = hb["gl4_%d" % par][hf]
                    pa = self.bank()
                    self.PE(self.ps[:, pa, 0:128], wT4[:, cs], S16, [HB("wT4", hf), S16_b], [self.ps_b[pa]]); yield
                    self.PE(self.ps[:, PO, cs], qd[:, cs], S16, [HB("qd", hf), S16_b], [self.ps_b[PO]], start=True, stop=False); yield
                    self.TT(vn, y3[:, c, 0:128], self.ps[:, pa, 0:128], ALU.subtract, [HB("y16", hf), self.ps_b[pa]], [vn_b]); yield
                    self.PE(self.ps[:, PO, cs], Aq[:, cs], vn, [HB("Aq", hf), vn_b], [self.ps_b[PO]], start=False, stop=True); yield
                    pd = self.bank()
                    self.PE(self.ps[:, pd, 0:128], kd[:, cs], vn, [HB("kd", hf), vn_b], [self.ps_b[pd]]); yield
                    gl = gl4[:, c:c + 1]
                    self.STT(S16, S32, gl, self.ps[:, pd, 0:128], ALU.mult, ALU.add, [S32_b, glb, self.ps_b[pd]], [S16_b]); yield
                    self.STT(S32, S32, gl, self.ps[:, pd, 0:128], ALU.mult, ALU.add, [S32_b, glb, self.ps_b[pd]], [S32_b]); yield
                for c in range(4):
                    cs = slice(c * 128, (c + 1) * 128)
                    self.ACT(sqj, self.ps[:, PO, cs], AF.Square, [self.ps_b[PO]], [sqj_b, sco_b], accum=sco[:, c:c + 1]); yield
                self.ACT(sco[:, 4:8], sco[:, 0:4], AF.Ln, [sco_b] + CF, [sco_b], scale=1.0 / 128.0, bias=cf[:, 896:897]); yield
                self.ACT(sco[:, 4:8], sco[:, 4:8], AF.Exp, [sco_b], [sco_b], scale=-0.5); yield
                for c in range(4):
                    cs = slice(c * 128, (c + 1) * 128)
                    self.ACT(on16[:, cs], self.ps[:, PO, cs], AF.Identity, [self.ps_b[PO], sco_b], [on16_b], scale=sco[:, 4 + c:5 + c]); yield
                o4 = on16.rearrange("p (c i) -> p c i", i=128)
                self.TT(o4, o4, sm[:, 128:256].unsqueeze(1).to_broadcast([128, 4, 128]), ALU.mult, [on16_b, sm_b], [on16_b]); yield
                pb = self.bank()
                pbv = self.ps[:, pb, :].bitcast(BF16)
                for c in range(4):
                    cs = slice(c * 128, (c + 1) * 128)
                    self.TR(pbv[:, cs], on16[:, cs], ident_b, [on16_b] + CB, [self.ps_b[pb]])
                yield
                oq, oq_b = oTq[qi % 2]
                self.TT(oq, pbv[:, 0:512], sgate[:, ts], ALU.mult, [self.ps_b[pb], sgate_b], [oq_b]); yield
                S.op("sp", lambda e, h=h, oq=oq, ts=ts: e.dma_start(out=self.odn_d[h][:, ts], in_=oq), [oq_b], [self.odn_b[h]], dma=True)
                yield

            def drive(gens, weights):
                gens = list(gens)
                weights = list(weights)
                while gens:
                    for k in range(len(gens) - 1, -1, -1):
                        try:
                            for _ in range(weights[k]):
                                next(gens[k])
                        except StopIteration:
                            del gens[k]
                            del weights[k]
            drive([prep(0, 0), prep(1, 0)], [1, 1])
            for qi in range(4):
                if qi < 3:
                    drive([prep(0, qi + 1), prep(1, qi + 1), scan_out(qi)], [2, 2, 1])
                else:
                    drive([scan_out(qi)], [1])

    def att_branch(self, l):
        S = self.S
        S.barrier()
        self.tmps = {}
        self.aoff = 0
        hT = self.hT
        cb, cf = self.cb, self.cf
        CB, CF = [self.cb_b], [self.cf_b]
        sm, sm_b = self.tmp("small", 512, F32)
        sm_b = self.sm_b
        win = self.w_in[l].rearrange("(kc p) n -> p kc n", p=128)
        attn, attn_b = self.tmp("attn", 4 * T)
        self.attn, self.attn_b = attn, attn_b
        mrg, mrg_b = self.tmp("mrg", 8 * T)
        self.mrg, self.mrg_b = mrg, mrg_b
        acc, acc_b = self.tmp("attacc", 2 * T, F32)
        acc3 = acc.rearrange("p (h t) -> p h t", t=T)
        attn3 = attn.rearrange("p (h t) -> p h t", t=T)
        BD_b = cb[:, 768:896]
        msk = cb[:, 640:768]
        U_b = cb[:, 128:256]
        self.aoff = self.AR
        qn, qn_b = self.tmp("qn", T)
        kn, kn_b = self.tmp("kn", T)
        sqs = [self.tmp("asq_%d" % i, 512) for i in range(2)]
        rss = [self.tmp("ars_%d" % i, 512, F32) for i in range(2)]
        vt, vt_b = self.tmp("vt", 16 * 2 * 66)
        vt4 = vt.rearrange("p (b h d) -> p b h d", h=2, d=66)
        pt = [self.tmp("pt%d" % i, 256) for i in range(4)]
        negm, negm_b = self.tmp("negm", 256)
        S.op("dve", lambda e: e.memset(vt, 1.0), [], [vt_b])
        self.TS(negm[:, 0:128], msk, -1.0, ALU.add, CB, [negm_b], s2=30000.0, op1=ALU.mult)
        self.TS(negm[:, 128:256], U_b, -1.0, ALU.add, CB, [negm_b], s2=30000.0, op1=ALU.mult)
        groups = ((128, 1), (512, 4), (2048, 16))
        import os
        for c in [int(v) for v in os.environ.get('ATT_CHUNKS', '0,2,4,1,3,5').split(',')]:
            g = c // 2
            d = groups[g][1]
            nblk = (T // d) // 128
            w1 = self.next_w()
            w2 = self.next_w()
            wb1, wb2 = self.wbuf[w1], self.wbuf[w2]
            S.op("pool", lambda e, wb1=wb1, c=c: [e.dma_start(out=wb1[:, :, 0:128], in_=win[:, :, c * 128:(c + 1) * 128]),
                                                  e.dma_start(out=wb1[:, :, 128:256], in_=win[:, :, 768 + c * 128:768 + (c + 1) * 128])],
                 [], [self.w_b[w1]], dma=True, ndma=2)
            S.op("pool", lambda e, wb2=wb2, c=c: e.dma_start(out=wb2[:, :, 0:128], in_=win[:, :, 1536 + c * 128:1536 + (c + 1) * 128]),
                 [], [self.w_b[w2]], dma=True)
            tiles = [(qi, t) for qi in range(2) for t in range(NT)]
            dsts = ((qn, qn_b), (kn, kn_b))
            pbs_ = {}

            def norm_a(i):
                qi, t = tiles[i]
                sl = slice(t * 512, (t + 1) * 512)
                sq, sq_b = sqs[i % 2]
                pb = self.bank()
                pbs_[i] = pb
                for kc in range(KC):
                    self.PE(self.ps[:, pb, :], wb1[:, kc, qi * 128:(qi + 1) * 128], hT[:, kc, sl],
                            [self.w_b[w1], self.h_b[kc]], [self.ps_b[pb]], start=(kc == 0), stop=(kc == KC - 1))
                self.ACT(sq, self.ps[:, pb, :], AF.Square, [self.ps_b[pb]], [sq_b])

            def norm_b(i):
                qi, t = tiles[i]
                sl = slice(t * 512, (t + 1) * 512)
                dst, dst_b = dsts[qi]
                sq, sq_b = sqs[i % 2]
                rs, rs_b = rss[i % 2]
                pb = pbs_[i]
                p2 = self.bank()
                self.PE(self.ps[:, p2, :], BD_b, sq, CB + [sq_b], [self.ps_b[p2]])
                if qi == 0:
                    self.ACT(rs, self.ps[:, p2, :], AF.Ln, [self.ps_b[p2]] + CF, [rs_b], scale=1.0, bias=cf[:, 899:900])
                else:
                    self.ACT(rs, self.ps[:, p2, :], AF.Ln, [self.ps_b[p2]] + CF, [rs_b], scale=1.0 / 64.0, bias=cf[:, 896:897])
                self.ACT(rs, rs, AF.Exp, [rs_b], [rs_b], scale=-0.5)
                self.STT(dst[:, sl], self.ps[:, pb, :], sm[:, 96 + qi:97 + qi], rs, ALU.mult, ALU.mult, [self.ps_b[pb], sm_b, rs_b], [dst_b])
            for i in range(len(tiles) + 1):
                if i < len(tiles):
                    norm_a(i)
                if i >= 1:
                    norm_b(i - 1)
            blocks = [(r, nb) for r in range(d) for nb in range(nblk)]
            for bi, (r, nb) in enumerate(blocks):
                st = nb * 128 * d + r
                tsl = slice(st, st + 127 * d + 1, d)
                pb = self.bank()
                for kc in range(KC):
                    self.PE(self.ps[:, pb, 0:128], hT[:, kc, tsl], wb2[:, kc, 0:128], [self.h_b[kc], self.w_b[w2]], [self.ps_b[pb]],
                            start=(kc == 0), stop=(kc == KC - 1))
                self.CPY("act", vt4[:, bi, :, 0:64], self.ps[:, pb, 0:128].rearrange("p (h d) -> p h d", d=64), [self.ps_b[pb]], [vt_b])
            items = [(hh2, bi, r, nb) for hh2 in range(2) for bi, (r, nb) in enumerate(blocks)]

            def st_scores(i):
                hh2, bi, r, nb = items[i]
                ps_ = slice(hh2 * 64, hh2 * 64 + 64)
                st = nb * 128 * d + r
                qsl = slice(st, st + 127 * d + 1, d)
                pS = self.bank()
                has_prev = nb > 0
                if has_prev:
                    sp_ = (nb - 1) * 128 * d + r
                    ksl = slice(sp_, sp_ + 127 * d + 1, d)
                    self.PE(self.ps[:, pS, 0:128], kn[ps_, ksl], qn[ps_, qsl], [kn_b, qn_b], [self.ps_b[pS]], start=True, stop=False)
                self.PE(self.ps[:, pS, 128:256], kn[ps_, qsl], qn[ps_, qsl], [kn_b, qn_b], [self.ps_b[pS]], start=(not has_prev), stop=False)
                lo = 0 if has_prev else 128
                self.PE(self.ps[:, pS, lo:256], cb[:, 0:128], negm[:, lo:256], CB + [negm_b], [self.ps_b[pS]], start=False, stop=True)
                P, P_b = pt[i % 4]
                self.ACT(P[:, lo:256], self.ps[:, pS, lo:256], AF.Exp, [self.ps_b[pS]], [P_b])

            def st_pv(i):
                hh2, bi, r, nb = items[i]
                st = nb * 128 * d + r
                qsl = slice(st, st + 127 * d + 1, d)
                has_prev = nb > 0
                P, P_b = pt[i % 4]
                pO = self.bank()
                if has_prev:
                    self.PE(self.ps[0:65, pO, 0:128], vt4[:, bi - 1, hh2, 0:65], P[:, 0:128], [vt_b, P_b], [self.ps_b[pO]], start=True, stop=False)
                self.PE(self.ps[0:65, pO, 0:128], vt4[:, bi, hh2, 0:65], P[:, 128:256], [vt_b, P_b], [self.ps_b[pO]],
                        start=(not has_prev), stop=True)
                if g == 0:
                    self.CPY("dve", acc3[0:65, hh2, qsl], self.ps[0:65, pO, 0:128], [self.ps_b[pO]], [acc_b])
                else:
                    self.TT(acc3[0:65, hh2, qsl], self.ps[0:65, pO, 0:128], acc3[0:65, hh2, qsl], ALU.add, [self.ps_b[pO], acc_b], [acc_b])
            SK = 2
            for i in range(len(items) + SK):
                if i < len(items):
                    st_scores(i)
                if i >= SK:
                    st_pv(i - SK)
            if g == 2:
                self.ACT(acc3[64:65, :, :], acc3[64:65, :, :], AF.Ln, [acc_b], [acc_b])
                self.ACT(acc3[64:65, :, :], acc3[64:65, :, :], AF.Exp, [acc_b], [acc_b], scale=-1.0)
                for hh2 in range(2):
                    hh = (c % 2) * 2 + hh2
                    for t in range(NT):
                        sl = slice(t * 512, (t + 1) * 512)
                        pb = self.bank()
                        self.PE(self.ps[0:64, pb, :], cf[64:65, 256:320], acc3[64:65, hh2, sl], CF + [acc_b], [self.ps_b[pb]])
                        self.TT(attn3[0:64, hh, sl], acc3[0:64, hh2, sl], self.ps[0:64, pb, :], ALU.mult, [acc_b, self.ps_b[pb]], [attn_b])

    def mix_final(self, l):
        S = self.S
        S.barrier(engines=("pe", "act", "dve", "sp", "pool"))
        hT, xT = self.hT, self.xT
        cb, cf = self.cb, self.cf
        attn3 = self.attn.rearrange("p (h t) -> p h t", t=T)
        mrg3 = self.mrg.rearrange("p (c t) -> p c t", t=T)
        mrg_b = self.mrg_b
        odn_all = self.tmps["attacc"][0].bitcast(BF16)
        odn_t = [odn_all[:, i * 4096:(i + 1) * 4096].rearrange("p (h t) -> p h t", t=512) for i in range(2)]
        odn_tb = S.bufs("odn_re", 2)
        odn_src = self.odn_d.rearrange("h p t -> p h t")
        oi = 0
        win = self.w_in[l].rearrange("(kc p) n -> p kc n", p=128)
        wpd = self.w_pd[l].rearrange("(kc p) n -> p kc n", p=128)
        wo = self.w_o[l].rearrange("(kc p) n -> p kc n", p=128)
        wpa_d = self.w_pa[l].rearrange("(hh p) n -> p hh n", p=64)
        self.aoff = self.AR
        self.tmps.pop("qn", None)
        wpa, wpa_b = self.tmp("wpa", 4 * D)
        wpa3 = wpa.rearrange("p (h n) -> p h n", n=D)
        S.op("pool", lambda e: e.dma_start(out=wpa3[0:64, :, :], in_=wpa_d), [], [wpa_b], dma=True)
        ge = [self.tmp("ge%d" % i, 512, F32) for i in range(2)]
        m1, m1_b = self.tmp("m1", 512, F32)
        mc = self.modc
        for dc in range(KC):
            w1 = self.next_w()
            w2 = self.next_w()
            wb1, wb2 = self.wbuf[w1], self.wbuf[w2]
            S.op("pool", lambda e, wb1=wb1, dc=dc: [e.dma_start(out=wb1[:, :, 0:128], in_=win[:, :, OFF_MERGE + dc * 128:OFF_MERGE + (dc + 1) * 128]),
                                                   e.dma_start(out=wb1[:, :, 128:256], in_=win[:, :, OFF_MERGE + D + dc * 128:OFF_MERGE + D + (dc + 1) * 128])],
                 [], [self.w_b[w1]], dma=True, ndma=2)
            S.op("pool", lambda e, wb2=wb2, dc=dc: e.dma_start(out=wb2[:, :, 0:128], in_=wpd[:, :, dc * 128:(dc + 1) * 128]),
                 [], [self.w_b[w2]], dma=True)
            for t in range(NT):
                sl = slice(t * 512, (t + 1) * 512)
                od3, odn_b = odn_t[oi % 2], odn_tb[oi % 2]
                oi += 1
                S.op("sp", lambda e, od3=od3, sl=sl: e.dma_start(out=od3, in_=odn_src[:, :, sl]), self.odn_b, [odn_b], dma=True)
                pga, pgd, pya, pyd = self.bank(), self.bank(), self.bank(), self.bank()
                for gi, pg in enumerate((pga, pgd)):
                    for kc in range(KC):
                        self.PE(self.ps[:, pg, :], wb1[:, kc, gi * 128:(gi + 1) * 128], hT[:, kc, sl], [self.w_b[w1], self.h_b[kc]],
                                [self.ps_b[pg]], start=(kc == 0), stop=(kc == KC - 1))
                for hh in range(4):
                    self.PE(self.ps[:, pya, :], wpa3[0:64, hh, dc * 128:(dc + 1) * 128], attn3[0:64, hh, sl], [wpa_b, self.attn_b],
                            [self.ps_b[pya]], start=(hh == 0), stop=(hh == 3))
                for h8 in range(8):
                    self.PE(self.ps[:, pyd, :], wb2[:, h8, 0:128], od3[:, h8, :], [self.w_b[w2], odn_b], [self.ps_b[pyd]],
                            start=(h8 == 0), stop=(h8 == 7))
                for gi, (pg, py) in enumerate(((pga, pya), (pgd, pyd))):
                    gt_, gt_b = ge[gi]
                    self.ACT(gt_, self.ps[:, pg, :], AF.Sigmoid, [self.ps_b[pg]], [gt_b])
                    self.TT(gt_, gt_, self.ps[:, py, :], ALU.mult, [gt_b, self.ps_b[py]], [gt_b])
                self.TT(mrg3[:, dc, sl], ge[0][0], ge[1][0], ALU.add, [ge[0][1], ge[1][1]], [mrg_b])
        for half in range(4):
            w1 = self.next_w()
            wb1 = self.wbuf[w1]
            S.op("pool", lambda e, wb1=wb1, half=half: e.dma_start(out=wb1[:, :, 0:256], in_=wo[:, :, half * 256:(half + 1) * 256]),
                 [], [self.w_b[w1]], dma=True)
            for j2 in range(2):
                dc = half * 2 + j2
                for t in range(NT):
                    sl = slice(t * 512, (t + 1) * 512)
                    pb = self.bank()
                    for kc in range(KC):
                        self.PE(self.ps[:, pb, :], wb1[:, kc, j2 * 128:(j2 + 1) * 128], mrg3[:, kc, sl], [self.w_b[w1], mrg_b],
                                [self.ps_b[pb]], start=(kc == 0), stop=(kc == KC - 1))
                    self.STT(xT[:, dc, sl], self.ps[:, pb, :], mc[:, 24 + 16 + dc:24 + 16 + dc + 1], xT[:, dc, sl], ALU.mult, ALU.add,
                             [self.ps_b[pb], self.modc_b, self.x_b[dc]], [self.x_b[dc]])

def make_consts():
    c = np.zeros((128, 1024), np.float32)
    c[:, 0:128] = np.eye(128, dtype=np.float32)
    jj = np.arange(128)[:, None]
    ii = np.arange(128)[None, :]
    c[:, 128:256] = (jj <= ii).astype(np.float32)
    c[:, 256:384] = 1.0
    c[:, 384:512] = (jj < ii).astype(np.float32)
    c[:, 512:640] = (jj > ii).astype(np.float32)
    c[:, 640:768] = (jj >= ii).astype(np.float32)
    c[:, 768:896] = ((jj < 64) == (ii < 64)).astype(np.float32)
    c[:, 512:640] = np.where((jj > ii) & ((jj < 64) == (ii < 64)), 0.0, 30000.0)
    c[:, 896] = EPS
    c[:, 897] = 128.0 * EPS
    c[:, 898] = 1.0
    c[:, 899] = 64.0 * EPS
    return c


def make_small(inputs):
    L = DEPTH
    sm = np.zeros((L, 128, 512), np.float32)
    cw = np.asarray(inputs["conv_w"], np.float32)
    sm[:, :, 0:96] = cw.reshape(L, 4, 24, 128).transpose(0, 3, 2, 1).reshape(L, 128, 96)
    p = np.arange(128) % 64
    sm[:, :, 96] = np.asarray(inputs["q_norm"], np.float32)[:, p]
    sm[:, :, 97] = np.asarray(inputs["k_norm"], np.float32)[:, p]
    sm[:, :, 104:112] = np.asarray(inputs["a_log"], np.float32)[:, None, :]
    sm[:, :, 112:120] = np.asarray(inputs["dt_bias"], np.float32)[:, None, :]
    sm[:, :, 128:256] = np.asarray(inputs["dn_norm"], np.float32)[:, None, :]
    sm[:, :, 256:384] = np.tile(np.asarray(inputs["dt_bias"], np.float32), (1, 16))[:, None, :]
    sm[:, :, 384:512] = np.tile(np.asarray(inputs["a_log"], np.float32), (1, 16))[:, None, :]
    return sm


def prep_inputs(inputs, b):
    f = lambda a: np.ascontiguousarray(a, dtype=np.float32)
    m = {}
    m["x"] = f(inputs["x"][b])
    m["c"] = f(inputs["c"][b].reshape(KC, 128).T)
    m["ada_w"] = f(inputs["ada_w"])
    m["ada_b"] = f(inputs["ada_b"].reshape(DEPTH, 72, 128).transpose(0, 2, 1))
    nr = np.stack([inputs["norm_ff1"], inputs["norm_mix"], inputs["norm_ff2"]], axis=1)
    m["norms"] = f(nr.reshape(DEPTH, 3 * KC, 128).transpose(0, 2, 1))
    m["ffn1_w_up"] = f(inputs["ffn1_w_up"])
    m["ffn2_w_up"] = f(inputs["ffn2_w_up"])
    m["ffn1_w_down"] = f(inputs["ffn1_w_down"])
    m["ffn2_w_down"] = f(inputs["ffn2_w_down"])
    m["w_in"] = f(inputs["w_in"])
    m["w_proj_att"] = f(inputs["w_proj_att"])
    m["w_proj_dn"] = f(inputs["w_proj_dn"])
    m["w_out"] = f(inputs["w_out"])
    m["consts"] = make_consts()
    m["small"] = make_small(inputs)
    return m


_PROG = {}


def run(inputs, n_layers=DEPTH, stages=("ffn1", "mix", "ffn2"), cores=8, trace=False):
    key = (n_layers, tuple(stages))
    if key not in _PROG:
        _PROG[key] = Prog(n_layers, stages)
    p = _PROG[key]
    shared = prep_inputs(inputs, 0)
    in_maps = []
    for b in range(cores):
        m = dict(shared)
        m["x"] = np.ascontiguousarray(inputs["x"][b], dtype=np.float32)
        m["c"] = np.ascontiguousarray(inputs["c"][b].reshape(KC, 128).T, dtype=np.float32)
        in_maps.append(m)
    res = run_bass_kernel_spmd(p.nc, in_maps, core_ids=list(range(cores)), trace=trace)
    out = np.stack([r["out"] for r in res.results], axis=0)
    return out, res


def kernel(**inputs):
    out, _ = run(inputs)
    return out.astype(np.float32)
```
